# Optimizing a Trainium2 kernel written in Bass

```python
import math
import jax, jax.numpy as jnp
from jax import lax
import numpy as np


D_MODEL = 1024
BATCH = 32
SEQ = 2048
DEPTH = 2

N_EVEN = (DEPTH + 1) // 2
N_ODD = DEPTH // 2
HEAD_DIM = 64
ATT_HEADS = D_MODEL // (2 * HEAD_DIM)
ATT_WIDTH = ATT_HEADS * HEAD_DIM
ROPE_DIM = HEAD_DIM // 4
ROPE_THETA = 500000.0
DILATED_BRANCHES = ((128, 1), (512, 4), (2048, 16))
ATT_BLOCK = 128
NEG_INF = -1e30
SSM_WIDTH = D_MODEL - ATT_WIDTH
SSM_GROUP = 16
SSM_GROUPS = SSM_WIDTH // SSM_GROUP
SSM_STATE = 64
EVEN_IN = 3 * ATT_WIDTH + SSM_WIDTH
LRU_WIDTH = D_MODEL
LRU_BLOCKS = 4
LRU_BLOCK_DIM = LRU_WIDTH // LRU_BLOCKS
CONV_WIDTH = 4
RG_C = 8.0
FFN_HIDDEN = -(-8 * D_MODEL // (3 * 256)) * 256
NORM_EPS = 1e-6

kernel_name = 'hybrid_dilated_attn_s5_rglru_block'


def rms_norm(x, g):
    x32 = x.astype(jnp.float32)
    y = x32 * lax.rsqrt(jnp.mean(x32 * x32, axis=-1, keepdims=True) + NORM_EPS) * g.astype(jnp.float32)
    return y.astype(x.dtype)


def swiglu(x, w_gate, w_up, w_down):
    return (jax.nn.silu(x @ w_gate) * (x @ w_up)) @ w_down


def _linear_combine(e1, e2):
    a1, b1 = e1
    a2, b2 = e2
    return a1 * a2, a2 * b1 + b2


def partial_rope(t, positions):
    half = ROPE_DIM // 2
    inv_freq = ROPE_THETA ** (-(jnp.arange(half, dtype=jnp.float32) * 2.0 / ROPE_DIM))
    ang = positions.astype(jnp.float32)[..., None] * inv_freq
    cos = jnp.cos(ang)[:, :, None, :]
    sin = jnp.sin(ang)[:, :, None, :]
    t1 = t[..., :half]
    t2 = t[..., half:ROPE_DIM]
    return jnp.concatenate([t1 * cos - t2 * sin, t2 * cos + t1 * sin, t[..., ROPE_DIM:]], axis=-1)


def _dilated_branch(q, k, v, window, dilation):
    b, s, h, c = q.shape
    L = s // dilation
    nb = -(-L // ATT_BLOCK)
    Lp = nb * ATT_BLOCK
    span = window // dilation

    def strided(t):
        t = t.reshape(b, L, dilation, h, c).transpose(0, 2, 1, 3, 4)
        return jnp.pad(t, ((0, 0), (0, 0), (0, Lp - L), (0, 0), (0, 0)))

    def band(t):
        t = jnp.pad(strided(t), ((0, 0), (0, 0), (ATT_BLOCK, 0), (0, 0), (0, 0)))
        t = t.reshape(b, dilation, nb + 1, ATT_BLOCK, h, c)
        return jnp.concatenate([t[:, :, :-1], t[:, :, 1:]], axis=3)

    qb = strided(q).reshape(b, dilation, nb, ATT_BLOCK, h, c)
    kb = band(k)
    vb = band(v)
    scores = jnp.einsum('brnqhc,brnkhc->brnhqk', qb, kb)
    qi = jnp.arange(ATT_BLOCK)[:, None] + ATT_BLOCK
    ki = jnp.arange(2 * ATT_BLOCK)[None, :]
    dist = qi - ki
    kpos = jnp.arange(nb)[:, None, None] * ATT_BLOCK + ki[None] - ATT_BLOCK
    mask = ((dist >= 0) & (dist <= span))[None] & (kpos >= 0)
    scores = jnp.where(mask[None, None, :, None], scores, NEG_INF)
    m = jnp.max(scores, axis=-1)
    p = jnp.exp(scores - m[..., None])
    l = jnp.sum(p, axis=-1)
    acc = jnp.einsum('brnhqk,brnkhc->brnqhc', p, vb)

    def unstride(t):
        t = t.reshape((b, dilation, Lp) + t.shape[4:])[:, :, :L]
        t = jnp.moveaxis(t, 1, 2)
        return t.reshape((b, s) + t.shape[3:])

    m = jnp.moveaxis(m, -1, 3)
    l = jnp.moveaxis(l, -1, 3)
    return unstride(acc), unstride(m), unstride(l)


def dilated_attention(q, k, v):
    outs = [_dilated_branch(q, k, v, w, d) for (w, d) in DILATED_BRANCHES]
    acc_all = jnp.stack([o[0] for o in outs])
    m_all = jnp.stack([o[1] for o in outs])
    l_all = jnp.stack([o[2] for o in outs])
    wts = jnp.exp(m_all - jnp.max(m_all, axis=0, keepdims=True))
    num = jnp.einsum('ibsh,ibshc->bshc', wts, acc_all)
    den = jnp.sum(wts * l_all, axis=0)
    return num / den[..., None]


def s5_mixer(u, a_re, a_im, log_dt, b_re, b_im, c_re, c_im, d_skip, w_glu):
    b, s, _ = u.shape
    u = u.reshape(b, s, SSM_GROUPS, SSM_GROUP)
    f32 = jnp.float32
    A = lax.complex(a_re.astype(f32), a_im.astype(f32))
    dt = jnp.exp(log_dt.astype(f32))[:, None]
    a_bar = jnp.exp(A * dt)
    b_bar = ((a_bar - 1.0) / A)[..., None] * lax.complex(b_re.astype(f32), b_im.astype(f32))
    bu = jnp.einsum('gpc,bsgc->bsgp', b_bar, u.astype(jnp.complex64))
    a_seq = jnp.broadcast_to(a_bar, (s,) + a_bar.shape)
    h = jax.vmap(lambda bu_b: lax.associative_scan(_linear_combine, (a_seq, bu_b), axis=0)[1])(bu)
    c_mat = lax.complex(c_re.astype(f32), c_im.astype(f32))
    y = jnp.einsum('gcp,bsgp->bsgc', c_mat, h).real + d_skip.astype(f32) * u
    y = jax.nn.gelu(y.reshape(b, s, SSM_WIDTH))
    return y * jax.nn.sigmoid(y @ w_glu.astype(f32))


def rglru_mixer(xb, conv_w, conv_b, w_r, b_r, w_i, b_i, lam):
    b, s, r = xb.shape
    f32 = jnp.float32
    xc = lax.conv_general_dilated(xb, conv_w.astype(f32)[:, None, :], window_strides=(1,),
                                  padding=[(CONV_WIDTH - 1, 0)],
                                  dimension_numbers=('NWC', 'WIO', 'NWC'),
                                  feature_group_count=r) + conv_b.astype(f32)
    xh = xc.reshape(b, s, LRU_BLOCKS, LRU_BLOCK_DIM)
    gate_r = jax.nn.sigmoid(jnp.einsum('bshi,hij->bshj', xh, w_r.astype(f32)).reshape(b, s, r) + b_r.astype(f32))
    gate_i = jax.nn.sigmoid(jnp.einsum('bshi,hij->bshj', xh, w_i.astype(f32)).reshape(b, s, r) + b_i.astype(f32))
    log_a = -RG_C * gate_r * jax.nn.softplus(-lam.astype(f32))
    a = jnp.exp(log_a)
    mult = jnp.sqrt(-jnp.expm1(2.0 * log_a))
    _, h = lax.associative_scan(_linear_combine, (a, mult * (gate_i * xc)), axis=1)
    return h


def even_mixer(xn, positions, w_in, w_out, a_re, a_im, log_dt, b_re, b_im, c_re, c_im, d_skip, w_glu):
    b, s, _ = xn.shape
    z = (xn @ w_in).astype(jnp.float32)
    q = z[..., :ATT_WIDTH].reshape(b, s, ATT_HEADS, HEAD_DIM)
    k = z[..., ATT_WIDTH:2 * ATT_WIDTH].reshape(b, s, ATT_HEADS, HEAD_DIM)
    v = z[..., 2 * ATT_WIDTH:3 * ATT_WIDTH].reshape(b, s, ATT_HEADS, HEAD_DIM)
    u = z[..., 3 * ATT_WIDTH:]
    q = partial_rope(q, positions) * (HEAD_DIM ** -0.5)
    k = partial_rope(k, positions)
    att = dilated_attention(q, k, v).reshape(b, s, ATT_WIDTH)
    ssm = s5_mixer(u, a_re, a_im, log_dt, b_re, b_im, c_re, c_im, d_skip, w_glu)
    y = jnp.concatenate([att, ssm], axis=-1).astype(w_out.dtype)
    return (y @ w_out).astype(xn.dtype)


def odd_mixer(xn, w_in, w_out, conv_w, conv_b, w_r, b_r, w_i, b_i, lam):
    z = (xn @ w_in).astype(jnp.float32)
    h = rglru_mixer(z[..., :LRU_WIDTH], conv_w, conv_b, w_r, b_r, w_i, b_i, lam)
    y = (h * jax.nn.gelu(z[..., LRU_WIDTH:])).astype(w_out.dtype)
    return (y @ w_out).astype(xn.dtype)


def setup_inputs(seed: int = 0) -> dict:
    key = jax.random.key(seed)
    ks = iter(jax.random.split(key, 32))
    f32 = jnp.float32

    def nrm(shape, scale):
        return jax.random.normal(next(ks), shape, f32) * scale

    G, P, NC, BD = SSM_GROUPS, SSM_STATE, SSM_GROUP, LRU_BLOCK_DIM
    x = nrm((BATCH, SEQ, D_MODEL), 1.0)
    offset = jax.random.randint(next(ks), (BATCH, 1), 0, SEQ, dtype=jnp.int32)
    positions = offset + jnp.arange(SEQ, dtype=jnp.int32)[None, :]
    norm_mix_pre = 1.0 + nrm((DEPTH, D_MODEL), 0.05)
    norm_mix_post = 1.0 + nrm((DEPTH, D_MODEL), 0.05)
    norm_ffn_pre = 1.0 + nrm((DEPTH, D_MODEL), 0.05)
    norm_ffn_post = 1.0 + nrm((DEPTH, D_MODEL), 0.05)
    ev_w_in = nrm((N_EVEN, D_MODEL, EVEN_IN), D_MODEL ** -0.5)
    ev_w_out = nrm((N_EVEN, D_MODEL, D_MODEL), D_MODEL ** -0.5)
    s5_a_re = -0.5 * jnp.exp(nrm((N_EVEN, G, P), 0.02))
    s5_a_im = jnp.pi * jnp.arange(P, dtype=f32) + nrm((N_EVEN, G, P), 0.02)
    s5_log_dt = jax.random.uniform(next(ks), (N_EVEN, G), f32, math.log(1e-3), math.log(1e-1))
    s5_b_re = nrm((N_EVEN, G, P, NC), (2 * NC) ** -0.5)
    s5_b_im = nrm((N_EVEN, G, P, NC), (2 * NC) ** -0.5)
    s5_c_re = nrm((N_EVEN, G, NC, P), (2 * P) ** -0.5)
    s5_c_im = nrm((N_EVEN, G, NC, P), (2 * P) ** -0.5)
    s5_d = nrm((N_EVEN, G, NC), 1.0)
    s5_w_glu = nrm((N_EVEN, SSM_WIDTH, SSM_WIDTH), SSM_WIDTH ** -0.5)
    od_w_in = nrm((N_ODD, D_MODEL, 2 * LRU_WIDTH), D_MODEL ** -0.5)
    od_w_out = nrm((N_ODD, LRU_WIDTH, D_MODEL), LRU_WIDTH ** -0.5)
    rg_conv_w = nrm((N_ODD, CONV_WIDTH, LRU_WIDTH), CONV_WIDTH ** -0.5)
    rg_conv_b = nrm((N_ODD, LRU_WIDTH), 0.01)
    rg_w_r = nrm((N_ODD, LRU_BLOCKS, BD, BD), BD ** -0.5)
    rg_b_r = nrm((N_ODD, LRU_WIDTH), 0.01)
    rg_w_i = nrm((N_ODD, LRU_BLOCKS, BD, BD), BD ** -0.5)
    rg_b_i = nrm((N_ODD, LRU_WIDTH), 0.01)
    a0 = jax.random.uniform(next(ks), (N_ODD, LRU_WIDTH), f32, 0.9, 0.999)
    sig = a0 ** (1.0 / RG_C)
    rg_lam = jnp.log(sig) - jnp.log1p(-sig)
    ffn_w_gate = nrm((DEPTH, D_MODEL, FFN_HIDDEN), D_MODEL ** -0.5)
    ffn_w_up = nrm((DEPTH, D_MODEL, FFN_HIDDEN), D_MODEL ** -0.5)
    ffn_w_down = nrm((DEPTH, FFN_HIDDEN, D_MODEL), FFN_HIDDEN ** -0.5)
    return {'x': x, 'positions': positions,
            'norm_mix_pre': norm_mix_pre, 'norm_mix_post': norm_mix_post,
            'norm_ffn_pre': norm_ffn_pre, 'norm_ffn_post': norm_ffn_post,
            'ev_w_in': ev_w_in, 'ev_w_out': ev_w_out,
            's5_a_re': s5_a_re, 's5_a_im': s5_a_im, 's5_log_dt': s5_log_dt,
            's5_b_re': s5_b_re, 's5_b_im': s5_b_im, 's5_c_re': s5_c_re, 's5_c_im': s5_c_im,
            's5_d': s5_d, 's5_w_glu': s5_w_glu,
            'od_w_in': od_w_in, 'od_w_out': od_w_out,
            'rg_conv_w': rg_conv_w, 'rg_conv_b': rg_conv_b,
            'rg_w_r': rg_w_r, 'rg_b_r': rg_b_r, 'rg_w_i': rg_w_i, 'rg_b_i': rg_b_i, 'rg_lam': rg_lam,
            'ffn_w_gate': ffn_w_gate, 'ffn_w_up': ffn_w_up, 'ffn_w_down': ffn_w_down}


def reference(x, positions, norm_mix_pre, norm_mix_post, norm_ffn_pre, norm_ffn_post,
              ev_w_in, ev_w_out, s5_a_re, s5_a_im, s5_log_dt, s5_b_re, s5_b_im, s5_c_re, s5_c_im,
              s5_d, s5_w_glu, od_w_in, od_w_out, rg_conv_w, rg_conv_b, rg_w_r, rg_b_r, rg_w_i, rg_b_i,
              rg_lam, ffn_w_gate, ffn_w_up, ffn_w_down):
    for layer in range(DEPTH):
        hn = rms_norm(x, norm_mix_pre[layer])
        if layer % 2 == 0:
            e = layer // 2
            mix = even_mixer(hn, positions, ev_w_in[e], ev_w_out[e], s5_a_re[e], s5_a_im[e],
                             s5_log_dt[e], s5_b_re[e], s5_b_im[e], s5_c_re[e], s5_c_im[e],
                             s5_d[e], s5_w_glu[e])
        else:
            o = layer // 2
            mix = odd_mixer(hn, od_w_in[o], od_w_out[o], rg_conv_w[o], rg_conv_b[o],
                            rg_w_r[o], rg_b_r[o], rg_w_i[o], rg_b_i[o], rg_lam[o])
        x = x + rms_norm(mix, norm_mix_post[layer])
        hn = rms_norm(x, norm_ffn_pre[layer])
        ff = swiglu(hn, ffn_w_gate[layer], ffn_w_up[layer], ffn_w_down[layer])
        x = x + rms_norm(ff, norm_ffn_post[layer])
    return x
```

```python
import numpy as np
import ml_dtypes
import concourse.bass as bass
import concourse.mybir as mybir
from concourse.bass_utils import run_bass_kernel_spmd

F32 = mybir.dt.float32
BF16 = mybir.dt.bfloat16
I32 = mybir.dt.int32
AF = mybir.ActivationFunctionType
ALU = mybir.AluOpType
AX = mybir.AxisListType

SEM_LIMIT = 24000


class Ref:
    __slots__ = ("tile", "key", "ap")

    def __init__(self, tile, key, ap):
        self.tile = tile
        self.key = key
        self.ap = ap

    def __getitem__(self, idx):
        return Ref(self.tile, self.key, self.ap[idx])

    def re(self, pat, **kw):
        return Ref(self.tile, self.key, self.ap.rearrange(pat, **kw))


class _Keyed:
    def __init__(self, tile, key):
        self.tile = tile
        self.key = key

    def __getitem__(self, idx):
        return Ref(self.tile, self.key, self.tile.h[idx])


class Tile:
    def __init__(self, h, name):
        self.h = h
        self.name = name
        self.reg = {}

    def __getitem__(self, idx):
        return Ref(self, "*", self.h[idx])

    def k(self, key):
        return _Keyed(self, key)


def _merge(dst, src):
    for s, v in src.items():
        if dst.get(s, 0) < v:
            dst[s] = v


class Kern:
    ENG = ("pe", "act", "dve", "pool", "sp")

    def __init__(self, nc, sync_same=("act", "dve", "pool")):
        self.nc = nc
        self.stack = []
        self.prog = {e: [] for e in self.ENG}
        self.free_sems = []
        self.nsem = 0
        self.csem = {}
        self.ccnt = {}
        self.waited = {e: {} for e in self.ENG}
        self.dsems = {e: [] for e in self.ENG}
        self.drr = {e: 0 for e in self.ENG}
        self.dcnt = {}
        self.sync_same = set(sync_same)
        self.semname = {}
        for e in self.ENG:
            self._new_csem(e)
        self.ndma_sems = {"sp": 6, "act": 3, "pool": 4, "dve": 0, "pe": 0}
        for e in self.ENG:
            for _ in range(self.ndma_sems[e]):
                self.dsems[e].append(self._alloc_sem())

    def _alloc_sem(self):
        cm = self.nc.semaphore("s%d" % self.nsem)
        self.nsem += 1
        h = cm.__enter__()
        self.stack.append(cm)
        self.dcnt[h] = 0
        return h

    def _new_csem(self, e):
        self.csem[e] = self._alloc_sem()
        self.ccnt[e] = 0

    def sb(self, name, shape, dt):
        self.uid = getattr(self, "uid", 0) + 1
        name = "sb%d_%s" % (self.uid, name)
        cm = self.nc.sbuf_tensor(name, list(shape), dt)
        h = cm.__enter__()
        self.stack.append(cm)
        return Tile(h, name)

    def ps(self, name, shape, dt=F32):
        self.uid = getattr(self, "uid", 0) + 1
        name = "ps%d_%s" % (self.uid, name)
        nbytes = int(np.prod(shape[1:])) * (4 if dt == F32 else 2)
        assert nbytes == 2048, "PSUM tiles must be exactly one bank"
        cm = self.nc.psum_tensor(name, list(shape), dt)
        h = cm.__enter__()
        self.stack.append(cm)
        t = Tile(h, name)
        t.psum = True
        return t

    def dram(self, name, shape, dt, kind="Internal"):
        t = self.nc.dram_tensor(name, list(shape), dt, kind=kind)
        return Tile(t.ap(), name)

    def _conf(self, ref):
        t = ref.tile
        if ref.key == "*":
            return list(t.reg.values())
        out = []
        if "*" in t.reg:
            out.append(t.reg["*"])
        if ref.key in t.reg:
            out.append(t.reg[ref.key])
        return out

    def emit(self, eng, fn, reads=(), writes=(), dma=False):
        pr = [r for r in reads if getattr(r.tile, "psum", False)]
        if pr:
            reads = [r for r in reads if not getattr(r.tile, "psum", False)]
            writes = list(writes) + pr
        need = {}
        for r in reads:
            for w, _ in self._conf(r):
                _merge(need, w)
        for wr in writes:
            for w, rd in self._conf(wr):
                _merge(need, w)
                _merge(need, rd)
        waits = []
        wd = self.waited[eng]
        own = self.csem[eng]
        for s, v in need.items():
            if wd.get(s, 0) >= v:
                continue
            if (not dma) and s is own and eng not in self.sync_same:
                continue
            wd[s] = v
            waits.append((s, v))
        if dma:
            lst = self.dsems[eng]
            i = self.drr[eng] % len(lst)
            self.drr[eng] += 1
            s = lst[i]
            if self.dcnt[s] + 16 > SEM_LIMIT:
                s = self._alloc_sem()
                lst[i] = s
            self.dcnt[s] += 16
            ev = (s, self.dcnt[s])
            inc = 16
        else:
            if self.ccnt[eng] + 1 > SEM_LIMIT:
                self._new_csem(eng)
            self.ccnt[eng] += 1
            ev = (self.csem[eng], self.ccnt[eng])
            inc = 1
        self.prog[eng].append((waits, fn, ev[0], inc))
        evd = {ev[0]: ev[1]}
        for r in reads:
            reg = r.tile.reg.setdefault(r.key, [{}, {}])
            _merge(reg[1], evd)
        for wr in writes:
            if wr.key == "*":
                wr.tile.reg = {"*": [dict(evd), {}]}
            else:
                wr.tile.reg[wr.key] = [dict(evd), {}]
        return ev

    def wait_all(self, eng, refs):
        need = {}
        for r in refs:
            for w, rd in self._conf(r):
                _merge(need, w)
        waits = [(s, v) for s, v in need.items()]
        self.prog[eng].append((waits, None, None, 0))

    def dma(self, out, in_, eng="sp", **kw):
        return self.emit(eng, lambda e: e.dma_start(out=out.ap, in_=in_.ap, **kw),
                         reads=[in_], writes=[out], dma=True)

    def mm(self, out, lhsT, rhs, start=True, stop=True, **kw):
        return self.emit("pe", lambda e: e.matmul(out.ap, lhsT.ap, rhs.ap, start=start, stop=stop, **kw),
                         reads=[lhsT, rhs], writes=[out])

    def tr(self, out, in_, ident):
        return self.emit("pe", lambda e: e.transpose(out.ap, in_.ap, ident.ap),
                         reads=[in_, ident], writes=[out])

    def act(self, out, in_, func, bias=None, scale=None, accum=None, eng="act", extra_reads=()):
        kw = {}
        reads = [in_] + list(extra_reads)
        writes = [out]
        if bias is not None:
            if isinstance(bias, Ref):
                kw["bias"] = bias.ap
                reads.append(bias)
            else:
                kw["bias"] = bias
        if scale is not None:
            if isinstance(scale, Ref):
                kw["scale"] = scale.ap
                reads.append(scale)
            else:
                kw["scale"] = scale
        if accum is not None:
            kw["accum_out"] = accum.ap
            writes.append(accum)
        return self.emit(eng, lambda e: e.activation(out.ap, in_.ap, func, **kw), reads=reads, writes=writes)

    def tt(self, out, a, b, op, eng="dve"):
        return self.emit(eng, lambda e: e.tensor_tensor(out.ap, a.ap, b.ap, op), reads=[a, b], writes=[out])

    def ts(self, out, a, s1, op0, s2=None, op1=None, accum=None, eng="dve"):
        reads = [a]
        writes = [out]
        v1 = s1
        v2 = s2
        if isinstance(s1, Ref):
            reads.append(s1)
            v1 = s1.ap
        if isinstance(s2, Ref):
            reads.append(s2)
            v2 = s2.ap
        kw = {}
        if op1 is not None:
            kw["op1"] = op1
        if accum is not None:
            kw["accum_out"] = accum.ap
            writes.append(accum)
        return self.emit(eng, lambda e: e.tensor_scalar(out.ap, a.ap, v1, v2, op0, **kw), reads=reads, writes=writes)

    def stt(self, out, a, s, b, op0, op1, eng="dve"):
        reads = [a, b]
        v = s
        if isinstance(s, Ref):
            reads.append(s)
            v = s.ap
        return self.emit(eng, lambda e: e.scalar_tensor_tensor(out.ap, a.ap, v, b.ap, op0, op1), reads=reads, writes=[out])

    def copy(self, out, in_, eng="dve"):
        if eng == "act":
            return self.emit(eng, lambda e: e.copy(out.ap, in_.ap), reads=[in_], writes=[out])
        return self.emit(eng, lambda e: e.tensor_copy(out.ap, in_.ap), reads=[in_], writes=[out])

    def memset(self, out, val, eng="dve"):
        return self.emit(eng, lambda e: e.memset(out.ap, val), reads=[], writes=[out])

    def scan(self, out, d0, d1, init, op0=ALU.mult, op1=ALU.add, eng="dve"):
        reads = [d0, d1]
        v = init
        if isinstance(init, Ref):
            reads.append(init)
            v = init.ap
        return self.emit(eng, lambda e: e.tensor_tensor_scan(out.ap, d0.ap, d1.ap, v, op0, op1), reads=reads, writes=[out])

    def recip(self, out, in_):
        return self.emit("dve", lambda e: e.reciprocal(out.ap, in_.ap), reads=[in_], writes=[out])

    def finish(self):
        nc = self.nc
        prog = self.prog
        with nc.Block() as block:
            def run(e, name):
                for waits, fn, sem, inc in prog[name]:
                    for s, v in waits:
                        e.wait_ge(s, v)
                    if fn is not None:
                        fn(e).then_inc(sem, inc)

            @block.sync
            def _(e):
                run(e, "sp")

            @block.scalar
            def _(e):
                run(e, "act")

            @block.vector
            def _(e):
                run(e, "dve")

            @block.gpsimd
            def _(e):
                run(e, "pool")

            @block.tensor
            def _(e):
                run(e, "pe")
        while self.stack:
            self.stack.pop().__exit__(None, None, None)


def _ref_bc(self, shape):
    return Ref(self.tile, self.key, self.ap.to_broadcast(list(shape)))


def _ref_pb(self, n):
    return Ref(self.tile, self.key, self.ap.partition_broadcast(n))


Ref.bc = _ref_bc
Ref.pb = _ref_pb


class _Phase:
    def __init__(self, K):
        self.K = K

    def __enter__(self):
        self.h = len(self.K.stack)
        return self

    def __exit__(self, *a):
        K = self.K
        K.barrier()
        K.flush()
        while len(K.stack) > self.h:
            K.stack.pop().__exit__(None, None, None)
        return False


def _phase(self):
    return _Phase(self)


def _barrier(self):
    need = {}
    for e in self.ENG:
        if self.ccnt[e] > 0:
            need[self.csem[e]] = self.ccnt[e]
        for s in self.dsems[e]:
            if self.dcnt[s] > 0:
                need[s] = self.dcnt[s]
    for e in self.ENG:
        wd = self.waited[e]
        waits = []
        for s, v in need.items():
            if wd.get(s, 0) >= v:
                continue
            wd[s] = v
            waits.append((s, v))
        if waits:
            self.prog[e].append((waits, None, None, 0))


def _flush(self):
    nc = self.nc
    prog = self.prog
    self.prog = {e: [] for e in self.ENG}
    if not any(prog.values()):
        return
    with nc.Block() as block:
        def run(e, name):
            for waits, fn, sem, inc in prog[name]:
                for s, v in waits:
                    e.wait_ge(s, v)
                if fn is not None:
                    fn(e).then_inc(sem, inc)

        @block.sync
        def _(e):
            run(e, "sp")

        @block.scalar
        def _(e):
            run(e, "act")

        @block.vector
        def _(e):
            run(e, "dve")

        @block.gpsimd
        def _(e):
            run(e, "pool")

        @block.tensor
        def _(e):
            run(e, "pe")


def _finish(self):
    self.barrier()
    self.flush()
    while self.stack:
        self.stack.pop().__exit__(None, None, None)


Kern.phase = _phase
Kern.barrier = _barrier
Kern.flush = _flush
Kern.finish = _finish


import math

NCORES = 8
D = 1024
SEQ = 2048
NSEQ = 4
TOK = NSEQ * SEQ
FF = 2816
NFT = FF // 128
EPS = 1e-6
PI = math.pi
C1 = 6.28125
C2 = 2 * math.pi - 6.28125


def host_consts():
    c = {}
    c["ident"] = np.eye(128, dtype=np.float32)
    invf = np.zeros((128, 1), np.float32)
    sgn = np.zeros((128, 1), np.float32)
    for p in range(128):
        i = p % 64
        if i < 16:
            invf[p, 0] = np.float32(500000.0) ** np.float32(-((i % 8) * 2.0 / 16.0))
            sgn[p, 0] = -1.0 if i < 8 else 1.0
    c["rope_cols"] = np.concatenate([invf, sgn], axis=1)
    x = np.arange(2816)[None, :] - np.arange(128)[:, None] - 384
    m = ((x >= 0) & (x <= 128)).astype(np.float32) + ((x >= 0) & (x % 4 == 0) & (x <= 512)) + ((x >= 0) & (x % 16 == 0) & (x <= 2048))
    c["gmask"] = m.astype(np.float32)
    hm = np.zeros((128, 3), np.float32)
    hm[:64, 0] = 1
    hm[64:, 1] = 1
    hm[:64, 2] = 1
    hm[64:, 2] = -1
    c["hmask"] = hm
    return c


def swap_perm():
    perm = np.arange(512)
    for h in range(8):
        for i in range(16):
            perm[h * 64 + i] = h * 64 + (i + 8 if i < 8 else i - 8)
    return perm


class Ctx:
    pass


def load_bc(K, name, src_row, n, eng="sp"):
    t = K.sb(name, [128, n], F32)
    K.dma(t[:], src_row.pb(128), eng=eng)
    return t


def rms_pre(K, C, xt, nsub, gain, out_bf, tagp):
    for s in range(nsub):
        ss = C.small.k(tagp + "ss%d" % s)[:, C.si:C.si + 1]
        sq = C.scr_j[:, 0:D]
        K.act(sq, xt[:, s, :], AF.Square, accum=ss)
        rs = C.small.k(tagp + "rs%d" % s)[:, C.si + 1:C.si + 2]
        K.act(rs, ss, AF.Sqrt, scale=1.0 / D, bias=C.epsc[:, 0:1])
        K.recip(rs, rs)
        K.stt(out_bf[:, s, :], xt[:, s, :], rs, gain[:], ALU.mult, ALU.mult)
        C.si = (C.si + 2) % 60


def transpose_to(K, C, src_bf, nsub, dstT, col0):
    for kc in range(D // 128):
        pt = C.pst[C.pti % len(C.pst)]
        C.pti += 1
        for s in range(nsub):
            K.tr(pt[:, s * 128:(s + 1) * 128], src_bf[:, s, kc * 128:(kc + 1) * 128], C.identb[:])
        eng = "act" if kc % 2 == 0 else "dve"
        K.copy(dstT.k(kc)[:, kc, col0:col0 + nsub * 128], pt[:, 0:nsub * 128], eng=eng)


def post_norm_add(K, C, ps_pair, gain, xt_sub, tagp, sub=0):
    ssa = C.small.k(tagp + "a")[:, C.si:C.si + 1]
    ssb = C.small.k(tagp + "b")[:, C.si + 1:C.si + 2]
    rs = C.small.k(tagp + "c")[:, C.si + 2:C.si + 3]
    C.si = (C.si + 3) % 60
    C.pn = getattr(C, "pn", 0) + 1
    scr = C.scr_f if C.pn % 2 == 0 else C.scr_g
    K.act(C.scr_j[:, 0:512], ps_pair[0], AF.Square, accum=ssa)
    K.act(C.scr_j[:, 512:1024], ps_pair[1], AF.Square, accum=ssb)
    K.tt(rs, ssa, ssb, ALU.add)
    K.act(rs, rs, AF.Sqrt, scale=1.0 / D, bias=C.epsc[:, 0:1])
    K.recip(rs, rs)
    for h in range(2):
        tmp = scr.k("ab"[h])[:, h * 512:(h + 1) * 512]
        K.stt(tmp, ps_pair[h], rs, gain[:, h * 512:(h + 1) * 512], ALU.mult, ALU.mult)
        xa = Ref(xt_sub.tile, ("xt", sub, h), xt_sub.ap[:, h * 512:(h + 1) * 512])
        K.tt(xa, xa, tmp, ALU.add, eng=("pool" if h == 0 else "dve"))


def rot(C):
    b = C.rotb[C.roti % len(C.rotb)]
    C.roti += 1
    return b


class View:
    def __init__(self, tile, ap):
        self.tile = tile
        self.ap = ap

    def __getitem__(self, idx):
        return Ref(self.tile, "*", self.ap[idx])


def alloc_psum(K, C, tag):
    C.pall = [K.ps("%s%d" % (tag, i), [128, 512], F32) for i in range(8)]
    C.psm = C.pall[0:6]
    C.pst = [View(t, t.h[:].bitcast(BF16)) for t in C.pall[6:8]]


def proj_tokmajor(K, C, lhs_fn, nk, wview, gain, xt, tagp):
    for ps_ in range(2):
        acc = C.pall[ps_ * 4:ps_ * 4 + 4]
        for k in range(nk):
            wb = rot(C)
            K.dma(wb[:], wview[:, k, :], eng="sp")
            for si in range(2):
                sub = ps_ * 2 + si
                for h in range(2):
                    K.mm(acc[si * 2 + h][:], lhs_fn(k, sub), wb[:, h * 512:(h + 1) * 512], start=(k == 0), stop=(k == nk - 1))
        for si in range(2):
            sub = ps_ * 2 + si
            post_norm_add(K, C, (acc[si * 2][:], acc[si * 2 + 1][:]), gain, xt[:, sub, :], tagp, sub)


def ffn_tile(K, C, xt, L, W):
    hn = C.hn
    rms_pre(K, C, xt[:], 4, C.g_ffn_pre[L], hn[:], "f")
    transpose_to(K, C, hn[:], 4, C.hnT, 0)
    NCH = FF // 256
    gv = W["ffn_g%d" % L][:].re("(kc p) n -> p kc n", p=128)
    uv = W["ffn_u%d" % L][:].re("(kc p) n -> p kc n", p=128)
    for ch in range(NCH):
        wg = C.wgb[ch % len(C.wgb)]
        wu = C.wub[ch % len(C.wub)]
        K.dma(wg[:], gv[:, :, ch * 256:(ch + 1) * 256], eng="sp")
        K.dma(wu[:], uv[:, :, ch * 256:(ch + 1) * 256], eng="sp")
        for mi in range(2):
            m = ch * 2 + mi
            pg = C.psm[C.pmi % len(C.psm)]
            pu = C.psm[(C.pmi + 1) % len(C.psm)]
            C.pmi += 2
            for kc in range(8):
                K.mm(pg[:], wg[:, kc, mi * 128:(mi + 1) * 128], C.hnT.k(kc)[:, kc, :], start=(kc == 0), stop=(kc == 7))
            for kc in range(8):
                K.mm(pu[:], wu[:, kc, mi * 128:(mi + 1) * 128], C.hnT.k(kc)[:, kc, :], start=(kc == 0), stop=(kc == 7))
            sg = C.sgb[m % 2]
            K.act(sg[:], pg[:], AF.Silu)
            K.tt(C.actT[:, m, :], sg[:], pu[:], ALU.mult)
    wdv = W["ffn_d%d" % L][:].re("(m p) n -> p m n", p=128)
    proj_tokmajor(K, C, lambda m, sub: C.actT[:, m, sub * 128:(sub + 1) * 128], NFT, wdv, C.g_ffn_post[L], xt, "fp")


def bc_last(ref, n):
    sh = list(ref.ap.shape)
    return Ref(ref.tile, ref.key, ref.ap.unsqueeze(len(sh)).to_broadcast(sh + [n]))


def bc_mid(ref, n):
    sh = list(ref.ap.shape)
    return Ref(ref.tile, ref.key, ref.ap.unsqueeze(1).to_broadcast([sh[0], n] + sh[1:]))


def sincos(K, C, ang, shape, sn, cs, tg):
    if isinstance(tg, str):
        t = K.sb(tg + "_t", shape, F32)
        ti = K.sb(tg + "_ti", shape, I32)
        kf = K.sb(tg + "_kf", shape, F32)
        r = K.sb(tg + "_r", shape, F32)
    else:
        t, ti, kf, r = tg
    for shift, dst in ((0.0, sn), (PI / 2, cs)):
        K.ts(t[:], ang, 1.0 / (2 * PI), ALU.mult, shift / (2 * PI), ALU.add)
        K.copy(ti[:], t[:])
        K.copy(kf[:], ti[:])
        K.stt(r[:], kf[:], -C1, ang, ALU.mult, ALU.add)
        K.stt(r[:], kf[:], -C2, r[:], ALU.mult, ALU.add)
        K.ts(r[:], r[:], shift, ALU.add, PI, ALU.min)
        K.ts(r[:], r[:], -PI, ALU.max)
        K.act(dst, r[:], AF.Sin)


def cast_weight(K, C, dst, src, rows, cols):
    r0 = 0
    while r0 < rows:
        n = min(128, rows - r0)
        b = C.castb[C.casti % len(C.castb)]
        C.casti += 1
        K.dma(b[0:n, 0:cols], src[r0:r0 + n, :], eng="pool")
        K.dma(dst[r0:r0 + n, :], b[0:n, 0:cols], eng="sp")
        r0 += n


class Stop(Exception):
    pass


CUT = [None]


def chk(n):
    if CUT[0] == n:
        raise Stop()


def build(stage="full", dbg=()):
    try:
        return _build(stage, dbg)
    except Stop:
        K = LASTK[0]
        K.finish()
        return K.nc, DBGG[0]


LASTK = [None]
DBGG = [None]


def _build(stage="full", dbg=()):
    nc = bass.Bass("TRN2", target_bir_lowering=False)
    K = Kern(nc)
    LASTK[0] = K
    C = Ctx()
    C.si = 0
    C.pti = 0
    C.pmi = 0
    C.casti = 0
    C.roti = 0
    I = {}

    def inp(name, shape, dt=F32):
        I[name] = K.dram(name, shape, dt, kind="ExternalInput")
        return I[name]

    inp("x", [TOK, D])
    inp("pos", [1, TOK], I32)
    for nm in ("norm_mix_pre", "norm_mix_post", "norm_ffn_pre", "norm_ffn_post"):
        inp(nm, [2, D])
    inp("w_in0", [D, 3072])
    inp("ev_w_out", [D, D])
    inp("s5_a_re", [32, 64]); inp("s5_a_im", [32, 64]); inp("s5_log_dt", [1, 32])
    inp("s5_b_re", [32, 64, 16]); inp("s5_b_im", [32, 64, 16])
    inp("s5_c_re", [32, 16, 64]); inp("s5_c_im", [32, 16, 64]); inp("s5_d", [32, 16])
    inp("s5_w_glu", [512, 512])
    inp("od_w_in", [D, 2048]); inp("od_w_out", [D, D])
    inp("rg_conv_w", [4, D]); inp("rg_conv_b", [1, D])
    inp("rg_w_r", [1024, 256]); inp("rg_b_r", [1, D]); inp("rg_w_i", [1024, 256]); inp("rg_b_i", [1, D]); inp("rg_lam", [1, D])
    for L in range(2):
        inp("ffn_g%d" % L, [D, FF]); inp("ffn_u%d" % L, [D, FF]); inp("ffn_d%d" % L, [FF, D])
    inp("ident", [128, 128]); inp("rope_cols", [128, 2]); inp("gmask", [128, 2816]); inp("hmask", [128, 3])
    out = K.dram("out", [TOK, D], F32, kind="ExternalOutput")
    DBG = {}
    DBGG[0] = DBG

    def dbgout(name, shape, dt=F32):
        DBG[name] = K.dram("dbg_" + name, shape, dt, kind="ExternalOutput")
        return DBG[name]

    W = {}
    for nm, r, c in (("w_in0", D, 3072), ("ev_w_out", D, D), ("s5_w_glu", 512, 512), ("od_w_in", D, 2048), ("od_w_out", D, D),
                     ("rg_w_r", 1024, 256), ("rg_w_i", 1024, 256),
                     ("ffn_g0", D, FF), ("ffn_u0", D, FF), ("ffn_d0", FF, D), ("ffn_g1", D, FF), ("ffn_u1", D, FF), ("ffn_d1", FF, D)):
        if stage != "S5prep":
            W[nm] = K.dram("wb_" + nm, [r, c], BF16)
    x1d = K.dram("x1d", [TOK, D], F32) if stage != "S5prep" else None

    identf = K.sb("identf", [128, 128], F32)
    C.identb = K.sb("identb", [128, 128], BF16)
    C.epsc = K.sb("epsc", [128, 1], F32)
    C.onec = K.sb("onec", [128, 1], F32)
    C.small = K.sb("small", [128, 64], F32)
    onesb = K.sb("onesb", [128, 64], BF16)
    gm = K.sb("gm", [128, 2816], BF16)
    ropec = K.sb("ropec", [128, 2], F32)
    hmask = K.sb("hmask", [128, 3], F32)
    C.dTz = K.dram("dTz", [128, 32, 128], BF16)
    C.dBs = K.dram("dBs", [128, 32, 128], BF16)
    C.dCsRe = K.dram("dCsRe", [128, 16, 128], BF16)
    C.dCsIm = K.dram("dCsIm", [128, 16, 128], BF16)
    C.dMU = K.dram("dMU", [128, 3, 16], F32)
    C.dMUP = K.dram("dMUP", [128, 3, 16, 16], F32)
    K.dma(identf[:], I["ident"][:])
    junk = K.sb("junk", [1, 64], F32)
    junki = K.sb("junki", [1, 4], I32)
    for nm_, t_ in I.items():
        flat = t_[:]
        while len(flat.ap.shape) > 1:
            flat = flat[0]
        if nm_ == "pos":
            K.dma(junki[0:1, 0:1], Ref(flat.tile, "*", flat.ap[0:1].unsqueeze(0)))
        else:
            K.dma(junk[0:1, 0:1], Ref(flat.tile, "*", flat.ap[0:1].unsqueeze(0)))
    if stage != "full":
        K.dma(out[0:1, 0:64], junk[:])
    chk(-3)
    K.copy(C.identb[:], identf[:])
    K.memset(C.epsc[:], EPS)
    K.memset(C.onec[:], 1.0)
    K.memset(onesb[:], 1.0)
    K.dma(ropec[:], I["rope_cols"][:])
    K.dma(hmask[:], I["hmask"][:])
    chk(-2)
    K.dma(gm[:], I["gmask"][:], eng="pool")
    chk(-1)

    with K.phase():
        C.castb = [K.sb("castb%d" % i, [128, 3072], BF16) for i in range(4)]
        names = list(W.keys())
        C.lazy = {}
        if stage == "full":
            names = ["w_in0"]
            l0 = ["ev_w_out", "s5_w_glu", "ffn_g0", "ffn_u0", "ffn_d0"]
            l1 = ["od_w_in", "od_w_out", "rg_w_r", "rg_w_i", "ffn_g1", "ffn_u1", "ffn_d1"]
            for sq_, lst in ((0, l0), (1, l1)):
                jobs = []
                for nm in lst:
                    r, c = W[nm].h.shape
                    for r0 in range(0, r, 128):
                        jobs.append((W[nm], I[nm], r0, min(128, r - r0), c))
                C.lazy[sq_] = jobs
        if stage in ("A", "S5prep"):
            names = ["w_in0"]
        if stage == "L1a":
            names = ["od_w_in", "od_w_out", "rg_w_r", "rg_w_i", "ffn_g1", "ffn_u1", "ffn_d1"]
        if stage == "S5prep":
            names = []
        for nm in names:
            r, c = W[nm].h.shape
            cast_weight(K, C, W[nm], I[nm], r, c)

    chk(0)
    with K.phase():
        ps_a = K.ps("ps_a", [128, 512], F32)
        ps_b = K.ps("ps_b", [128, 512], F32)
        Tz = K.sb("Tz", [128, 32, 128], BF16)
        Bs = K.sb("Bs", [128, 32, 128], BF16)
        CsRe = K.sb("CsRe", [128, 16, 128], BF16)
        CsIm = K.sb("CsIm", [128, 16, 128], BF16)
        MU = K.sb("MU", [128, 3, 16], F32)
        araw = K.sb("araw", [32, 2, 128], F32)
        for j, nm in enumerate(("s5_a_re", "s5_a_im")):
            K.dma(araw[:, j, 0:64], I[nm][:])
            K.dma(araw[:, j, 64:128], I[nm][:])
        are = K.sb("are", [128, 32], F32)
        aim = K.sb("aim", [128, 32], F32)
        K.tr(ps_a[:, 0:32], araw[:, 0, :], identf[0:32, 0:32])
        K.tr(ps_a[:, 32:64], araw[:, 1, :], identf[0:32, 0:32])
        K.copy(are[:], ps_a[:, 0:32])
        K.copy(aim[:], ps_a[:, 32:64])
        chk(1)
        dtb = K.sb("dtb", [128, 32], F32)
        K.dma(dtb[:], I["s5_log_dt"][0:1, :].pb(128))
        K.act(dtb[:], dtb[:], AF.Exp)
        mag = K.sb("mag", [128, 32], F32)
        ang = K.sb("ang", [128, 32], F32)
        K.tt(mag[:], are[:], dtb[:], ALU.mult)
        K.act(mag[:], mag[:], AF.Exp)
        K.tt(ang[:], aim[:], dtb[:], ALU.mult)
        chk(2)
        sn = K.sb("sn", [128, 32], F32)
        cs = K.sb("cs", [128, 32], F32)
        sincos(K, C, ang[:], [128, 32], sn[:], cs[:], "sc1")
        chk(3)
        lr = K.sb("lr", [128, 32], F32)
        li = K.sb("li", [128, 32], F32)
        K.tt(lr[:], mag[:], cs[:], ALU.mult)
        K.tt(li[:], mag[:], sn[:], ALU.mult)
        PA = K.sb("PA", [128, 8, 32], F32)
        PB = K.sb("PB", [128, 8, 32], F32)
        tq = K.sb("tq", [128, 32], F32)
        K.copy(PA[:, 0, :], hmask[:, 0:1].bc([128, 32]))
        K.copy(PB[:, 0, :], hmask[:, 1:2].bc([128, 32]))
        for k in range(1, 8):
            K.tt(tq[:], li[:], PB[:, k - 1, :], ALU.mult)
            K.tt(PA[:, k, :], lr[:], PA[:, k - 1, :], ALU.mult)
            K.tt(PA[:, k, :], PA[:, k, :], tq[:], ALU.add)
            K.tt(tq[:], li[:], PA[:, k - 1, :], ALU.mult)
            K.tt(PB[:, k, :], lr[:], PB[:, k - 1, :], ALU.mult)
            K.tt(PB[:, k, :], PB[:, k, :], tq[:], ALU.subtract)
        lm1 = K.sb("lm1", [128, 32], F32)
        K.ts(lm1[:], lr[:], -1.0, ALU.add)
        den = K.sb("den", [128, 32], F32)
        K.tt(den[:], are[:], are[:], ALU.mult)
        K.tt(tq[:], aim[:], aim[:], ALU.mult)
        K.tt(den[:], den[:], tq[:], ALU.add)
        K.recip(den[:], den[:])
        wr = K.sb("wr", [128, 32], F32)
        wi = K.sb("wi", [128, 32], F32)
        K.tt(wr[:], lm1[:], are[:], ALU.mult)
        K.tt(tq[:], li[:], aim[:], ALU.mult)
        K.tt(wr[:], wr[:], tq[:], ALU.add)
        K.tt(wr[:], wr[:], den[:], ALU.mult)
        K.tt(wi[:], li[:], are[:], ALU.mult)
        K.tt(tq[:], lm1[:], aim[:], ALU.mult)
        K.tt(wi[:], wi[:], tq[:], ALU.subtract)
        K.tt(wi[:], wi[:], den[:], ALU.mult)
        chk(4)
        bre = K.sb("bre", [128, 32, 16], F32)
        bim = K.sb("bim", [128, 32, 16], F32)
        for t_, nm in ((bre, "s5_b_re"), (bim, "s5_b_im")):
            for h in range(2):
                K.dma(t_.k(h)[h * 64:(h + 1) * 64, :, :], I[nm][:].re("g p c -> p g c"), eng=("sp" if h == 0 else "pool"))
        chk(5)
        Bbr = K.sb("Bbr", [128, 32, 16], F32)
        Bbi = K.sb("Bbi", [128, 32, 16], F32)
        t3 = K.sb("t3", [128, 32, 16], F32)
        K.tt(Bbr[:], bre[:], bc_last(wr[:], 16), ALU.mult)
        K.tt(t3[:], bim[:], bc_last(wi[:], 16), ALU.mult)
        K.tt(Bbr[:], Bbr[:], t3[:], ALU.subtract)
        K.tt(Bbi[:], bim[:], bc_last(wr[:], 16), ALU.mult)
        K.tt(t3[:], bre[:], bc_last(wi[:], 16), ALU.mult)
        K.tt(Bbi[:], Bbi[:], t3[:], ALU.add)
        Wpad = K.sb("Wpad", [128, 32, 15, 16], F32)
        K.memset(Wpad[:], 0.0)
        for m in range(8):
            k = 7 - m
            K.tt(Wpad[:, :, m, :], Bbr[:], bc_last(PA[:, k, :], 16), ALU.mult)
            K.tt(t3[:], Bbi[:], bc_last(PB[:, k, :], 16), ALU.mult)
            K.tt(Wpad[:, :, m, :], Wpad[:, :, m, :], t3[:], ALU.add)
        chk(6)
        craw = K.sb("craw", [128, 4, 128], F32)
        for t_ in range(4):
            K.dma(craw.k((t_, 0))[:, t_, 0:64], I["s5_c_re"][t_ * 8:(t_ + 1) * 8].re("g c p -> (g c) p"))
            K.dma(craw.k((t_, 1))[:, t_, 64:128], I["s5_c_im"][t_ * 8:(t_ + 1) * 8].re("g c p -> (g c) p"), eng="pool")
        Vst = K.sb("Vst", [128, 32, 16], F32)
        for t_ in range(4):
            K.tr(ps_a[:, t_ * 128:(t_ + 1) * 128], craw[:, t_, :], identf[:])
        K.ts(Vst[:].re("p g c -> p (g c)"), ps_a[:], hmask[:, 2:3], ALU.mult)
        chk(7)
        dcol = K.sb("dcol", [128, 32], F32)
        for j in range(8):
            K.dma(dcol.k(j)[j * 16:(j + 1) * 16, :], I["s5_d"][:].re("g c -> c g"), allow_slow_non_contiguous=True, eng=("sp" if j % 2 == 0 else "pool"))
        chk(8)
        for g in range(32):
            pz = ps_b if g % 2 == 0 else ps_a
            for i in range(8):
                K.mm(pz[:, i * 16:(i + 1) * 16], Wpad[:, g, 7 - i:15 - i, :].re("p m c -> p (m c)"), Vst[:, g, :])
            K.stt(Tz[:, g, :], identf[:], dcol[:, g:g + 1], pz[:, 0:128], ALU.mult, ALU.add)
            K.tr(pz[:, 128:256], Wpad[:, g, 0:8, :].re("p m c -> p (m c)"), identf[:])
            K.copy(Bs[:, g, :], pz[:, 128:256], eng="act")
        chk(9)
        praw = K.sb("praw", [16, 2, 128], F32)
        K.dma(praw[:, 0, :], I["s5_a_re"][:].re("(pr h) p -> pr (h p)", h=2))
        K.dma(praw[:, 1, :], I["s5_a_im"][:].re("(pr h) p -> pr (h p)", h=2))
        K.tr(ps_a[:, 0:16], praw[:, 0, :], identf[0:16, 0:16])
        K.tr(ps_a[:, 16:32], praw[:, 1, :], identf[0:16, 0:16])
        arp = K.sb("arp", [128, 16], F32)
        aip = K.sb("aip", [128, 16], F32)
        K.copy(arp[:], ps_a[:, 0:16])
        K.copy(aip[:], ps_a[:, 16:32])
        chk(10)
        dtp = K.sb("dtp", [128, 16], F32)
        ldv = I["s5_log_dt"][0:1, :].re("o (pr h) -> o h pr", h=2)
        K.dma(dtp[0:64, :], ldv[:, 0, :].pb(64), allow_slow_non_contiguous=True)
        K.dma(dtp[64:128, :], ldv[:, 1, :].pb(64), allow_slow_non_contiguous=True)
        K.act(dtp[:], dtp[:], AF.Exp)
        chk(11)
        magp = K.sb("magp", [128, 16], F32)
        angp = K.sb("angp", [128, 16], F32)
        K.tt(magp[:], arp[:], dtp[:], ALU.mult)
        K.act(magp[:], magp[:], AF.Exp)
        K.tt(angp[:], aip[:], dtp[:], ALU.mult)
        snp = K.sb("snp", [128, 16], F32)
        csp = K.sb("csp", [128, 16], F32)
        sincos(K, C, angp[:], [128, 16], snp[:], csp[:], "sc2")
        Pr = K.sb("Pr", [128, 9, 16], F32)
        Pi = K.sb("Pi", [128, 9, 16], F32)
        K.tt(Pr[:, 1, :], magp[:], csp[:], ALU.mult)
        K.tt(Pi[:, 1, :], magp[:], snp[:], ALU.mult)
        tp = K.sb("tp", [128, 16], F32)
        for k in range(2, 9):
            K.tt(tp[:], Pi[:, 1, :], Pi[:, k - 1, :], ALU.mult)
            K.tt(Pr[:, k, :], Pr[:, 1, :], Pr[:, k - 1, :], ALU.mult)
            K.tt(Pr[:, k, :], Pr[:, k, :], tp[:], ALU.subtract)
            K.tt(tp[:], Pi[:, 1, :], Pr[:, k - 1, :], ALU.mult)
            K.tt(Pi[:, k, :], Pr[:, 1, :], Pi[:, k - 1, :], ALU.mult)
            K.tt(Pi[:, k, :], Pi[:, k, :], tp[:], ALU.add)
        K.copy(MU[:, 0, :], Pr[:, 8, :])
        K.copy(MU[:, 1, :], Pi[:, 8, :])
        K.ts(MU[:, 2, :], Pi[:, 8, :], -1.0, ALU.mult)
        MUP = K.sb("MUP", [128, 3, 16, 16], F32)
        K.copy(MUP[:, 0, 0, :], Pr[:, 8, :])
        K.copy(MUP[:, 1, 0, :], Pi[:, 8, :])
        for k in range(1, 16):
            K.tt(tp[:], MU[:, 1, :], MUP[:, 1, k - 1, :], ALU.mult)
            K.tt(MUP[:, 0, k, :], MU[:, 0, :], MUP[:, 0, k - 1, :], ALU.mult)
            K.tt(MUP[:, 0, k, :], MUP[:, 0, k, :], tp[:], ALU.subtract)
            K.tt(tp[:], MU[:, 1, :], MUP[:, 0, k - 1, :], ALU.mult)
            K.tt(MUP[:, 1, k, :], MU[:, 0, :], MUP[:, 1, k - 1, :], ALU.mult)
            K.tt(MUP[:, 1, k, :], MUP[:, 1, k, :], tp[:], ALU.add)
        K.ts(MUP[:, 2, :, :], MUP[:, 1, :, :], -1.0, ALU.mult)
        K.dma(C.dMUP[:], MUP[:])
        chk(12)
        cpraw = K.sb("cpraw", [128, 4, 128], F32)
        for ri, nm in enumerate(("s5_c_re", "s5_c_im")):
            for pr in range(16):
                for h in range(2):
                    K.dma(cpraw.k((ri, pr, h))[(pr % 8) * 16:(pr % 8 + 1) * 16, ri * 2 + pr // 8, h * 64:(h + 1) * 64], I[nm][2 * pr + h], eng="sp" if h == 0 else "pool")
        chk(13)
        Cp = K.sb("Cp", [128, 2, 16, 16], F32)
        for q in range(4):
            K.tr(ps_b[:, q * 128:(q + 1) * 128], cpraw[:, q, :], identf[:])
        K.copy(Cp[:].re("p r q c -> p (r q c)"), ps_b[:])
        t4 = K.sb("t4", [128, 16, 16], F32)
        t5 = K.sb("t5", [128, 16, 16], F32)
        for i in range(8):
            pr_b = bc_last(Pr[:, i + 1, :], 16)
            pi_b = bc_last(Pi[:, i + 1, :], 16)
            K.tt(t4[:], Cp[:, 0, :, :], pr_b, ALU.mult)
            K.tt(t5[:], Cp[:, 1, :, :], pi_b, ALU.mult)
            K.tt(CsRe[:].re("p q (i c) -> p q i c", i=8)[:, :, i, :], t4[:], t5[:], ALU.subtract)
            K.tt(t4[:], Cp[:, 0, :, :], pi_b, ALU.mult)
            K.tt(t5[:], Cp[:, 1, :, :], pr_b, ALU.mult)
            K.tt(t4[:], t4[:], t5[:], ALU.add)
            K.ts(CsIm[:].re("p q (i c) -> p q i c", i=8)[:, :, i, :], t4[:], -1.0, ALU.mult)
        chk(14)
        K.dma(C.dTz[:], Tz[:]); K.dma(C.dBs[:], Bs[:]); K.dma(C.dCsRe[:], CsRe[:]); K.dma(C.dCsIm[:], CsIm[:]); K.dma(C.dMU[:], MU[:])
        if "s5prep" in dbg:
            for nm, t_, sh in (("Tz", Tz, [128, 32, 128]), ("Bs", Bs, [128, 32, 128]), ("CsRe", CsRe, [128, 16, 128]), ("CsIm", CsIm, [128, 16, 128])):
                o = dbgout(nm, sh)
                tf = K.sb("dbgf_" + nm, sh, F32)
                K.copy(tf[:], t_[:])
                K.dma(o[:], tf[:])
            o = dbgout("MU", [128, 3, 16])
            K.dma(o[:], MU[:])
    if stage == "S5prep":
        K.finish()
        return nc, DBG
    C.gm = gm; C.ropec = ropec; C.onesb = onesb; C.identf = identf
    build_layers(K, C, I, W, out, x1d, stage, dbg, dbgout)
    K.finish()
    return nc, DBG


def make_in_maps(inputs, ncores=NCORES):
    hc = host_consts()
    perm = swap_perm()
    w_in = np.asarray(inputs["ev_w_in"][0])
    q, k_, v, u = w_in[:, 0:512], w_in[:, 512:1024], w_in[:, 1024:1536], w_in[:, 1536:2048]
    w_in0 = np.ascontiguousarray(np.concatenate([q, q[:, perm], k_, k_[:, perm], v, u], axis=1))
    shared = {
        "norm_mix_pre": inputs["norm_mix_pre"], "norm_mix_post": inputs["norm_mix_post"],
        "norm_ffn_pre": inputs["norm_ffn_pre"], "norm_ffn_post": inputs["norm_ffn_post"],
        "w_in0": w_in0, "ev_w_out": inputs["ev_w_out"][0],
        "s5_a_re": inputs["s5_a_re"][0], "s5_a_im": inputs["s5_a_im"][0], "s5_log_dt": inputs["s5_log_dt"].reshape(1, 32),
        "s5_b_re": inputs["s5_b_re"][0], "s5_b_im": inputs["s5_b_im"][0], "s5_c_re": inputs["s5_c_re"][0], "s5_c_im": inputs["s5_c_im"][0],
        "s5_d": inputs["s5_d"][0], "s5_w_glu": inputs["s5_w_glu"][0],
        "od_w_in": inputs["od_w_in"][0], "od_w_out": inputs["od_w_out"][0],
        "rg_conv_w": inputs["rg_conv_w"][0], "rg_conv_b": inputs["rg_conv_b"].reshape(1, D),
        "rg_w_r": inputs["rg_w_r"][0].reshape(1024, 256), "rg_b_r": inputs["rg_b_r"].reshape(1, D),
        "rg_w_i": inputs["rg_w_i"][0].reshape(1024, 256), "rg_b_i": inputs["rg_b_i"].reshape(1, D), "rg_lam": inputs["rg_lam"].reshape(1, D),
    }
    for L in range(2):
        shared["ffn_g%d" % L] = inputs["ffn_w_gate"][L]
        shared["ffn_u%d" % L] = inputs["ffn_w_up"][L]
        shared["ffn_d%d" % L] = inputs["ffn_w_down"][L]
    shared.update(hc)
    shared = {k: np.ascontiguousarray(np.asarray(v_)) for k, v_ in shared.items()}
    maps = []
    x = np.asarray(inputs["x"])
    pos = np.asarray(inputs["positions"])
    for c in range(ncores):
        m = dict(shared)
        m["x"] = np.ascontiguousarray(x[c * NSEQ:(c + 1) * NSEQ].reshape(TOK, D))
        m["pos"] = np.ascontiguousarray(pos[c * NSEQ:(c + 1) * NSEQ].reshape(1, TOK).astype(np.int32))
        maps.append(m)
    return maps


def build_layers(K, C, I, W, out, x1d, stage, dbg, dbgout):
    nseq = NSEQ
    if stage in ("A", "B", "C", "D0"):
        nseq = 1
    gm = C.gm
    if stage == "L1a":
        C.g_mix_pre = [None, None]; C.g_mix_post = [None, None]; C.g_ffn_pre = [None, None]; C.g_ffn_post = [None, None]
        with K.phase():
            K.dma(x1d[0:1024, :], I["x"][0:1024, :])
        build_layer1(K, C, I, W, out, x1d, stage, dbg, dbgout)
        return
    with K.phase():
        C.g_mix_pre = [None, None]; C.g_mix_post = [None, None]; C.g_ffn_pre = [None, None]; C.g_ffn_post = [None, None]
        C.g_mix_pre[0] = load_bc(K, "g_mp0", I["norm_mix_pre"][0:1, :], D)
        C.g_mix_post[0] = load_bc(K, "g_mo0", I["norm_mix_post"][0:1, :], D)
        C.g_ffn_pre[0] = load_bc(K, "g_fp0", I["norm_ffn_pre"][0:1, :], D)
        C.g_ffn_post[0] = load_bc(K, "g_fo0", I["norm_ffn_post"][0:1, :], D)
        C.scr_j = K.sb("scr_j", [128, D], BF16)
        C.scr_f = K.sb("scr_f", [128, D], F32)
        C.scr_g = K.sb("scr_g", [128, D], F32)
        C.hn = K.sb("hn", [128, 4, D], BF16)
        B1 = K.sb("B1", [128, 4, SEQ], BF16)
        B2 = K.sb("B2", [128, 4, SEQ], BF16)
        B3 = K.sb("B3", [128, 8192], BF16)
        UT = K.sb("UT", [128, 32, 256], BF16)
        QT = B1; attT = B1; KT = B2; ssmT = B2
        V = B3[:].re("p (s f) -> p s f", f=512)
        ygT = B3[:].re("p (k t) -> p k t", k=4)
        for s in range(nseq):
            tok0 = s * SEQ
            with K.phase():
                xt = K.sb("xtA", [128, 4, D], F32)
                xnT = K.sb("xnT", [128, 8, 1024], BF16)
                wch = [K.sb("wch%d" % i, [128, 8, 512], BF16) for i in range(2)]
                posi = K.sb("posi", [128, 512], I32)
                ang = K.sb("angA", [128, 512], F32)
                cosT = K.sb("cosT", [128, 1024], F32)
                sinT = K.sb("sinT", [128, 1024], F32)
                sct = (K.sb("sct", [128, 512], F32), K.sb("scti", [128, 512], I32), K.sb("sckf", [128, 512], F32), K.sb("scr", [128, 512], F32))
                tA = K.sb("tA", [128, 512], F32)
                tB = K.sb("tB", [128, 512], F32)
                UA = K.sb("UA", [128, 32, 8, 16], BF16)
                C.pst = [K.ps("pst%d" % i, [128, 1024], BF16) for i in range(2)]
                psm = [K.ps("psm%d" % i, [128, 512], F32) for i in range(6)]
                pmi = 0
                wv = W["w_in0"][:].re("(kc p) n -> p kc n", p=128)
                for blk in range(2):
                    b0 = tok0 + blk * 1024
                    for half in range(2):
                        K.dma(xt[:], I["x"][b0 + half * 512:b0 + (half + 1) * 512, :].re("(s p) d -> p s d", p=128))
                        rms_pre(K, C, xt[:], 4, C.g_mix_pre[0], C.hn[:], "a")
                        transpose_to(K, C, C.hn[:], 4, xnT, half * 512)
                        K.dma(posi[:], I["pos"][0:1, b0 + half * 512:b0 + (half + 1) * 512].pb(128))
                        K.copy(ang[:], posi[:])
                        K.ts(ang[:], ang[:], C.ropec[:, 0:1], ALU.mult)
                        hsl = slice(half * 512, (half + 1) * 512)
                        sincos(K, C, ang[:], [128, 512], sinT[:, hsl], cosT[:, hsl], sct)
                        K.ts(sinT[:, hsl], sinT[:, hsl], C.ropec[:, 1:2], ALU.mult)
                    for qk in range(2):
                        dst = QT if qk == 0 else KT
                        wq, wsw = wch[0], wch[1]
                        K.dma(wq[:], wv[:, :, (2 * qk) * 512:(2 * qk + 1) * 512])
                        K.dma(wsw[:], wv[:, :, (2 * qk + 1) * 512:(2 * qk + 2) * 512])
                        for t in range(4):
                            for nh in range(2):
                                pq = psm[pmi % 6]; psw = psm[(pmi + 1) % 6]; pmi += 2
                                for kc in range(8):
                                    K.mm(pq[:], wq[:, kc, t * 128:(t + 1) * 128], xnT.k(kc)[:, kc, nh * 512:(nh + 1) * 512], start=(kc == 0), stop=(kc == 7))
                                for kc in range(8):
                                    K.mm(psw[:], wsw[:, kc, t * 128:(t + 1) * 128], xnT.k(kc)[:, kc, nh * 512:(nh + 1) * 512], start=(kc == 0), stop=(kc == 7))
                                K.tt(tA[:], pq[:], cosT[:, nh * 512:(nh + 1) * 512], ALU.mult)
                                K.tt(tB[:], psw[:], sinT[:, nh * 512:(nh + 1) * 512], ALU.mult)
                                c0 = blk * 1024 + nh * 512
                                K.tt(dst[:, t, c0:c0 + 512], tA[:], tB[:], ALU.add, eng="pool")
                    wvv = wch[0]
                    K.dma(wvv[:], wv[:, :, 4 * 512:5 * 512])
                    for sub in range(8):
                        pv = psm[pmi % 6]; pmi += 1
                        for kc in range(8):
                            K.mm(pv[:], xnT.k(kc)[:, kc, sub * 128:(sub + 1) * 128], wvv[:, kc, :], start=(kc == 0), stop=(kc == 7))
                        K.copy(V[:, blk * 8 + sub, :], pv[:], eng="act")
                    wu_ = wch[1]
                    K.dma(wu_[:], wv[:, :, 5 * 512:6 * 512])
                    for j in range(8):
                        pu = psm[pmi % 6]; pmi += 1
                        for kc in range(8):
                            K.mm(pu[:], xnT.k(kc)[:, kc, :].re("p (c j) -> p j c", j=8)[:, j, :], wu_[:, kc, :], start=(kc == 0), stop=(kc == 7))
                        K.copy(UA[:, :, j, :], pu[:].re("p (g c) -> p g c", c=16), eng="act")
                    for g4 in range(8):
                        pt = C.pst[g4 % 2]
                        for gi in range(4):
                            g = g4 * 4 + gi
                            K.tr(pt[:, gi * 128:(gi + 1) * 128], UA[:, g, :, :].re("p j c -> p (j c)"), C.identb[:])
                        K.copy(UT[:, g4 * 4:(g4 + 1) * 4, blk * 128:(blk + 1) * 128], pt[:, 0:512].re("p (g c) -> p g c", g=4), eng=("act" if g4 % 2 == 0 else "dve"))
            if "A" in dbg and s == 0:
                for nm, t_, sh in (("QT", QT[:], [128, 4, SEQ]), ("KT", KT[:], [128, 4, SEQ]), ("V", V, [128, 16, 512]), ("UT", UT[:], [128, 32, 256])):
                    with K.phase():
                        o = dbgout(nm, sh)
                        tf = K.sb("dbgf" + nm, sh, F32)
                        K.copy(tf[:], t_)
                        K.dma(o[:], tf[:])
            if stage == "A":
                raise Stop()
            with K.phase():
                pS = [K.ps("pS%d" % i, [128, 512], F32) for i in range(4)]
                pnum = [K.ps("pnum%d" % i, [128, 512], F32) for i in range(2)]
                pden = [K.ps("pden%d" % i, [128, 512], F32) for i in range(2)]
                Pb = [K.sb("Pb%d" % i, [128, 512], BF16) for i in range(4)]
                Pm = [K.sb("Pm%d" % i, [128, 512], BF16) for i in range(4)]
                rden = K.sb("rden", [128, 512], F32)
                Pacc = [[K.sb("Pacc%d_%d" % (g_, h_), [128, 512], F32) for h_ in range(2)] for g_ in range(2)]
                onesf = K.sb("onesf", [128, 64], F32)
                K.memset(onesf[:], 1.0)
                jobs = list(C.lazy.get(s, []))
                njobs = len(jobs)
                if njobs:
                    lzb = [K.sb("lzb%d" % i, [128, 2816], BF16) for i in range(4)]
                lzi = 0

                def emit_job():
                    nonlocal lzi
                    dst_, src_, r0_, n_, c_ = jobs.pop(0)
                    b_ = lzb[lzi % 4]
                    lzi += 1
                    K.dma(b_[0:n_, 0:c_], src_[r0_:r0_ + n_, :], eng="pool")
                    K.dma(dst_.k(r0_)[r0_:r0_ + n_, :], b_[0:n_, 0:c_], eng="sp")
                LA = 3
                items = []
                for t in range(4):
                    for qc in range(4):
                        nkb = 4 * qc + 4
                        for kb in range(nkb):
                            for hh in range(2):
                                items.append((t, qc, kb, hh, nkb))
                n_it = len(items)
                for it_ in range(n_it + LA):
                    if it_ < n_it:
                        t, qc, kb, hh, nkb = items[it_]
                        hs = slice(hh * 64, (hh + 1) * 64)
                        delta = 4 * qc - kb
                        ps_ = pS[it_ % 4]
                        K.mm(ps_[:], KT[hs, t, kb * 128:(kb + 1) * 128], QT.k((t, qc))[hs, t, qc * 512:(qc + 1) * 512])
                        pb_ = Pb[it_ % 4]; pm_ = Pm[it_ % 4]
                        K.act(pb_[:], ps_[:], AF.Exp, scale=0.125)
                        K.tt(pm_[:], pb_[:], gm[:, 128 * (delta + 3):128 * (delta + 3) + 512], ALU.mult, eng=("dve" if (njobs or it_ % 3 != 2) else "pool"))
                        if jobs and it_ % 8 == 4:
                            emit_job()
                    j_ = it_ - LA
                    if j_ >= 0:
                        t, qc, kb, hh, nkb = items[j_]
                        hs = slice(hh * 64, (hh + 1) * 64)
                        grp = t * 4 + qc
                        num = pnum[grp % 2]; den = pden[grp % 2]
                        pm_ = Pm[j_ % 4]
                        hcol = (2 * t + hh) * 64
                        K.mm(num[hs, :], V[:, kb, hcol:hcol + 64], pm_[:], start=(kb == 0), stop=(kb == nkb - 1))
                        pa_ = Pacc[grp % 2][hh]
                        if kb == 0:
                            K.copy(pa_[:], pm_[:])
                        else:
                            K.tt(pa_[:], pa_[:], pm_[:], ALU.add)
                        if kb == nkb - 1:
                            K.mm(den[hs, :], onesf[:, 0:64], pa_[:])
                        if kb == nkb - 1 and hh == 1:
                            K.recip(rden[:], den[:])
                            K.tt(attT.k((t, qc))[:, t, qc * 512:(qc + 1) * 512], num[:], rden[:], ALU.mult)
                while jobs:
                    emit_job()
            if "B" in dbg and s == 0:
                with K.phase():
                    o = dbgout("attT", [128, 4, SEQ]); tf = K.sb("dbgfa", [128, 4, SEQ], F32)
                    K.copy(tf[:], attT[:]); K.dma(o[:], tf[:])
            if stage == "B":
                raise Stop()
            with K.phase():
                EX = K.sb("EX", [128, 2, 16, 257], F32)
                EXb = K.sb("EXb", [128, 2, 16, 257], BF16)
                MU = K.sb("MUc", [128, 3, 16], F32)
                K.dma(MU[:], C.dMU[:])
                wglu = K.sb("wglu", [128, 4, 512], BF16)
                K.dma(wglu[:], W["s5_w_glu"][:].re("(kc p) n -> p kc n", p=128))
                ptr = [K.ps("ptr%d" % i, [128, 1024], BF16) for i in range(2)]
                pe_ = [K.ps("pe%d" % i, [128, 512], F32) for i in range(4)]
                tr1 = K.sb("tr1", [128, 2, 16], F32)
                tr2 = K.sb("tr2", [128, 2, 16], F32)
                with K.phase():
                    Bs = K.sb("BsC", [128, 32, 128], BF16)
                    K.dma(Bs[:], C.dBs[:])
                    K.memset(EX[:, :, :, 0:1], 0.0)
                    for pr in range(16):
                        for ri in range(2):
                            pp = pe_[(pr * 2 + ri) % 4]
                            for h in range(2):
                                g = 2 * pr + h
                                K.mm(pp[h * 64:(h + 1) * 64, 0:256], Bs[:, g, ri * 64:(ri + 1) * 64], UT[:, g, :])
                            K.copy(EX[:, ri, pr, 1:257], pp[:, 0:256], eng=("act" if ri == 0 else "dve"))
                chk(20)
                MUP = K.sb("MUPc", [128, 3, 16, 16], F32)
                K.dma(MUP[:], C.dMUP[:])
                XV = EX[:, :, :, 1:257].re("p r q (b i) -> p r q b i", i=16)
                T1 = K.sb("T1", [128, 2, 16, 16], F32)
                T2 = K.sb("T2", [128, 2, 16, 16], F32)

                def bc4(ref, nb):
                    return Ref(ref.tile, ref.key, ref.ap.unsqueeze(1).unsqueeze(3).to_broadcast([128, 2, 16, nb]))

                def bc3(ref, nb):
                    return Ref(ref.tile, ref.key, ref.ap.unsqueeze(2).to_broadcast([128, 16, nb]))

                def cmul_add(dst, src, k, nb):
                    K.tt(T1[:, :, :, 0:nb], src, bc4(MUP[:, 0, k, :], nb), ALU.mult)
                    K.tt(T2[:, 0, :, 0:nb], src[:, 1], bc3(MUP[:, 2, k, :], nb), ALU.mult)
                    K.tt(T2[:, 1, :, 0:nb], src[:, 0], bc3(MUP[:, 1, k, :], nb), ALU.mult)
                    K.tt(T1[:, :, :, 0:nb], T1[:, :, :, 0:nb], T2[:, :, :, 0:nb], ALU.add)
                    K.tt(dst, dst, T1[:, :, :, 0:nb], ALU.add)

                for i in range(1, 16):
                    cmul_add(XV[:, :, :, :, i], XV[:, :, :, :, i - 1], 0, 16)
                for bb in range(1, 16):
                    cmul_add(XV[:, :, :, bb:bb + 1, 15], XV[:, :, :, bb - 1:bb, 15], 15, 1)
                for i in range(15):
                    cmul_add(XV[:, :, :, 1:16, i], XV[:, :, :, 0:15, 15], i, 15)
                chk(21)
                K.copy(EXb[:], EX[:])
                if "C1" in dbg and s == 0:
                    o = dbgout("EX", [128, 2, 16, 257])
                    K.dma(o[:], EX[:])
                with K.phase():
                    Tz = K.sb("TzC", [128, 32, 128], BF16)
                    CsRe = K.sb("CsReC", [128, 16, 128], BF16)
                    CsIm = K.sb("CsImC", [128, 16, 128], BF16)
                    K.dma(Tz[:], C.dTz[:]); K.dma(CsRe[:], C.dCsRe[:]); K.dma(CsIm[:], C.dCsIm[:])
                    Yg = [K.sb("Yg%d" % i, [128, 128], BF16) for i in range(2)]
                    YGb = K.sb("YGb", [128, 8, 512], BF16)
                    sg_ = [K.sb("sgC%d" % i, [128, 512], BF16) for i in range(2)]
                    if "C1" in dbg and s == 0:
                        oYA = dbgout("YA", [2, 128, 8, 512])
                        YAf = K.sb("YAf", [128, 8, 512], F32)
                    for blk in range(2):
                        cs_ = slice(blk * 128, (blk + 1) * 128)
                        for g4 in range(8):
                            pt = ptr[g4 % 2]
                            for gi in range(4):
                                g = g4 * 4 + gi
                                pr, h = g // 2, g % 2
                                hs = slice(h * 64, (h + 1) * 64)
                                py = pe_[g % 4]
                                K.mm(py[:, 0:128], Tz[:, g, :], UT[:, g, cs_], start=True, stop=False)
                                K.mm(py[:, 0:128], CsRe[hs, pr, :], EXb[hs, 0, pr, blk * 128:blk * 128 + 128], start=False, stop=False)
                                K.mm(py[:, 0:128], CsIm[hs, pr, :], EXb[hs, 1, pr, blk * 128:blk * 128 + 128], start=False, stop=True)
                                yg_ = Yg[g % 2]
                                K.copy(yg_[:], py[:, 0:128], eng="act")
                                K.tr(pt[:, gi * 128:(gi + 1) * 128], yg_[:], C.identb[:])
                            dstv = YGb[:, :, g4 * 64:(g4 + 1) * 64].re("p i (g c) -> p g i c", g=4)
                            srcv = pt[:, 0:512].re("p (g i c) -> p g i c", g=4, i=8)
                            if "C1" in dbg and s == 0:
                                K.copy(YAf[:, :, g4 * 64:(g4 + 1) * 64].re("p i (g c) -> p g i c", g=4), srcv)
                            K.act(dstv, srcv, AF.Gelu_apprx_tanh)
                        if "C1" in dbg and s == 0:
                            K.dma(oYA[blk], YAf[:])
                        for fc in range(4):
                            pt = ptr[fc % 2]
                            for i in range(8):
                                K.tr(pt[:, i * 128:(i + 1) * 128], YGb[:, i, fc * 128:(fc + 1) * 128], C.identb[:])
                            K.copy(ygT[:, fc, blk * 1024:(blk + 1) * 1024].re("p (c i) -> p i c", i=8), pt[:].re("p (i c) -> p i c", i=8),
                                   eng=("act" if fc % 2 == 0 else "dve"))
                    chk(22)
                    for mt in range(4):
                        for nq in range(4):
                            pg = pe_[(mt * 4 + nq) % 4]
                            for kc in range(4):
                                K.mm(pg[:], wglu[:, kc, mt * 128:(mt + 1) * 128], ygT[:, kc, nq * 512:(nq + 1) * 512], start=(kc == 0), stop=(kc == 3))
                            sg = sg_[(mt * 4 + nq) % 2]
                            K.act(sg[:], pg[:], AF.Sigmoid)
                            K.tt(ssmT[:, mt, nq * 512:(nq + 1) * 512], sg[:], ygT[:, mt, nq * 512:(nq + 1) * 512], ALU.mult)
            chk(23)
            if "C" in dbg and s == 0:
                with K.phase():
                    o = dbgout("ssmT", [128, 4, SEQ]); tf = K.sb("dbgfs", [128, 4, SEQ], F32)
                    K.copy(tf[:], ssmT[:]); K.dma(o[:], tf[:])
            if stage == "C":
                raise Stop()
            with K.phase():
                xts = [K.sb("xtD%d" % i, [128, 4, D], F32) for i in range(2)]
                C.hnT = K.sb("hnT", [128, 8, 512], BF16)
                C.actT = K.sb("actT", [128, NFT, 512], BF16)
                C.wgb = [K.sb("wgb%d" % i, [128, 8, 256], BF16) for i in range(3)]
                C.wub = [K.sb("wub%d" % i, [128, 8, 256], BF16) for i in range(3)]
                C.sgb = [K.sb("sgb%d" % i, [128, 512], F32) for i in range(2)]
                C.rotb = [K.sb("rotb%d" % i, [128, D], BF16) for i in range(6)]
                alloc_psum(K, C, "pD")
                wov = W["ev_w_out"][:].re("(kc p) n -> p kc n", p=128)
                K.dma(xts[0][:], I["x"][tok0:tok0 + 512, :].re("(s p) d -> p s d", p=128), eng="pool")
                for tl in range(4):
                    t0 = tok0 + tl * 512
                    xt = xts[tl % 2]
                    if tl + 1 < 4:
                        K.dma(xts[(tl + 1) % 2][:], I["x"][t0 + 512:t0 + 1024, :].re("(s p) d -> p s d", p=128), eng="pool")
                    proj_tokmajor(K, C, lambda kc, sub: (attT if kc < 4 else ssmT)[:, kc % 4, tl * 512 + sub * 128:tl * 512 + (sub + 1) * 128],
                                  8, wov, C.g_mix_post[0], xt, "mp")
                    if "D0" in dbg and s == 0 and tl == 0:
                        o = dbgout("xa0", [128, 4, D])
                        K.dma(o[:], xt[:])
                    ffn_tile(K, C, xt, 0, W)
                    K.dma(x1d[t0:t0 + 512, :].re("(s p) d -> p s d", p=128), xt[:], eng="pool")
                    if "D0" in dbg and s == 0 and tl == 0:
                        o = dbgout("x1", [128, 4, D])
                        K.dma(o[:], xt[:])
                    if stage == "D0":
                        raise Stop()
    build_layer1(K, C, I, W, out, x1d, stage, dbg, dbgout)


def build_layer1(K, C, I, W, out, x1d, stage, dbg, dbgout):
    ntl = 16
    if stage == "L1a":
        ntl = 2
    with K.phase():
        C.g_mix_pre[1] = load_bc(K, "g_mp1", I["norm_mix_pre"][1:2, :], D)
        C.g_mix_post[1] = load_bc(K, "g_mo1", I["norm_mix_post"][1:2, :], D)
        C.g_ffn_pre[1] = load_bc(K, "g_fp1", I["norm_ffn_pre"][1:2, :], D)
        C.g_ffn_post[1] = load_bc(K, "g_fo1", I["norm_ffn_post"][1:2, :], D)
        C.scr_j = K.sb("scr_j1", [128, D], BF16)
        C.scr_f = K.sb("scr_f1", [128, D], F32)
        C.scr_g = K.sb("scr_g1", [128, D], F32)
        C.hn = K.sb("hn1", [128, 4, D], BF16)
        xts = [K.sb("xt1_%d" % i, [128, 4, D], F32) for i in range(2)]
        C.hnT = K.sb("hnT1", [128, 8, 512], BF16)
        C.actT = K.sb("actT1", [128, NFT, 512], BF16)
        C.wgb = [K.sb("wgb1_%d" % i, [128, 8, 256], BF16) for i in range(2)]
        C.wub = [K.sb("wub1_%d" % i, [128, 8, 256], BF16) for i in range(2)]
        C.sgb = [K.sb("sgb1_%d" % i, [128, 512], F32) for i in range(2)]
        C.rotb = [K.sb("rotb1_%d" % i, [128, D], BF16) for i in range(6)]
        alloc_psum(K, C, "pL")
        xbuf = K.sb("xbuf", [128, 8, 515], F32)
        xc = K.sb("xc", [128, 8, 512], F32)
        yT = C.hnT
        xcb = C.actT[:, 0:8, :]
        gz = C.actT[:, 8:16, :]
        GRb = K.sb("GRb", [128, 4, 512], BF16); GIb = K.sb("GIb", [128, 4, 512], BF16)
        A4 = K.sb("A4", [128, 4, 512], F32); M4 = K.sb("M4", [128, 4, 512], F32)
        hbuf = [K.sb("h_%d" % i, [128, 512], F32) for i in range(2)]
        hc = K.sb("hc", [128, 8], F32)
        cw = K.sb("cw", [128, 8, 4], F32)
        cb = K.sb("cb", [128, 8], F32); br = K.sb("br", [128, 8], F32); bi = K.sb("bi", [128, 8], F32)
        cch = K.sb("cch", [128, 8], F32)
        Wr = K.sb("Wr", [128, 8, 256], BF16); Wi = K.sb("Wi", [128, 8, 256], BF16)
        for j in range(4):
            K.dma(cw[:, :, j], I["rg_conv_w"][j:j + 1, :].re("o (ct p) -> p (o ct)", p=128), allow_slow_non_contiguous=True)
        for t_, nm in ((cb, "rg_conv_b"), (br, "rg_b_r"), (bi, "rg_b_i"), (cch, "rg_lam")):
            K.dma(t_[:], I[nm][0:1, :].re("o (ct p) -> p (o ct)", p=128), allow_slow_non_contiguous=True)
        K.act(cch[:], cch[:], AF.Exp, scale=-1.0)
        K.act(cch[:], cch[:], AF.Ln, bias=C.onec[:, 0:1])
        K.ts(cch[:], cch[:], -8.0, ALU.mult)
        cch2 = K.sb("cch2", [128, 8], F32)
        K.ts(cch2[:], cch[:], 2.0, ALU.mult)
        K.dma(Wr[:], W["rg_w_r"][:].re("(q p) n -> p q n", p=128))
        K.dma(Wi[:], W["rg_w_i"][:].re("(q p) n -> p q n", p=128))
        wiv = W["od_w_in"][:].re("(kc p) n -> p kc n", p=128)
        wov = W["od_w_out"][:].re("(kc p) n -> p kc n", p=128)
        K.dma(xts[0][:], x1d[0:512, :].re("(s p) d -> p s d", p=128), eng="pool")
        for tl in range(ntl):
            t0 = tl * 512
            first = (tl % 4 == 0)
            xt = xts[tl % 2]
            if tl + 1 < ntl:
                K.dma(xts[(tl + 1) % 2][:], x1d[t0 + 512:t0 + 1024, :].re("(s p) d -> p s d", p=128), eng="pool")
            rms_pre(K, C, xt[:], 4, C.g_mix_pre[1], C.hn[:], "l")
            transpose_to(K, C, C.hn[:], 4, C.hnT, 0)
            if first:
                K.memset(xbuf[:, :, 0:3], 0.0)
            else:
                K.copy(xbuf[:, :, 0:3], xbuf[:, :, 512:515])
            for ch in range(8):
                wb = C.wgb[ch % 2] if ch % 4 < 2 else C.wub[ch % 2]
                K.dma(wb[:], wiv[:, :, ch * 256:(ch + 1) * 256], eng="sp")
                for mi in range(2):
                    mt = ch * 2 + mi
                    pz = C.psm[C.pmi % 6]; C.pmi += 1
                    for kc in range(8):
                        K.mm(pz[:], wb[:, kc, mi * 128:(mi + 1) * 128], C.hnT.k(kc)[:, kc, :], start=(kc == 0), stop=(kc == 7))
                    if mt < 8:
                        ct = mt
                        K.copy(xbuf.k(ct)[:, ct, 3:515], pz[:], eng="act")
                        K.ts(xc.k(ct)[:, ct, :], xbuf.k(ct)[:, ct, 0:512], cw[:, ct, 0:1], ALU.mult, cb[:, ct:ct + 1], ALU.add)
                        for j in range(1, 4):
                            K.stt(xc.k(ct)[:, ct, :], xbuf.k(ct)[:, ct, j:j + 512], cw[:, ct, j:j + 1], xc.k(ct)[:, ct, :], ALU.mult, ALU.add)
                        K.copy(C.actT.k(("x", ct))[:, ct, :], xc.k(ct)[:, ct, :], eng="pool")
                    else:
                        K.act(C.actT.k(("g", mt - 8))[:, mt, :], pz[:], AF.Gelu_apprx_tanh)
            for half in range(2):
                cts = list(range(half * 4, half * 4 + 4))
                for j, ct in enumerate(cts):
                    hq = (ct // 2) * 2
                    cs_ = slice((ct % 2) * 128, (ct % 2 + 1) * 128)
                    pr_ = C.psm[C.pmi % 6]; pi_ = C.psm[(C.pmi + 1) % 6]; C.pmi += 2
                    for kc in range(2):
                        K.mm(pr_[:], Wr[:, hq + kc, cs_], C.actT.k(("x", hq + kc))[:, hq + kc, :], start=(kc == 0), stop=(kc == 1))
                    for kc in range(2):
                        K.mm(pi_[:], Wi[:, hq + kc, cs_], C.actT.k(("x", hq + kc))[:, hq + kc, :], start=(kc == 0), stop=(kc == 1))
                    K.act(GRb.k(j)[:, j, :], pr_[:], AF.Sigmoid, bias=br[:, ct:ct + 1])
                    K.act(GIb.k(j)[:, j, :], pi_[:], AF.Sigmoid, bias=bi[:, ct:ct + 1])
                    K.tt(xc.k(ct)[:, ct, :], xc.k(ct)[:, ct, :], GIb.k(j)[:, j, :], ALU.mult, eng="pool")
                for j, ct in enumerate(cts):
                    K.act(A4.k(j)[:, j, :], GRb.k(j)[:, j, :], AF.Exp, scale=cch[:, ct:ct + 1])
                    K.act(M4.k(j)[:, j, :], GRb.k(j)[:, j, :], AF.Exp, scale=cch2[:, ct:ct + 1])
                for j, ct in enumerate(cts):
                    K.act(M4.k(j)[:, j, :], M4.k(j)[:, j, :], AF.Sqrt, scale=-1.0, bias=C.onec[:, 0:1])
                for j, ct in enumerate(cts):
                    K.tt(M4.k(j)[:, j, :], M4.k(j)[:, j, :], xc.k(ct)[:, ct, :], ALU.mult)
                    h_ = hbuf[ct % 2]
                    if first:
                        K.scan(h_[:], A4.k(j)[:, j, :], M4.k(j)[:, j, :], 0.0)
                    else:
                        K.scan(h_[:], A4.k(j)[:, j, :], M4.k(j)[:, j, :], hc.k(ct)[:, ct:ct + 1])
                    K.copy(hc.k(ct)[:, ct:ct + 1], h_[:, 511:512], eng="pool")
                    K.tt(yT.k(ct)[:, ct, :], h_[:], C.actT.k(("g", ct))[:, 8 + ct, :], ALU.mult)
                    if "L1" in dbg and tl < 2:
                        if ct == 0 and tl == 0:
                            C.oh = dbgout("h1", [2, 8, 128, 512])
                        K.dma(C.oh[tl, ct], h_[:])
            proj_tokmajor(K, C, lambda kc, sub: yT[:, kc, sub * 128:(sub + 1) * 128], 8, wov, C.g_mix_post[1], xt, "mq")
            if "L1" in dbg and tl < 2:
                if tl == 0:
                    C.oxa = dbgout("xa1", [2, 128, 4, D])
                K.dma(C.oxa[tl], xt[:])
            ffn_tile(K, C, xt, 1, W)
            K.dma(out[t0:t0 + 512, :].re("(s p) d -> p s d", p=128), xt[:], eng="pool")


def kernel(**inputs):
    nc, _ = build("full")
    maps = make_in_maps(inputs)
    res = run_bass_kernel_spmd(nc, maps, core_ids=list(range(NCORES)))
    outs = [np.asarray(r["out"], dtype=np.float32).reshape(NSEQ, SEQ, D) for r in res.results]
    return np.concatenate(outs, axis=0)
```

```python
import numpy as np
import ml_dtypes
import concourse.bass as bass
import concourse.mybir as mybir
from concourse.bass_utils import run_bass_kernel_spmd

F32 = mybir.dt.float32
BF16 = mybir.dt.bfloat16
I32 = mybir.dt.int32
AF = mybir.ActivationFunctionType
ALU = mybir.AluOpType
AX = mybir.AxisListType

SEM_LIMIT = 24000


class Ref:
    __slots__ = ("tile", "key", "ap")

    def __init__(self, tile, key, ap):
        self.tile = tile
        self.key = key
        self.ap = ap

    def __getitem__(self, idx):
        return Ref(self.tile, self.key, self.ap[idx])

    def re(self, pat, **kw):
        return Ref(self.tile, self.key, self.ap.rearrange(pat, **kw))


class _Keyed:
    def __init__(self, tile, key):
        self.tile = tile
        self.key = key

    def __getitem__(self, idx):
        return Ref(self.tile, self.key, self.tile.h[idx])


class Tile:
    def __init__(self, h, name):
        self.h = h
        self.name = name
        self.reg = {}

    def __getitem__(self, idx):
        return Ref(self, "*", self.h[idx])

    def k(self, key):
        return _Keyed(self, key)


def _merge(dst, src):
    for s, v in src.items():
        if dst.get(s, 0) < v:
            dst[s] = v


class Kern:
    ENG = ("pe", "act", "dve", "pool", "sp")

    def __init__(self, nc, sync_same=("act", "dve", "pool")):
        self.nc = nc
        self.stack = []
        self.prog = {e: [] for e in self.ENG}
        self.free_sems = []
        self.nsem = 0
        self.csem = {}
        self.ccnt = {}
        self.waited = {e: {} for e in self.ENG}
        self.dsems = {e: [] for e in self.ENG}
        self.drr = {e: 0 for e in self.ENG}
        self.dcnt = {}
        self.sync_same = set(sync_same)
        self.semname = {}
        for e in self.ENG:
            self._new_csem(e)
        self.ndma_sems = {"sp": 6, "act": 3, "pool": 4, "dve": 0, "pe": 0}
        for e in self.ENG:
            for _ in range(self.ndma_sems[e]):
                self.dsems[e].append(self._alloc_sem())

    def _alloc_sem(self):
        cm = self.nc.semaphore("s%d" % self.nsem)
        self.nsem += 1
        h = cm.__enter__()
        self.stack.append(cm)
        self.dcnt[h] = 0
        return h

    def _new_csem(self, e):
        self.csem[e] = self._alloc_sem()
        self.ccnt[e] = 0

    def sb(self, name, shape, dt):
        self.uid = getattr(self, "uid", 0) + 1
        name = "sb%d_%s" % (self.uid, name)
        cm = self.nc.sbuf_tensor(name, list(shape), dt)
        h = cm.__enter__()
        self.stack.append(cm)
        return Tile(h, name)

    def ps(self, name, shape, dt=F32):
        self.uid = getattr(self, "uid", 0) + 1
        name = "ps%d_%s" % (self.uid, name)
        nbytes = int(np.prod(shape[1:])) * (4 if dt == F32 else 2)
        assert nbytes == 2048, "PSUM tiles must be exactly one bank"
        cm = self.nc.psum_tensor(name, list(shape), dt)
        h = cm.__enter__()
        self.stack.append(cm)
        t = Tile(h, name)
        t.psum = True
        return t

    def dram(self, name, shape, dt, kind="Internal"):
        t = self.nc.dram_tensor(name, list(shape), dt, kind=kind)
        return Tile(t.ap(), name)

    def _conf(self, ref):
        t = ref.tile
        if ref.key == "*":
            return list(t.reg.values())
        out = []
        if "*" in t.reg:
            out.append(t.reg["*"])
        if ref.key in t.reg:
            out.append(t.reg[ref.key])
        return out

    def emit(self, eng, fn, reads=(), writes=(), dma=False):
        pr = [r for r in reads if getattr(r.tile, "psum", False)]
        if pr:
            reads = [r for r in reads if not getattr(r.tile, "psum", False)]
            writes = list(writes) + pr
        need = {}
        for r in reads:
            for w, _ in self._conf(r):
                _merge(need, w)
        for wr in writes:
            for w, rd in self._conf(wr):
                _merge(need, w)
                _merge(need, rd)
        waits = []
        wd = self.waited[eng]
        own = self.csem[eng]
        for s, v in need.items():
            if wd.get(s, 0) >= v:
                continue
            if (not dma) and s is own and eng not in self.sync_same:
                continue
            wd[s] = v
            waits.append((s, v))
        if dma:
            lst = self.dsems[eng]
            i = self.drr[eng] % len(lst)
            self.drr[eng] += 1
            s = lst[i]
            if self.dcnt[s] + 16 > SEM_LIMIT:
                s = self._alloc_sem()
                lst[i] = s
            self.dcnt[s] += 16
            ev = (s, self.dcnt[s])
            inc = 16
        else:
            if self.ccnt[eng] + 1 > SEM_LIMIT:
                self._new_csem(eng)
            self.ccnt[eng] += 1
            ev = (self.csem[eng], self.ccnt[eng])
            inc = 1
        self.prog[eng].append((waits, fn, ev[0], inc))
        evd = {ev[0]: ev[1]}
        for r in reads:
            reg = r.tile.reg.setdefault(r.key, [{}, {}])
            _merge(reg[1], evd)
        for wr in writes:
            if wr.key == "*":
                wr.tile.reg = {"*": [dict(evd), {}]}
            else:
                wr.tile.reg[wr.key] = [dict(evd), {}]
        return ev

    def wait_all(self, eng, refs):
        need = {}
        for r in refs:
            for w, rd in self._conf(r):
                _merge(need, w)
        waits = [(s, v) for s, v in need.items()]
        self.prog[eng].append((waits, None, None, 0))

    def dma(self, out, in_, eng="sp", **kw):
        return self.emit(eng, lambda e: e.dma_start(out=out.ap, in_=in_.ap, **kw),
                         reads=[in_], writes=[out], dma=True)

    def mm(self, out, lhsT, rhs, start=True, stop=True, **kw):
        return self.emit("pe", lambda e: e.matmul(out.ap, lhsT.ap, rhs.ap, start=start, stop=stop, **kw),
                         reads=[lhsT, rhs], writes=[out])

    def tr(self, out, in_, ident):
        return self.emit("pe", lambda e: e.transpose(out.ap, in_.ap, ident.ap),
                         reads=[in_, ident], writes=[out])

    def act(self, out, in_, func, bias=None, scale=None, accum=None, eng="act", extra_reads=()):
        kw = {}
        reads = [in_] + list(extra_reads)
        writes = [out]
        if bias is not None:
            if isinstance(bias, Ref):
                kw["bias"] = bias.ap
                reads.append(bias)
            else:
                kw["bias"] = bias
        if scale is not None:
            if isinstance(scale, Ref):
                kw["scale"] = scale.ap
                reads.append(scale)
            else:
                kw["scale"] = scale
        if accum is not None:
            kw["accum_out"] = accum.ap
            writes.append(accum)
        return self.emit(eng, lambda e: e.activation(out.ap, in_.ap, func, **kw), reads=reads, writes=writes)

    def tt(self, out, a, b, op, eng="dve"):
        return self.emit(eng, lambda e: e.tensor_tensor(out.ap, a.ap, b.ap, op), reads=[a, b], writes=[out])

    def ts(self, out, a, s1, op0, s2=None, op1=None, accum=None, eng="dve"):
        reads = [a]
        writes = [out]
        v1 = s1
        v2 = s2
        if isinstance(s1, Ref):
            reads.append(s1)
            v1 = s1.ap
        if isinstance(s2, Ref):
            reads.append(s2)
            v2 = s2.ap
        kw = {}
        if op1 is not None:
            kw["op1"] = op1
        if accum is not None:
            kw["accum_out"] = accum.ap
            writes.append(accum)
        return self.emit(eng, lambda e: e.tensor_scalar(out.ap, a.ap, v1, v2, op0, **kw), reads=reads, writes=writes)

    def stt(self, out, a, s, b, op0, op1, eng="dve"):
        reads = [a, b]
        v = s
        if isinstance(s, Ref):
            reads.append(s)
            v = s.ap
        return self.emit(eng, lambda e: e.scalar_tensor_tensor(out.ap, a.ap, v, b.ap, op0, op1), reads=reads, writes=[out])

    def copy(self, out, in_, eng="dve"):
        if eng == "act":
            return self.emit(eng, lambda e: e.copy(out.ap, in_.ap), reads=[in_], writes=[out])
        return self.emit(eng, lambda e: e.tensor_copy(out.ap, in_.ap), reads=[in_], writes=[out])

    def memset(self, out, val, eng="dve"):
        return self.emit(eng, lambda e: e.memset(out.ap, val), reads=[], writes=[out])

    def scan(self, out, d0, d1, init, op0=ALU.mult, op1=ALU.add, eng="dve"):
        reads = [d0, d1]
        v = init
        if isinstance(init, Ref):
            reads.append(init)
            v = init.ap
        return self.emit(eng, lambda e: e.tensor_tensor_scan(out.ap, d0.ap, d1.ap, v, op0, op1), reads=reads, writes=[out])

    def recip(self, out, in_):
        return self.emit("dve", lambda e: e.reciprocal(out.ap, in_.ap), reads=[in_], writes=[out])

    def finish(self):
        nc = self.nc
        prog = self.prog
        with nc.Block() as block:
            def run(e, name):
                for waits, fn, sem, inc in prog[name]:
                    for s, v in waits:
                        e.wait_ge(s, v)
                    if fn is not None:
                        fn(e).then_inc(sem, inc)

            @block.sync
            def _(e):
                run(e, "sp")

            @block.scalar
            def _(e):
                run(e, "act")

            @block.vector
            def _(e):
                run(e, "dve")

            @block.gpsimd
            def _(e):
                run(e, "pool")

            @block.tensor
            def _(e):
                run(e, "pe")
        while self.stack:
            self.stack.pop().__exit__(None, None, None)


def _ref_bc(self, shape):
    return Ref(self.tile, self.key, self.ap.to_broadcast(list(shape)))


def _ref_pb(self, n):
    return Ref(self.tile, self.key, self.ap.partition_broadcast(n))


Ref.bc = _ref_bc
Ref.pb = _ref_pb


class _Phase:
    def __init__(self, K):
        self.K = K

    def __enter__(self):
        self.h = len(self.K.stack)
        return self

    def __exit__(self, *a):
        K = self.K
        K.barrier()
        K.flush()
        while len(K.stack) > self.h:
            K.stack.pop().__exit__(None, None, None)
        return False


def _phase(self):
    return _Phase(self)


def _barrier(self):
    need = {}
    for e in self.ENG:
        if self.ccnt[e] > 0:
            need[self.csem[e]] = self.ccnt[e]
        for s in self.dsems[e]:
            if self.dcnt[s] > 0:
                need[s] = self.dcnt[s]
    for e in self.ENG:
        wd = self.waited[e]
        waits = []
        for s, v in need.items():
            if wd.get(s, 0) >= v:
                continue
            wd[s] = v
            waits.append((s, v))
        if waits:
            self.prog[e].append((waits, None, None, 0))


def _flush(self):
    nc = self.nc
    prog = self.prog
    self.prog = {e: [] for e in self.ENG}
    if not any(prog.values()):
        return
    with nc.Block() as block:
        def run(e, name):
            for waits, fn, sem, inc in prog[name]:
                for s, v in waits:
                    e.wait_ge(s, v)
                if fn is not None:
                    fn(e).then_inc(sem, inc)

        @block.sync
        def _(e):
            run(e, "sp")

        @block.scalar
        def _(e):
            run(e, "act")

        @block.vector
        def _(e):
            run(e, "dve")

        @block.gpsimd
        def _(e):
            run(e, "pool")

        @block.tensor
        def _(e):
            run(e, "pe")


def _finish(self):
    self.barrier()
    self.flush()
    while self.stack:
        self.stack.pop().__exit__(None, None, None)


Kern.phase = _phase
Kern.barrier = _barrier
Kern.flush = _flush
Kern.finish = _finish


import math

NCORES = 8
D = 1024
SEQ = 2048
NSEQ = 4
TOK = NSEQ * SEQ
FF = 2816
NFT = FF // 128
EPS = 1e-6
PI = math.pi
C1 = 6.28125
C2 = 2 * math.pi - 6.28125


def host_consts():
    c = {}
    c["ident"] = np.eye(128, dtype=np.float32)
    invf = np.zeros((128, 1), np.float32)
    sgn = np.zeros((128, 1), np.float32)
    for p in range(128):
        i = p % 64
        if i < 16:
            invf[p, 0] = np.float32(500000.0) ** np.float32(-((i % 8) * 2.0 / 16.0))
            sgn[p, 0] = -1.0 if i < 8 else 1.0
    c["rope_cols"] = np.concatenate([invf, sgn], axis=1)
    x = np.arange(2816)[None, :] - np.arange(128)[:, None] - 384
    m = ((x >= 0) & (x <= 128)).astype(np.float32) + ((x >= 0) & (x % 4 == 0) & (x <= 512)) + ((x >= 0) & (x % 16 == 0) & (x <= 2048))
    c["gmask"] = m.astype(np.float32)
    hm = np.zeros((128, 3), np.float32)
    hm[:64, 0] = 1
    hm[64:, 1] = 1
    hm[:64, 2] = 1
    hm[64:, 2] = -1
    c["hmask"] = hm
    return c


def swap_perm():
    perm = np.arange(512)
    for h in range(8):
        for i in range(16):
            perm[h * 64 + i] = h * 64 + (i + 8 if i < 8 else i - 8)
    return perm


class Ctx:
    pass


def load_bc(K, name, src_row, n, eng="sp"):
    t = K.sb(name, [128, n], F32)
    K.dma(t[:], src_row.pb(128), eng=eng)
    return t


def rms_pre(K, C, xt, nsub, gain, out_bf, tagp):
    for s in range(nsub):
        ss = C.small.k(tagp + "ss%d" % s)[:, C.si:C.si + 1]
        sq = C.scr_j[:, 0:D]
        K.act(sq, xt[:, s, :], AF.Square, accum=ss)
        rs = C.small.k(tagp + "rs%d" % s)[:, C.si + 1:C.si + 2]
        K.act(rs, ss, AF.Sqrt, scale=1.0 / D, bias=C.epsc[:, 0:1])
        K.recip(rs, rs)
        K.stt(out_bf[:, s, :], xt[:, s, :], rs, gain[:], ALU.mult, ALU.mult)
        C.si = (C.si + 2) % 60


def transpose_to(K, C, src_bf, nsub, dstT, col0):
    for kc in range(D // 128):
        pt = C.pst[C.pti % len(C.pst)]
        C.pti += 1
        for s in range(nsub):
            K.tr(pt[:, s * 128:(s + 1) * 128], src_bf[:, s, kc * 128:(kc + 1) * 128], C.identb[:])
        eng = "act" if kc % 2 == 0 else "dve"
        K.copy(dstT.k(kc)[:, kc, col0:col0 + nsub * 128], pt[:, 0:nsub * 128], eng=eng)


def post_norm_add(K, C, ps_pair, gain, xt_sub, tagp, sub=0):
    ssa = C.small.k(tagp + "a")[:, C.si:C.si + 1]
    ssb = C.small.k(tagp + "b")[:, C.si + 1:C.si + 2]
    rs = C.small.k(tagp + "c")[:, C.si + 2:C.si + 3]
    C.si = (C.si + 3) % 60
    C.pn = getattr(C, "pn", 0) + 1
    scr = C.scr_f if C.pn % 2 == 0 else C.scr_g
    K.act(C.scr_j[:, 0:512], ps_pair[0], AF.Square, accum=ssa)
    K.act(C.scr_j[:, 512:1024], ps_pair[1], AF.Square, accum=ssb)
    K.tt(rs, ssa, ssb, ALU.add)
    K.act(rs, rs, AF.Sqrt, scale=1.0 / D, bias=C.epsc[:, 0:1])
    K.recip(rs, rs)
    for h in range(2):
        tmp = scr.k("ab"[h])[:, h * 512:(h + 1) * 512]
        K.stt(tmp, ps_pair[h], rs, gain[:, h * 512:(h + 1) * 512], ALU.mult, ALU.mult)
        xa = Ref(xt_sub.tile, ("xt", sub, h), xt_sub.ap[:, h * 512:(h + 1) * 512])
        K.tt(xa, xa, tmp, ALU.add, eng=("pool" if h == 0 else "dve"))


def rot(C):
    b = C.rotb[C.roti % len(C.rotb)]
    C.roti += 1
    return b


class View:
    def __init__(self, tile, ap):
        self.tile = tile
        self.ap = ap

    def __getitem__(self, idx):
        return Ref(self.tile, "*", self.ap[idx])


def alloc_psum(K, C, tag):
    C.pall = [K.ps("%s%d" % (tag, i), [128, 512], F32) for i in range(8)]
    C.psm = C.pall[0:6]
    C.pst = [View(t, t.h[:].bitcast(BF16)) for t in C.pall[6:8]]


def proj_tokmajor(K, C, lhs_fn, nk, wview, gain, xt, tagp):
    for ps_ in range(2):
        acc = C.pall[ps_ * 4:ps_ * 4 + 4]
        for k in range(nk):
            wb = rot(C)
            K.dma(wb[:], wview[:, k, :], eng="sp")
            for si in range(2):
                sub = ps_ * 2 + si
                for h in range(2):
                    K.mm(acc[si * 2 + h][:], lhs_fn(k, sub), wb[:, h * 512:(h + 1) * 512], start=(k == 0), stop=(k == nk - 1))
        for si in range(2):
            sub = ps_ * 2 + si
            post_norm_add(K, C, (acc[si * 2][:], acc[si * 2 + 1][:]), gain, xt[:, sub, :], tagp, sub)


def ffn_tile(K, C, xt, L, W):
    hn = C.hn
    rms_pre(K, C, xt[:], 4, C.g_ffn_pre[L], hn[:], "f")
    transpose_to(K, C, hn[:], 4, C.hnT, 0)
    NCH = FF // 256
    gv = W["ffn_g%d" % L][:].re("(kc p) n -> p kc n", p=128)
    uv = W["ffn_u%d" % L][:].re("(kc p) n -> p kc n", p=128)
    for ch in range(NCH):
        wg = C.wgb[ch % len(C.wgb)]
        wu = C.wub[ch % len(C.wub)]
        K.dma(wg[:], gv[:, :, ch * 256:(ch + 1) * 256], eng="sp")
        K.dma(wu[:], uv[:, :, ch * 256:(ch + 1) * 256], eng="sp")
        for mi in range(2):
            m = ch * 2 + mi
            pg = C.psm[C.pmi % len(C.psm)]
            pu = C.psm[(C.pmi + 1) % len(C.psm)]
            C.pmi += 2
            for kc in range(8):
                K.mm(pg[:], wg[:, kc, mi * 128:(mi + 1) * 128], C.hnT.k(kc)[:, kc, :], start=(kc == 0), stop=(kc == 7))
            for kc in range(8):
                K.mm(pu[:], wu[:, kc, mi * 128:(mi + 1) * 128], C.hnT.k(kc)[:, kc, :], start=(kc == 0), stop=(kc == 7))
            sg = C.sgb[m % 2]
            K.act(sg[:], pg[:], AF.Silu)
            K.tt(C.actT[:, m, :], sg[:], pu[:], ALU.mult)
    wdv = W["ffn_d%d" % L][:].re("(m p) n -> p m n", p=128)
    proj_tokmajor(K, C, lambda m, sub: C.actT[:, m, sub * 128:(sub + 1) * 128], NFT, wdv, C.g_ffn_post[L], xt, "fp")


def bc_last(ref, n):
    sh = list(ref.ap.shape)
    return Ref(ref.tile, ref.key, ref.ap.unsqueeze(len(sh)).to_broadcast(sh + [n]))


def bc_mid(ref, n):
    sh = list(ref.ap.shape)
    return Ref(ref.tile, ref.key, ref.ap.unsqueeze(1).to_broadcast([sh[0], n] + sh[1:]))


def sincos(K, C, ang, shape, sn, cs, tg):
    if isinstance(tg, str):
        t = K.sb(tg + "_t", shape, F32)
        ti = K.sb(tg + "_ti", shape, I32)
        kf = K.sb(tg + "_kf", shape, F32)
        r = K.sb(tg + "_r", shape, F32)
    else:
        t, ti, kf, r = tg
    for shift, dst in ((0.0, sn), (PI / 2, cs)):
        K.ts(t[:], ang, 1.0 / (2 * PI), ALU.mult, shift / (2 * PI), ALU.add)
        K.copy(ti[:], t[:])
        K.copy(kf[:], ti[:])
        K.stt(r[:], kf[:], -C1, ang, ALU.mult, ALU.add)
        K.stt(r[:], kf[:], -C2, r[:], ALU.mult, ALU.add)
        K.ts(r[:], r[:], shift, ALU.add, PI, ALU.min)
        K.ts(r[:], r[:], -PI, ALU.max)
        K.act(dst, r[:], AF.Sin)


def cast_weight(K, C, dst, src, rows, cols):
    r0 = 0
    while r0 < rows:
        n = min(128, rows - r0)
        b = C.castb[C.casti % len(C.castb)]
        C.casti += 1
        K.dma(b[0:n, 0:cols], src[r0:r0 + n, :], eng="pool")
        K.dma(dst[r0:r0 + n, :], b[0:n, 0:cols], eng="sp")
        r0 += n


class Stop(Exception):
    pass


CUT = [None]


def chk(n):
    if CUT[0] == n:
        raise Stop()


def build(stage="full", dbg=()):
    try:
        return _build(stage, dbg)
    except Stop:
        K = LASTK[0]
        K.finish()
        return K.nc, DBGG[0]


LASTK = [None]
DBGG = [None]


def _build(stage="full", dbg=()):
    nc = bass.Bass("TRN2", target_bir_lowering=False)
    K = Kern(nc)
    LASTK[0] = K
    C = Ctx()
    C.si = 0
    C.pti = 0
    C.pmi = 0
    C.casti = 0
    C.roti = 0
    I = {}

    def inp(name, shape, dt=F32):
        I[name] = K.dram(name, shape, dt, kind="ExternalInput")
        return I[name]

    inp("x", [TOK, D])
    inp("pos", [1, TOK], I32)
    for nm in ("norm_mix_pre", "norm_mix_post", "norm_ffn_pre", "norm_ffn_post"):
        inp(nm, [2, D])
    inp("w_in0", [D, 3072])
    inp("ev_w_out", [D, D])
    inp("s5_a_re", [32, 64]); inp("s5_a_im", [32, 64]); inp("s5_log_dt", [1, 32])
    inp("s5_b_re", [32, 64, 16]); inp("s5_b_im", [32, 64, 16])
    inp("s5_c_re", [32, 16, 64]); inp("s5_c_im", [32, 16, 64]); inp("s5_d", [32, 16])
    inp("s5_w_glu", [512, 512])
    inp("od_w_in", [D, 2048]); inp("od_w_out", [D, D])
    inp("rg_conv_w", [4, D]); inp("rg_conv_b", [1, D])
    inp("rg_w_r", [1024, 256]); inp("rg_b_r", [1, D]); inp("rg_w_i", [1024, 256]); inp("rg_b_i", [1, D]); inp("rg_lam", [1, D])
    for L in range(2):
        inp("ffn_g%d" % L, [D, FF]); inp("ffn_u%d" % L, [D, FF]); inp("ffn_d%d" % L, [FF, D])
    inp("ident", [128, 128]); inp("rope_cols", [128, 2]); inp("gmask", [128, 2816]); inp("hmask", [128, 3])
    out = K.dram("out", [TOK, D], F32, kind="ExternalOutput")
    DBG = {}
    DBGG[0] = DBG

    def dbgout(name, shape, dt=F32):
        DBG[name] = K.dram("dbg_" + name, shape, dt, kind="ExternalOutput")
        return DBG[name]

    W = {}
    for nm, r, c in (("w_in0", D, 3072), ("ev_w_out", D, D), ("s5_w_glu", 512, 512), ("od_w_in", D, 2048), ("od_w_out", D, D),
                     ("rg_w_r", 1024, 256), ("rg_w_i", 1024, 256),
                     ("ffn_g0", D, FF), ("ffn_u0", D, FF), ("ffn_d0", FF, D), ("ffn_g1", D, FF), ("ffn_u1", D, FF), ("ffn_d1", FF, D)):
        if stage != "S5prep":
            W[nm] = K.dram("wb_" + nm, [r, c], BF16)
    x1d = K.dram("x1d", [TOK, D], F32) if stage != "S5prep" else None

    identf = K.sb("identf", [128, 128], F32)
    C.identb = K.sb("identb", [128, 128], BF16)
    C.epsc = K.sb("epsc", [128, 1], F32)
    C.onec = K.sb("onec", [128, 1], F32)
    C.small = K.sb("small", [128, 64], F32)
    onesb = K.sb("onesb", [128, 64], BF16)
    gm = K.sb("gm", [128, 2816], BF16)
    ropec = K.sb("ropec", [128, 2], F32)
    hmask = K.sb("hmask", [128, 3], F32)
    C.dTz = K.dram("dTz", [128, 32, 128], BF16)
    C.dBs = K.dram("dBs", [128, 32, 128], BF16)
    C.dCsRe = K.dram("dCsRe", [128, 16, 128], BF16)
    C.dCsIm = K.dram("dCsIm", [128, 16, 128], BF16)
    C.dMU = K.dram("dMU", [128, 3, 16], F32)
    C.dMUP = K.dram("dMUP", [128, 3, 16, 16], F32)
    K.dma(identf[:], I["ident"][:])
    junk = K.sb("junk", [1, 64], F32)
    junki = K.sb("junki", [1, 4], I32)
    for nm_, t_ in I.items():
        flat = t_[:]
        while len(flat.ap.shape) > 1:
            flat = flat[0]
        if nm_ == "pos":
            K.dma(junki[0:1, 0:1], Ref(flat.tile, "*", flat.ap[0:1].unsqueeze(0)))
        else:
            K.dma(junk[0:1, 0:1], Ref(flat.tile, "*", flat.ap[0:1].unsqueeze(0)))
    if stage != "full":
        K.dma(out[0:1, 0:64], junk[:])
    chk(-3)
    K.copy(C.identb[:], identf[:])
    K.memset(C.epsc[:], EPS)
    K.memset(C.onec[:], 1.0)
    K.memset(onesb[:], 1.0)
    K.dma(ropec[:], I["rope_cols"][:])
    K.dma(hmask[:], I["hmask"][:])
    chk(-2)
    K.dma(gm[:], I["gmask"][:], eng="pool")
    chk(-1)

    with K.phase():
        C.castb = [K.sb("castb%d" % i, [128, 3072], BF16) for i in range(4)]
        names = list(W.keys())
        C.lazy = {}
        if stage == "full":
            names = ["w_in0"]
            l0 = ["ev_w_out", "s5_w_glu", "ffn_g0", "ffn_u0", "ffn_d0"]
            l1 = ["od_w_in", "od_w_out", "rg_w_r", "rg_w_i", "ffn_g1", "ffn_u1", "ffn_d1"]
            for sq_, lst in ((0, l0), (1, l1)):
                jobs = []
                for nm in lst:
                    r, c = W[nm].h.shape
                    for r0 in range(0, r, 128):
                        jobs.append((W[nm], I[nm], r0, min(128, r - r0), c))
                C.lazy[sq_] = jobs
        if stage in ("A", "S5prep"):
            names = ["w_in0"]
        if stage == "L1a":
            names = ["od_w_in", "od_w_out", "rg_w_r", "rg_w_i", "ffn_g1", "ffn_u1", "ffn_d1"]
        if stage == "S5prep":
            names = []
        for nm in names:
            r, c = W[nm].h.shape
            cast_weight(K, C, W[nm], I[nm], r, c)

    chk(0)
    with K.phase():
        ps_a = K.ps("ps_a", [128, 512], F32)
        ps_b = K.ps("ps_b", [128, 512], F32)
        Tz = K.sb("Tz", [128, 32, 128], BF16)
        Bs = K.sb("Bs", [128, 32, 128], BF16)
        CsRe = K.sb("CsRe", [128, 16, 128], BF16)
        CsIm = K.sb("CsIm", [128, 16, 128], BF16)
        MU = K.sb("MU", [128, 3, 16], F32)
        araw = K.sb("araw", [32, 2, 128], F32)
        for j, nm in enumerate(("s5_a_re", "s5_a_im")):
            K.dma(araw[:, j, 0:64], I[nm][:])
            K.dma(araw[:, j, 64:128], I[nm][:])
        are = K.sb("are", [128, 32], F32)
        aim = K.sb("aim", [128, 32], F32)
        K.tr(ps_a[:, 0:32], araw[:, 0, :], identf[0:32, 0:32])
        K.tr(ps_a[:, 32:64], araw[:, 1, :], identf[0:32, 0:32])
        K.copy(are[:], ps_a[:, 0:32])
        K.copy(aim[:], ps_a[:, 32:64])
        chk(1)
        dtb = K.sb("dtb", [128, 32], F32)
        K.dma(dtb[:], I["s5_log_dt"][0:1, :].pb(128))
        K.act(dtb[:], dtb[:], AF.Exp)
        mag = K.sb("mag", [128, 32], F32)
        ang = K.sb("ang", [128, 32], F32)
        K.tt(mag[:], are[:], dtb[:], ALU.mult)
        K.act(mag[:], mag[:], AF.Exp)
        K.tt(ang[:], aim[:], dtb[:], ALU.mult)
        chk(2)
        sn = K.sb("sn", [128, 32], F32)
        cs = K.sb("cs", [128, 32], F32)
        sincos(K, C, ang[:], [128, 32], sn[:], cs[:], "sc1")
        chk(3)
        lr = K.sb("lr", [128, 32], F32)
        li = K.sb("li", [128, 32], F32)
        K.tt(lr[:], mag[:], cs[:], ALU.mult)
        K.tt(li[:], mag[:], sn[:], ALU.mult)
        PA = K.sb("PA", [128, 8, 32], F32)
        PB = K.sb("PB", [128, 8, 32], F32)
        tq = K.sb("tq", [128, 32], F32)
        K.copy(PA[:, 0, :], hmask[:, 0:1].bc([128, 32]))
        K.copy(PB[:, 0, :], hmask[:, 1:2].bc([128, 32]))
        for k in range(1, 8):
            K.tt(tq[:], li[:], PB[:, k - 1, :], ALU.mult)
            K.tt(PA[:, k, :], lr[:], PA[:, k - 1, :], ALU.mult)
            K.tt(PA[:, k, :], PA[:, k, :], tq[:], ALU.add)
            K.tt(tq[:], li[:], PA[:, k - 1, :], ALU.mult)
            K.tt(PB[:, k, :], lr[:], PB[:, k - 1, :], ALU.mult)
            K.tt(PB[:, k, :], PB[:, k, :], tq[:], ALU.subtract)
        lm1 = K.sb("lm1", [128, 32], F32)
        K.ts(lm1[:], lr[:], -1.0, ALU.add)
        den = K.sb("den", [128, 32], F32)
        K.tt(den[:], are[:], are[:], ALU.mult)
        K.tt(tq[:], aim[:], aim[:], ALU.mult)
        K.tt(den[:], den[:], tq[:], ALU.add)
        K.recip(den[:], den[:])
        wr = K.sb("wr", [128, 32], F32)
        wi = K.sb("wi", [128, 32], F32)
        K.tt(wr[:], lm1[:], are[:], ALU.mult)
        K.tt(tq[:], li[:], aim[:], ALU.mult)
        K.tt(wr[:], wr[:], tq[:], ALU.add)
        K.tt(wr[:], wr[:], den[:], ALU.mult)
        K.tt(wi[:], li[:], are[:], ALU.mult)
        K.tt(tq[:], lm1[:], aim[:], ALU.mult)
        K.tt(wi[:], wi[:], tq[:], ALU.subtract)
        K.tt(wi[:], wi[:], den[:], ALU.mult)
        chk(4)
        bre = K.sb("bre", [128, 32, 16], F32)
        bim = K.sb("bim", [128, 32, 16], F32)
        for t_, nm in ((bre, "s5_b_re"), (bim, "s5_b_im")):
            for h in range(2):
                K.dma(t_.k(h)[h * 64:(h + 1) * 64, :, :], I[nm][:].re("g p c -> p g c"), eng=("sp" if h == 0 else "pool"))
        chk(5)
        Bbr = K.sb("Bbr", [128, 32, 16], F32)
        Bbi = K.sb("Bbi", [128, 32, 16], F32)
        t3 = K.sb("t3", [128, 32, 16], F32)
        K.tt(Bbr[:], bre[:], bc_last(wr[:], 16), ALU.mult)
        K.tt(t3[:], bim[:], bc_last(wi[:], 16), ALU.mult)
        K.tt(Bbr[:], Bbr[:], t3[:], ALU.subtract)
        K.tt(Bbi[:], bim[:], bc_last(wr[:], 16), ALU.mult)
        K.tt(t3[:], bre[:], bc_last(wi[:], 16), ALU.mult)
        K.tt(Bbi[:], Bbi[:], t3[:], ALU.add)
        Wpad = K.sb("Wpad", [128, 32, 15, 16], F32)
        K.memset(Wpad[:], 0.0)
        for m in range(8):
            k = 7 - m
            K.tt(Wpad[:, :, m, :], Bbr[:], bc_last(PA[:, k, :], 16), ALU.mult)
            K.tt(t3[:], Bbi[:], bc_last(PB[:, k, :], 16), ALU.mult)
            K.tt(Wpad[:, :, m, :], Wpad[:, :, m, :], t3[:], ALU.add)
        chk(6)
        craw = K.sb("craw", [128, 4, 128], F32)
        for t_ in range(4):
            K.dma(craw.k((t_, 0))[:, t_, 0:64], I["s5_c_re"][t_ * 8:(t_ + 1) * 8].re("g c p -> (g c) p"))
            K.dma(craw.k((t_, 1))[:, t_, 64:128], I["s5_c_im"][t_ * 8:(t_ + 1) * 8].re("g c p -> (g c) p"), eng="pool")
        Vst = K.sb("Vst", [128, 32, 16], F32)
        for t_ in range(4):
            K.tr(ps_a[:, t_ * 128:(t_ + 1) * 128], craw[:, t_, :], identf[:])
        K.ts(Vst[:].re("p g c -> p (g c)"), ps_a[:], hmask[:, 2:3], ALU.mult)
        chk(7)
        dcol = K.sb("dcol", [128, 32], F32)
        for j in range(8):
            K.dma(dcol.k(j)[j * 16:(j + 1) * 16, :], I["s5_d"][:].re("g c -> c g"), allow_slow_non_contiguous=True, eng=("sp" if j % 2 == 0 else "pool"))
        chk(8)
        for g in range(32):
            pz = ps_b if g % 2 == 0 else ps_a
            for i in range(8):
                K.mm(pz[:, i * 16:(i + 1) * 16], Wpad[:, g, 7 - i:15 - i, :].re("p m c -> p (m c)"), Vst[:, g, :])
            K.stt(Tz[:, g, :], identf[:], dcol[:, g:g + 1], pz[:, 0:128], ALU.mult, ALU.add)
            K.tr(pz[:, 128:256], Wpad[:, g, 0:8, :].re("p m c -> p (m c)"), identf[:])
            K.copy(Bs[:, g, :], pz[:, 128:256], eng="act")
        chk(9)
        praw = K.sb("praw", [16, 2, 128], F32)
        K.dma(praw[:, 0, :], I["s5_a_re"][:].re("(pr h) p -> pr (h p)", h=2))
        K.dma(praw[:, 1, :], I["s5_a_im"][:].re("(pr h) p -> pr (h p)", h=2))
        K.tr(ps_a[:, 0:16], praw[:, 0, :], identf[0:16, 0:16])
        K.tr(ps_a[:, 16:32], praw[:, 1, :], identf[0:16, 0:16])
        arp = K.sb("arp", [128, 16], F32)
        aip = K.sb("aip", [128, 16], F32)
        K.copy(arp[:], ps_a[:, 0:16])
        K.copy(aip[:], ps_a[:, 16:32])
        chk(10)
        dtp = K.sb("dtp", [128, 16], F32)
        ldv = I["s5_log_dt"][0:1, :].re("o (pr h) -> o h pr", h=2)
        K.dma(dtp[0:64, :], ldv[:, 0, :].pb(64), allow_slow_non_contiguous=True)
        K.dma(dtp[64:128, :], ldv[:, 1, :].pb(64), allow_slow_non_contiguous=True)
        K.act(dtp[:], dtp[:], AF.Exp)
        chk(11)
        magp = K.sb("magp", [128, 16], F32)
        angp = K.sb("angp", [128, 16], F32)
        K.tt(magp[:], arp[:], dtp[:], ALU.mult)
        K.act(magp[:], magp[:], AF.Exp)
        K.tt(angp[:], aip[:], dtp[:], ALU.mult)
        snp = K.sb("snp", [128, 16], F32)
        csp = K.sb("csp", [128, 16], F32)
        sincos(K, C, angp[:], [128, 16], snp[:], csp[:], "sc2")
        Pr = K.sb("Pr", [128, 9, 16], F32)
        Pi = K.sb("Pi", [128, 9, 16], F32)
        K.tt(Pr[:, 1, :], magp[:], csp[:], ALU.mult)
        K.tt(Pi[:, 1, :], magp[:], snp[:], ALU.mult)
        tp = K.sb("tp", [128, 16], F32)
        for k in range(2, 9):
            K.tt(tp[:], Pi[:, 1, :], Pi[:, k - 1, :], ALU.mult)
            K.tt(Pr[:, k, :], Pr[:, 1, :], Pr[:, k - 1, :], ALU.mult)
            K.tt(Pr[:, k, :], Pr[:, k, :], tp[:], ALU.subtract)
            K.tt(tp[:], Pi[:, 1, :], Pr[:, k - 1, :], ALU.mult)
            K.tt(Pi[:, k, :], Pr[:, 1, :], Pi[:, k - 1, :], ALU.mult)
            K.tt(Pi[:, k, :], Pi[:, k, :], tp[:], ALU.add)
        K.copy(MU[:, 0, :], Pr[:, 8, :])
        K.copy(MU[:, 1, :], Pi[:, 8, :])
        K.ts(MU[:, 2, :], Pi[:, 8, :], -1.0, ALU.mult)
        MUP = K.sb("MUP", [128, 3, 16, 16], F32)
        K.copy(MUP[:, 0, 0, :], Pr[:, 8, :])
        K.copy(MUP[:, 1, 0, :], Pi[:, 8, :])
        for k in range(1, 16):
            K.tt(tp[:], MU[:, 1, :], MUP[:, 1, k - 1, :], ALU.mult)
            K.tt(MUP[:, 0, k, :], MU[:, 0, :], MUP[:, 0, k - 1, :], ALU.mult)
            K.tt(MUP[:, 0, k, :], MUP[:, 0, k, :], tp[:], ALU.subtract)
            K.tt(tp[:], MU[:, 1, :], MUP[:, 0, k - 1, :], ALU.mult)
            K.tt(MUP[:, 1, k, :], MU[:, 0, :], MUP[:, 1, k - 1, :], ALU.mult)
            K.tt(MUP[:, 1, k, :], MUP[:, 1, k, :], tp[:], ALU.add)
        K.ts(MUP[:, 2, :, :], MUP[:, 1, :, :], -1.0, ALU.mult)
        K.dma(C.dMUP[:], MUP[:])
        chk(12)
        cpraw = K.sb("cpraw", [128, 4, 128], F32)
        for ri, nm in enumerate(("s5_c_re", "s5_c_im")):
            for pr in range(16):
                for h in range(2):
                    K.dma(cpraw.k((ri, pr, h))[(pr % 8) * 16:(pr % 8 + 1) * 16, ri * 2 + pr // 8, h * 64:(h + 1) * 64], I[nm][2 * pr + h], eng="sp" if h == 0 else "pool")
        chk(13)
        Cp = K.sb("Cp", [128, 2, 16, 16], F32)
        for q in range(4):
            K.tr(ps_b[:, q * 128:(q + 1) * 128], cpraw[:, q, :], identf[:])
        K.copy(Cp[:].re("p r q c -> p (r q c)"), ps_b[:])
        t4 = K.sb("t4", [128, 16, 16], F32)
        t5 = K.sb("t5", [128, 16, 16], F32)
        for i in range(8):
            pr_b = bc_last(Pr[:, i + 1, :], 16)
            pi_b = bc_last(Pi[:, i + 1, :], 16)
            K.tt(t4[:], Cp[:, 0, :, :], pr_b, ALU.mult)
            K.tt(t5[:], Cp[:, 1, :, :], pi_b, ALU.mult)
            K.tt(CsRe[:].re("p q (i c) -> p q i c", i=8)[:, :, i, :], t4[:], t5[:], ALU.subtract)
            K.tt(t4[:], Cp[:, 0, :, :], pi_b, ALU.mult)
            K.tt(t5[:], Cp[:, 1, :, :], pr_b, ALU.mult)
            K.tt(t4[:], t4[:], t5[:], ALU.add)
            K.ts(CsIm[:].re("p q (i c) -> p q i c", i=8)[:, :, i, :], t4[:], -1.0, ALU.mult)
        chk(14)
        K.dma(C.dTz[:], Tz[:]); K.dma(C.dBs[:], Bs[:]); K.dma(C.dCsRe[:], CsRe[:]); K.dma(C.dCsIm[:], CsIm[:]); K.dma(C.dMU[:], MU[:])
        if "s5prep" in dbg:
            for nm, t_, sh in (("Tz", Tz, [128, 32, 128]), ("Bs", Bs, [128, 32, 128]), ("CsRe", CsRe, [128, 16, 128]), ("CsIm", CsIm, [128, 16, 128])):
                o = dbgout(nm, sh)
                tf = K.sb("dbgf_" + nm, sh, F32)
                K.copy(tf[:], t_[:])
                K.dma(o[:], tf[:])
            o = dbgout("MU", [128, 3, 16])
            K.dma(o[:], MU[:])
    if stage == "S5prep":
        K.finish()
        return nc, DBG
    C.gm = gm; C.ropec = ropec; C.onesb = onesb; C.identf = identf
    build_layers(K, C, I, W, out, x1d, stage, dbg, dbgout)
    K.finish()
    return nc, DBG


def make_in_maps(inputs, ncores=NCORES):
    hc = host_consts()
    perm = swap_perm()
    w_in = np.asarray(inputs["ev_w_in"][0])
    q, k_, v, u = w_in[:, 0:512], w_in[:, 512:1024], w_in[:, 1024:1536], w_in[:, 1536:2048]
    w_in0 = np.ascontiguousarray(np.concatenate([q, q[:, perm], k_, k_[:, perm], v, u], axis=1))
    shared = {
        "norm_mix_pre": inputs["norm_mix_pre"], "norm_mix_post": inputs["norm_mix_post"],
        "norm_ffn_pre": inputs["norm_ffn_pre"], "norm_ffn_post": inputs["norm_ffn_post"],
        "w_in0": w_in0, "ev_w_out": inputs["ev_w_out"][0],
        "s5_a_re": inputs["s5_a_re"][0], "s5_a_im": inputs["s5_a_im"][0], "s5_log_dt": inputs["s5_log_dt"].reshape(1, 32),
        "s5_b_re": inputs["s5_b_re"][0], "s5_b_im": inputs["s5_b_im"][0], "s5_c_re": inputs["s5_c_re"][0], "s5_c_im": inputs["s5_c_im"][0],
        "s5_d": inputs["s5_d"][0], "s5_w_glu": inputs["s5_w_glu"][0],
        "od_w_in": inputs["od_w_in"][0], "od_w_out": inputs["od_w_out"][0],
        "rg_conv_w": inputs["rg_conv_w"][0], "rg_conv_b": inputs["rg_conv_b"].reshape(1, D),
        "rg_w_r": inputs["rg_w_r"][0].reshape(1024, 256), "rg_b_r": inputs["rg_b_r"].reshape(1, D),
        "rg_w_i": inputs["rg_w_i"][0].reshape(1024, 256), "rg_b_i": inputs["rg_b_i"].reshape(1, D), "rg_lam": inputs["rg_lam"].reshape(1, D),
    }
    for L in range(2):
        shared["ffn_g%d" % L] = inputs["ffn_w_gate"][L]
        shared["ffn_u%d" % L] = inputs["ffn_w_up"][L]
        shared["ffn_d%d" % L] = inputs["ffn_w_down"][L]
    shared.update(hc)
    shared = {k: np.ascontiguousarray(np.asarray(v_)) for k, v_ in shared.items()}
    maps = []
    x = np.asarray(inputs["x"])
    pos = np.asarray(inputs["positions"])
    for c in range(ncores):
        m = dict(shared)
        m["x"] = np.ascontiguousarray(x[c * NSEQ:(c + 1) * NSEQ].reshape(TOK, D))
        m["pos"] = np.ascontiguousarray(pos[c * NSEQ:(c + 1) * NSEQ].reshape(1, TOK).astype(np.int32))
        maps.append(m)
    return maps


def build_layers(K, C, I, W, out, x1d, stage, dbg, dbgout):
    nseq = NSEQ
    if stage in ("A", "B", "C", "D0"):
        nseq = 1
    gm = C.gm
    if stage == "L1a":
        C.g_mix_pre = [None, None]; C.g_mix_post = [None, None]; C.g_ffn_pre = [None, None]; C.g_ffn_post = [None, None]
        with K.phase():
            K.dma(x1d[0:1024, :], I["x"][0:1024, :])
        build_layer1(K, C, I, W, out, x1d, stage, dbg, dbgout)
        return
    with K.phase():
        C.g_mix_pre = [None, None]; C.g_mix_post = [None, None]; C.g_ffn_pre = [None, None]; C.g_ffn_post = [None, None]
        C.g_mix_pre[0] = load_bc(K, "g_mp0", I["norm_mix_pre"][0:1, :], D)
        C.g_mix_post[0] = load_bc(K, "g_mo0", I["norm_mix_post"][0:1, :], D)
        C.g_ffn_pre[0] = load_bc(K, "g_fp0", I["norm_ffn_pre"][0:1, :], D)
        C.g_ffn_post[0] = load_bc(K, "g_fo0", I["norm_ffn_post"][0:1, :], D)
        C.scr_j = K.sb("scr_j", [128, D], BF16)
        C.scr_f = K.sb("scr_f", [128, D], F32)
        C.scr_g = K.sb("scr_g", [128, D], F32)
        C.hn = K.sb("hn", [128, 4, D], BF16)
        B1 = K.sb("B1", [128, 4, SEQ], BF16)
        B2 = K.sb("B2", [128, 4, SEQ], BF16)
        B3 = K.sb("B3", [128, 8192], BF16)
        UT = K.sb("UT", [128, 32, 256], BF16)
        QT = B1; attT = B1; KT = B2; ssmT = B2
        V = B3[:].re("p (s f) -> p s f", f=512)
        ygT = B3[:].re("p (k t) -> p k t", k=4)
        for s in range(nseq):
            tok0 = s * SEQ
            with K.phase():
                xt = K.sb("xtA", [128, 4, D], F32)
                xnT = K.sb("xnT", [128, 8, 1024], BF16)
                wch = [K.sb("wch%d" % i, [128, 8, 512], BF16) for i in range(2)]
                posi = K.sb("posi", [128, 512], I32)
                ang = K.sb("angA", [128, 512], F32)
                cosT = K.sb("cosT", [128, 1024], F32)
                sinT = K.sb("sinT", [128, 1024], F32)
                sct = (K.sb("sct", [128, 512], F32), K.sb("scti", [128, 512], I32), K.sb("sckf", [128, 512], F32), K.sb("scr", [128, 512], F32))
                tA = K.sb("tA", [128, 512], F32)
                tB = K.sb("tB", [128, 512], F32)
                UA = K.sb("UA", [128, 32, 8, 16], BF16)
                C.pst = [K.ps("pst%d" % i, [128, 1024], BF16) for i in range(2)]
                psm = [K.ps("psm%d" % i, [128, 512], F32) for i in range(6)]
                pmi = 0
                wv = W["w_in0"][:].re("(kc p) n -> p kc n", p=128)
                for blk in range(2):
                    b0 = tok0 + blk * 1024
                    for half in range(2):
                        K.dma(xt[:], I["x"][b0 + half * 512:b0 + (half + 1) * 512, :].re("(s p) d -> p s d", p=128))
                        rms_pre(K, C, xt[:], 4, C.g_mix_pre[0], C.hn[:], "a")
                        transpose_to(K, C, C.hn[:], 4, xnT, half * 512)
                        K.dma(posi[:], I["pos"][0:1, b0 + half * 512:b0 + (half + 1) * 512].pb(128))
                        K.copy(ang[:], posi[:])
                        K.ts(ang[:], ang[:], C.ropec[:, 0:1], ALU.mult)
                        hsl = slice(half * 512, (half + 1) * 512)
                        sincos(K, C, ang[:], [128, 512], sinT[:, hsl], cosT[:, hsl], sct)
                        K.ts(sinT[:, hsl], sinT[:, hsl], C.ropec[:, 1:2], ALU.mult)
                    for qk in range(2):
                        dst = QT if qk == 0 else KT
                        wq, wsw = wch[0], wch[1]
                        K.dma(wq[:], wv[:, :, (2 * qk) * 512:(2 * qk + 1) * 512])
                        K.dma(wsw[:], wv[:, :, (2 * qk + 1) * 512:(2 * qk + 2) * 512])
                        for t in range(4):
                            for nh in range(2):
                                pq = psm[pmi % 6]; psw = psm[(pmi + 1) % 6]; pmi += 2
                                for kc in range(8):
                                    K.mm(pq[:], wq[:, kc, t * 128:(t + 1) * 128], xnT.k(kc)[:, kc, nh * 512:(nh + 1) * 512], start=(kc == 0), stop=(kc == 7))
                                for kc in range(8):
                                    K.mm(psw[:], wsw[:, kc, t * 128:(t + 1) * 128], xnT.k(kc)[:, kc, nh * 512:(nh + 1) * 512], start=(kc == 0), stop=(kc == 7))
                                K.tt(tA[:], pq[:], cosT[:, nh * 512:(nh + 1) * 512], ALU.mult)
                                K.tt(tB[:], psw[:], sinT[:, nh * 512:(nh + 1) * 512], ALU.mult)
                                c0 = blk * 1024 + nh * 512
                                K.tt(dst[:, t, c0:c0 + 512], tA[:], tB[:], ALU.add, eng="pool")
                    wvv = wch[0]
                    K.dma(wvv[:], wv[:, :, 4 * 512:5 * 512])
                    for sub in range(8):
                        pv = psm[pmi % 6]; pmi += 1
                        for kc in range(8):
                            K.mm(pv[:], xnT.k(kc)[:, kc, sub * 128:(sub + 1) * 128], wvv[:, kc, :], start=(kc == 0), stop=(kc == 7))
                        K.copy(V[:, blk * 8 + sub, :], pv[:], eng="act")
                    wu_ = wch[1]
                    K.dma(wu_[:], wv[:, :, 5 * 512:6 * 512])
                    for j in range(8):
                        pu = psm[pmi % 6]; pmi += 1
                        for kc in range(8):
                            K.mm(pu[:], xnT.k(kc)[:, kc, :].re("p (c j) -> p j c", j=8)[:, j, :], wu_[:, kc, :], start=(kc == 0), stop=(kc == 7))
                        K.copy(UA[:, :, j, :], pu[:].re("p (g c) -> p g c", c=16), eng="act")
                    for g4 in range(8):
                        pt = C.pst[g4 % 2]
                        for gi in range(4):
                            g = g4 * 4 + gi
                            K.tr(pt[:, gi * 128:(gi + 1) * 128], UA[:, g, :, :].re("p j c -> p (j c)"), C.identb[:])
                        K.copy(UT[:, g4 * 4:(g4 + 1) * 4, blk * 128:(blk + 1) * 128], pt[:, 0:512].re("p (g c) -> p g c", g=4), eng=("act" if g4 % 2 == 0 else "dve"))
            if "A" in dbg and s == 0:
                for nm, t_, sh in (("QT", QT[:], [128, 4, SEQ]), ("KT", KT[:], [128, 4, SEQ]), ("V", V, [128, 16, 512]), ("UT", UT[:], [128, 32, 256])):
                    with K.phase():
                        o = dbgout(nm, sh)
                        tf = K.sb("dbgf" + nm, sh, F32)
                        K.copy(tf[:], t_)
                        K.dma(o[:], tf[:])
            if stage == "A":
                raise Stop()
            with K.phase():
                pS = [K.ps("pS%d" % i, [128, 512], F32) for i in range(4)]
                pnum = [K.ps("pnum%d" % i, [128, 512], F32) for i in range(2)]
                pden = [K.ps("pden%d" % i, [128, 512], F32) for i in range(2)]
                Pb = [K.sb("Pb%d" % i, [128, 512], BF16) for i in range(4)]
                Pm = [K.sb("Pm%d" % i, [128, 512], BF16) for i in range(4)]
                rden = K.sb("rden", [128, 512], F32)
                jobs = list(C.lazy.get(s, []))
                njobs = len(jobs)
                if njobs:
                    lzb = [K.sb("lzb%d" % i, [128, 2816], BF16) for i in range(4)]
                lzi = 0

                def emit_job():
                    nonlocal lzi
                    dst_, src_, r0_, n_, c_ = jobs.pop(0)
                    b_ = lzb[lzi % 4]
                    lzi += 1
                    K.dma(b_[0:n_, 0:c_], src_[r0_:r0_ + n_, :], eng="pool")
                    K.dma(dst_.k(r0_)[r0_:r0_ + n_, :], b_[0:n_, 0:c_], eng="sp")
                LA = 3
                items = []
                for t in range(4):
                    for qc in range(4):
                        nkb = 4 * qc + 4
                        for kb in range(nkb):
                            for hh in range(2):
                                items.append((t, qc, kb, hh, nkb))
                n_it = len(items)
                for it_ in range(n_it + LA):
                    if it_ < n_it:
                        t, qc, kb, hh, nkb = items[it_]
                        hs = slice(hh * 64, (hh + 1) * 64)
                        delta = 4 * qc - kb
                        ps_ = pS[it_ % 4]
                        K.mm(ps_[:], KT[hs, t, kb * 128:(kb + 1) * 128], QT.k((t, qc))[hs, t, qc * 512:(qc + 1) * 512])
                        pb_ = Pb[it_ % 4]; pm_ = Pm[it_ % 4]
                        K.act(pb_[:], ps_[:], AF.Exp, scale=0.125)
                        K.tt(pm_[:], pb_[:], gm[:, 128 * (delta + 3):128 * (delta + 3) + 512], ALU.mult, eng=("dve" if (njobs or it_ % 3 != 2) else "pool"))
                        if jobs and it_ % 8 == 4:
                            emit_job()
                    j_ = it_ - LA
                    if j_ >= 0:
                        t, qc, kb, hh, nkb = items[j_]
                        hs = slice(hh * 64, (hh + 1) * 64)
                        grp = t * 4 + qc
                        num = pnum[grp % 2]; den = pden[grp % 2]
                        pm_ = Pm[j_ % 4]
                        hcol = (2 * t + hh) * 64
                        K.mm(num[hs, :], V[:, kb, hcol:hcol + 64], pm_[:], start=(kb == 0), stop=(kb == nkb - 1))
                        K.mm(den[hs, :], C.onesb[:, 0:64], pm_[:], start=(kb == 0), stop=(kb == nkb - 1))
                        if kb == nkb - 1 and hh == 1:
                            K.recip(rden[:], den[:])
                            K.tt(attT.k((t, qc))[:, t, qc * 512:(qc + 1) * 512], num[:], rden[:], ALU.mult)
                while jobs:
                    emit_job()
            if "B" in dbg and s == 0:
                with K.phase():
                    o = dbgout("attT", [128, 4, SEQ]); tf = K.sb("dbgfa", [128, 4, SEQ], F32)
                    K.copy(tf[:], attT[:]); K.dma(o[:], tf[:])
            if stage == "B":
                raise Stop()
            with K.phase():
                EX = K.sb("EX", [128, 2, 16, 257], F32)
                EXb = K.sb("EXb", [128, 2, 16, 257], BF16)
                MU = K.sb("MUc", [128, 3, 16], F32)
                K.dma(MU[:], C.dMU[:])
                wglu = K.sb("wglu", [128, 4, 512], BF16)
                K.dma(wglu[:], W["s5_w_glu"][:].re("(kc p) n -> p kc n", p=128))
                ptr = [K.ps("ptr%d" % i, [128, 1024], BF16) for i in range(2)]
                pe_ = [K.ps("pe%d" % i, [128, 512], F32) for i in range(4)]
                tr1 = K.sb("tr1", [128, 2, 16], F32)
                tr2 = K.sb("tr2", [128, 2, 16], F32)
                with K.phase():
                    Bs = K.sb("BsC", [128, 32, 128], BF16)
                    K.dma(Bs[:], C.dBs[:])
                    K.memset(EX[:, :, :, 0:1], 0.0)
                    for pr in range(16):
                        for ri in range(2):
                            pp = pe_[(pr * 2 + ri) % 4]
                            for h in range(2):
                                g = 2 * pr + h
                                K.mm(pp[h * 64:(h + 1) * 64, 0:256], Bs[:, g, ri * 64:(ri + 1) * 64], UT[:, g, :])
                            K.copy(EX[:, ri, pr, 1:257], pp[:, 0:256], eng=("act" if ri == 0 else "dve"))
                chk(20)
                MUP = K.sb("MUPc", [128, 3, 16, 16], F32)
                K.dma(MUP[:], C.dMUP[:])
                XV = EX[:, :, :, 1:257].re("p r q (b i) -> p r q b i", i=16)
                T1 = K.sb("T1", [128, 2, 16, 16], F32)
                T2 = K.sb("T2", [128, 2, 16, 16], F32)

                def bc4(ref, nb):
                    return Ref(ref.tile, ref.key, ref.ap.unsqueeze(1).unsqueeze(3).to_broadcast([128, 2, 16, nb]))

                def bc3(ref, nb):
                    return Ref(ref.tile, ref.key, ref.ap.unsqueeze(2).to_broadcast([128, 16, nb]))

                def cmul_add(dst, src, k, nb):
                    K.tt(T1[:, :, :, 0:nb], src, bc4(MUP[:, 0, k, :], nb), ALU.mult)
                    K.tt(T2[:, 0, :, 0:nb], src[:, 1], bc3(MUP[:, 2, k, :], nb), ALU.mult)
                    K.tt(T2[:, 1, :, 0:nb], src[:, 0], bc3(MUP[:, 1, k, :], nb), ALU.mult)
                    K.tt(T1[:, :, :, 0:nb], T1[:, :, :, 0:nb], T2[:, :, :, 0:nb], ALU.add)
                    K.tt(dst, dst, T1[:, :, :, 0:nb], ALU.add)

                for i in range(1, 16):
                    cmul_add(XV[:, :, :, :, i], XV[:, :, :, :, i - 1], 0, 16)
                for bb in range(1, 16):
                    cmul_add(XV[:, :, :, bb:bb + 1, 15], XV[:, :, :, bb - 1:bb, 15], 15, 1)
                for i in range(15):
                    cmul_add(XV[:, :, :, 1:16, i], XV[:, :, :, 0:15, 15], i, 15)
                chk(21)
                K.copy(EXb[:], EX[:])
                if "C1" in dbg and s == 0:
                    o = dbgout("EX", [128, 2, 16, 257])
                    K.dma(o[:], EX[:])
                with K.phase():
                    Tz = K.sb("TzC", [128, 32, 128], BF16)
                    CsRe = K.sb("CsReC", [128, 16, 128], BF16)
                    CsIm = K.sb("CsImC", [128, 16, 128], BF16)
                    K.dma(Tz[:], C.dTz[:]); K.dma(CsRe[:], C.dCsRe[:]); K.dma(CsIm[:], C.dCsIm[:])
                    Yg = [K.sb("Yg%d" % i, [128, 128], BF16) for i in range(2)]
                    YGb = K.sb("YGb", [128, 8, 512], BF16)
                    sg_ = [K.sb("sgC%d" % i, [128, 512], BF16) for i in range(2)]
                    if "C1" in dbg and s == 0:
                        oYA = dbgout("YA", [2, 128, 8, 512])
                        YAf = K.sb("YAf", [128, 8, 512], F32)
                    for blk in range(2):
                        cs_ = slice(blk * 128, (blk + 1) * 128)
                        for g4 in range(8):
                            pt = ptr[g4 % 2]
                            for gi in range(4):
                                g = g4 * 4 + gi
                                pr, h = g // 2, g % 2
                                hs = slice(h * 64, (h + 1) * 64)
                                py = pe_[g % 4]
                                K.mm(py[:, 0:128], Tz[:, g, :], UT[:, g, cs_], start=True, stop=False)
                                K.mm(py[:, 0:128], CsRe[hs, pr, :], EXb[hs, 0, pr, blk * 128:blk * 128 + 128], start=False, stop=False)
                                K.mm(py[:, 0:128], CsIm[hs, pr, :], EXb[hs, 1, pr, blk * 128:blk * 128 + 128], start=False, stop=True)
                                yg_ = Yg[g % 2]
                                K.copy(yg_[:], py[:, 0:128], eng="act")
                                K.tr(pt[:, gi * 128:(gi + 1) * 128], yg_[:], C.identb[:])
                            dstv = YGb[:, :, g4 * 64:(g4 + 1) * 64].re("p i (g c) -> p g i c", g=4)
                            srcv = pt[:, 0:512].re("p (g i c) -> p g i c", g=4, i=8)
                            if "C1" in dbg and s == 0:
                                K.copy(YAf[:, :, g4 * 64:(g4 + 1) * 64].re("p i (g c) -> p g i c", g=4), srcv)
                            K.act(dstv, srcv, AF.Gelu_apprx_tanh)
                        if "C1" in dbg and s == 0:
                            K.dma(oYA[blk], YAf[:])
                        for fc in range(4):
                            pt = ptr[fc % 2]
                            for i in range(8):
                                K.tr(pt[:, i * 128:(i + 1) * 128], YGb[:, i, fc * 128:(fc + 1) * 128], C.identb[:])
                            K.copy(ygT[:, fc, blk * 1024:(blk + 1) * 1024].re("p (c i) -> p i c", i=8), pt[:].re("p (i c) -> p i c", i=8),
                                   eng=("act" if fc % 2 == 0 else "dve"))
                    chk(22)
                    for mt in range(4):
                        for nq in range(4):
                            pg = pe_[(mt * 4 + nq) % 4]
                            for kc in range(4):
                                K.mm(pg[:], wglu[:, kc, mt * 128:(mt + 1) * 128], ygT[:, kc, nq * 512:(nq + 1) * 512], start=(kc == 0), stop=(kc == 3))
                            sg = sg_[(mt * 4 + nq) % 2]
                            K.act(sg[:], pg[:], AF.Sigmoid)
                            K.tt(ssmT[:, mt, nq * 512:(nq + 1) * 512], sg[:], ygT[:, mt, nq * 512:(nq + 1) * 512], ALU.mult)
            chk(23)
            if "C" in dbg and s == 0:
                with K.phase():
                    o = dbgout("ssmT", [128, 4, SEQ]); tf = K.sb("dbgfs", [128, 4, SEQ], F32)
                    K.copy(tf[:], ssmT[:]); K.dma(o[:], tf[:])
            if stage == "C":
                raise Stop()
            with K.phase():
                xts = [K.sb("xtD%d" % i, [128, 4, D], F32) for i in range(2)]
                C.hnT = K.sb("hnT", [128, 8, 512], BF16)
                C.actT = K.sb("actT", [128, NFT, 512], BF16)
                C.wgb = [K.sb("wgb%d" % i, [128, 8, 256], BF16) for i in range(3)]
                C.wub = [K.sb("wub%d" % i, [128, 8, 256], BF16) for i in range(3)]
                C.sgb = [K.sb("sgb%d" % i, [128, 512], F32) for i in range(2)]
                C.rotb = [K.sb("rotb%d" % i, [128, D], BF16) for i in range(6)]
                alloc_psum(K, C, "pD")
                wov = W["ev_w_out"][:].re("(kc p) n -> p kc n", p=128)
                K.dma(xts[0][:], I["x"][tok0:tok0 + 512, :].re("(s p) d -> p s d", p=128), eng="pool")
                for tl in range(4):
                    t0 = tok0 + tl * 512
                    xt = xts[tl % 2]
                    if tl + 1 < 4:
                        K.dma(xts[(tl + 1) % 2][:], I["x"][t0 + 512:t0 + 1024, :].re("(s p) d -> p s d", p=128), eng="pool")
                    proj_tokmajor(K, C, lambda kc, sub: (attT if kc < 4 else ssmT)[:, kc % 4, tl * 512 + sub * 128:tl * 512 + (sub + 1) * 128],
                                  8, wov, C.g_mix_post[0], xt, "mp")
                    if "D0" in dbg and s == 0 and tl == 0:
                        o = dbgout("xa0", [128, 4, D])
                        K.dma(o[:], xt[:])
                    ffn_tile(K, C, xt, 0, W)
                    K.dma(x1d[t0:t0 + 512, :].re("(s p) d -> p s d", p=128), xt[:], eng="pool")
                    if "D0" in dbg and s == 0 and tl == 0:
                        o = dbgout("x1", [128, 4, D])
                        K.dma(o[:], xt[:])
                    if stage == "D0":
                        raise Stop()
    build_layer1(K, C, I, W, out, x1d, stage, dbg, dbgout)


def build_layer1(K, C, I, W, out, x1d, stage, dbg, dbgout):
    ntl = 16
    if stage == "L1a":
        ntl = 2
    with K.phase():
        C.g_mix_pre[1] = load_bc(K, "g_mp1", I["norm_mix_pre"][1:2, :], D)
        C.g_mix_post[1] = load_bc(K, "g_mo1", I["norm_mix_post"][1:2, :], D)
        C.g_ffn_pre[1] = load_bc(K, "g_fp1", I["norm_ffn_pre"][1:2, :], D)
        C.g_ffn_post[1] = load_bc(K, "g_fo1", I["norm_ffn_post"][1:2, :], D)
        C.scr_j = K.sb("scr_j1", [128, D], BF16)
        C.scr_f = K.sb("scr_f1", [128, D], F32)
        C.scr_g = K.sb("scr_g1", [128, D], F32)
        C.hn = K.sb("hn1", [128, 4, D], BF16)
        xts = [K.sb("xt1_%d" % i, [128, 4, D], F32) for i in range(2)]
        C.hnT = K.sb("hnT1", [128, 8, 512], BF16)
        C.actT = K.sb("actT1", [128, NFT, 512], BF16)
        C.wgb = [K.sb("wgb1_%d" % i, [128, 8, 256], BF16) for i in range(2)]
        C.wub = [K.sb("wub1_%d" % i, [128, 8, 256], BF16) for i in range(2)]
        C.sgb = [K.sb("sgb1_%d" % i, [128, 512], F32) for i in range(2)]
        C.rotb = [K.sb("rotb1_%d" % i, [128, D], BF16) for i in range(6)]
        alloc_psum(K, C, "pL")
        xbuf = K.sb("xbuf", [128, 8, 515], F32)
        xc = K.sb("xc", [128, 8, 512], F32)
        yT = C.hnT
        xcb = C.actT[:, 0:8, :]
        gz = C.actT[:, 8:16, :]
        GRb = K.sb("GRb", [128, 4, 512], BF16); GIb = K.sb("GIb", [128, 4, 512], BF16)
        A4 = K.sb("A4", [128, 4, 512], F32); M4 = K.sb("M4", [128, 4, 512], F32)
        hbuf = [K.sb("h_%d" % i, [128, 512], F32) for i in range(2)]
        hc = K.sb("hc", [128, 8], F32)
        cw = K.sb("cw", [128, 8, 4], F32)
        cb = K.sb("cb", [128, 8], F32); br = K.sb("br", [128, 8], F32); bi = K.sb("bi", [128, 8], F32)
        cch = K.sb("cch", [128, 8], F32)
        Wr = K.sb("Wr", [128, 8, 256], BF16); Wi = K.sb("Wi", [128, 8, 256], BF16)
        for j in range(4):
            K.dma(cw[:, :, j], I["rg_conv_w"][j:j + 1, :].re("o (ct p) -> p (o ct)", p=128), allow_slow_non_contiguous=True)
        for t_, nm in ((cb, "rg_conv_b"), (br, "rg_b_r"), (bi, "rg_b_i"), (cch, "rg_lam")):
            K.dma(t_[:], I[nm][0:1, :].re("o (ct p) -> p (o ct)", p=128), allow_slow_non_contiguous=True)
        K.act(cch[:], cch[:], AF.Exp, scale=-1.0)
        K.act(cch[:], cch[:], AF.Ln, bias=C.onec[:, 0:1])
        K.ts(cch[:], cch[:], -8.0, ALU.mult)
        cch2 = K.sb("cch2", [128, 8], F32)
        K.ts(cch2[:], cch[:], 2.0, ALU.mult)
        K.dma(Wr[:], W["rg_w_r"][:].re("(q p) n -> p q n", p=128))
        K.dma(Wi[:], W["rg_w_i"][:].re("(q p) n -> p q n", p=128))
        wiv = W["od_w_in"][:].re("(kc p) n -> p kc n", p=128)
        wov = W["od_w_out"][:].re("(kc p) n -> p kc n", p=128)
        K.dma(xts[0][:], x1d[0:512, :].re("(s p) d -> p s d", p=128), eng="pool")
        for tl in range(ntl):
            t0 = tl * 512
            first = (tl % 4 == 0)
            xt = xts[tl % 2]
            if tl + 1 < ntl:
                K.dma(xts[(tl + 1) % 2][:], x1d[t0 + 512:t0 + 1024, :].re("(s p) d -> p s d", p=128), eng="pool")
            rms_pre(K, C, xt[:], 4, C.g_mix_pre[1], C.hn[:], "l")
            transpose_to(K, C, C.hn[:], 4, C.hnT, 0)
            if first:
                K.memset(xbuf[:, :, 0:3], 0.0)
            else:
                K.copy(xbuf[:, :, 0:3], xbuf[:, :, 512:515])
            for ch in range(8):
                wb = C.wgb[ch % 2] if ch % 4 < 2 else C.wub[ch % 2]
                K.dma(wb[:], wiv[:, :, ch * 256:(ch + 1) * 256], eng="sp")
                for mi in range(2):
                    mt = ch * 2 + mi
                    pz = C.psm[C.pmi % 6]; C.pmi += 1
                    for kc in range(8):
                        K.mm(pz[:], wb[:, kc, mi * 128:(mi + 1) * 128], C.hnT.k(kc)[:, kc, :], start=(kc == 0), stop=(kc == 7))
                    if mt < 8:
                        ct = mt
                        K.copy(xbuf.k(ct)[:, ct, 3:515], pz[:], eng="act")
                        K.ts(xc.k(ct)[:, ct, :], xbuf.k(ct)[:, ct, 0:512], cw[:, ct, 0:1], ALU.mult, cb[:, ct:ct + 1], ALU.add)
                        for j in range(1, 4):
                            K.stt(xc.k(ct)[:, ct, :], xbuf.k(ct)[:, ct, j:j + 512], cw[:, ct, j:j + 1], xc.k(ct)[:, ct, :], ALU.mult, ALU.add)
                        K.copy(C.actT.k(("x", ct))[:, ct, :], xc.k(ct)[:, ct, :], eng="pool")
                    else:
                        K.act(C.actT.k(("g", mt - 8))[:, mt, :], pz[:], AF.Gelu_apprx_tanh)
            for half in range(2):
                cts = list(range(half * 4, half * 4 + 4))
                for j, ct in enumerate(cts):
                    hq = (ct // 2) * 2
                    cs_ = slice((ct % 2) * 128, (ct % 2 + 1) * 128)
                    pr_ = C.psm[C.pmi % 6]; pi_ = C.psm[(C.pmi + 1) % 6]; C.pmi += 2
                    for kc in range(2):
                        K.mm(pr_[:], Wr[:, hq + kc, cs_], C.actT.k(("x", hq + kc))[:, hq + kc, :], start=(kc == 0), stop=(kc == 1))
                    for kc in range(2):
                        K.mm(pi_[:], Wi[:, hq + kc, cs_], C.actT.k(("x", hq + kc))[:, hq + kc, :], start=(kc == 0), stop=(kc == 1))
                    K.act(GRb.k(j)[:, j, :], pr_[:], AF.Sigmoid, bias=br[:, ct:ct + 1])
                    K.act(GIb.k(j)[:, j, :], pi_[:], AF.Sigmoid, bias=bi[:, ct:ct + 1])
                    K.tt(xc.k(ct)[:, ct, :], xc.k(ct)[:, ct, :], GIb.k(j)[:, j, :], ALU.mult, eng="pool")
                for j, ct in enumerate(cts):
                    K.act(A4.k(j)[:, j, :], GRb.k(j)[:, j, :], AF.Exp, scale=cch[:, ct:ct + 1])
                    K.act(M4.k(j)[:, j, :], GRb.k(j)[:, j, :], AF.Exp, scale=cch2[:, ct:ct + 1])
                for j, ct in enumerate(cts):
                    K.act(M4.k(j)[:, j, :], M4.k(j)[:, j, :], AF.Sqrt, scale=-1.0, bias=C.onec[:, 0:1])
                for j, ct in enumerate(cts):
                    K.tt(M4.k(j)[:, j, :], M4.k(j)[:, j, :], xc.k(ct)[:, ct, :], ALU.mult)
                    h_ = hbuf[ct % 2]
                    if first:
                        K.scan(h_[:], A4.k(j)[:, j, :], M4.k(j)[:, j, :], 0.0)
                    else:
                        K.scan(h_[:], A4.k(j)[:, j, :], M4.k(j)[:, j, :], hc.k(ct)[:, ct:ct + 1])
                    K.copy(hc.k(ct)[:, ct:ct + 1], h_[:, 511:512], eng="pool")
                    K.tt(yT.k(ct)[:, ct, :], h_[:], C.actT.k(("g", ct))[:, 8 + ct, :], ALU.mult)
                    if "L1" in dbg and tl < 2:
                        if ct == 0 and tl == 0:
                            C.oh = dbgout("h1", [2, 8, 128, 512])
                        K.dma(C.oh[tl, ct], h_[:])
            proj_tokmajor(K, C, lambda kc, sub: yT[:, kc, sub * 128:(sub + 1) * 128], 8, wov, C.g_mix_post[1], xt, "mq")
            if "L1" in dbg and tl < 2:
                if tl == 0:
                    C.oxa = dbgout("xa1", [2, 128, 4, D])
                K.dma(C.oxa[tl], xt[:])
            ffn_tile(K, C, xt, 1, W)
            K.dma(out[t0:t0 + 512, :].re("(s p) d -> p s d", p=128), xt[:], eng="pool")


def kernel(**inputs):
    nc, _ = build("full")
    maps = make_in_maps(inputs)
    res = run_bass_kernel_spmd(nc, maps, core_ids=list(range(NCORES)))
    outs = [np.asarray(r["out"], dtype=np.float32).reshape(NSEQ, SEQ, D) for r in res.results]
    return np.concatenate(outs, axis=0)
```

```python
import numpy as np
import ml_dtypes
import concourse.bass as bass
import concourse.mybir as mybir
from concourse.bass_utils import run_bass_kernel_spmd

F32 = mybir.dt.float32
BF16 = mybir.dt.bfloat16
I32 = mybir.dt.int32
AF = mybir.ActivationFunctionType
ALU = mybir.AluOpType
AX = mybir.AxisListType

SEM_LIMIT = 24000


class Ref:
    __slots__ = ("tile", "key", "ap")

    def __init__(self, tile, key, ap):
        self.tile = tile
        self.key = key
        self.ap = ap

    def __getitem__(self, idx):
        return Ref(self.tile, self.key, self.ap[idx])

    def re(self, pat, **kw):
        return Ref(self.tile, self.key, self.ap.rearrange(pat, **kw))


class _Keyed:
    def __init__(self, tile, key):
        self.tile = tile
        self.key = key

    def __getitem__(self, idx):
        return Ref(self.tile, self.key, self.tile.h[idx])


class Tile:
    def __init__(self, h, name):
        self.h = h
        self.name = name
        self.reg = {}

    def __getitem__(self, idx):
        return Ref(self, "*", self.h[idx])

    def k(self, key):
        return _Keyed(self, key)


def _merge(dst, src):
    for s, v in src.items():
        if dst.get(s, 0) < v:
            dst[s] = v


class Kern:
    ENG = ("pe", "act", "dve", "pool", "sp")

    def __init__(self, nc, sync_same=("act", "dve", "pool")):
        self.nc = nc
        self.stack = []
        self.prog = {e: [] for e in self.ENG}
        self.free_sems = []
        self.nsem = 0
        self.csem = {}
        self.ccnt = {}
        self.waited = {e: {} for e in self.ENG}
        self.dsems = {e: [] for e in self.ENG}
        self.drr = {e: 0 for e in self.ENG}
        self.dcnt = {}
        self.sync_same = set(sync_same)
        self.semname = {}
        for e in self.ENG:
            self._new_csem(e)
        self.ndma_sems = {"sp": 6, "act": 3, "pool": 4, "dve": 0, "pe": 0}
        for e in self.ENG:
            for _ in range(self.ndma_sems[e]):
                self.dsems[e].append(self._alloc_sem())

    def _alloc_sem(self):
        cm = self.nc.semaphore("s%d" % self.nsem)
        self.nsem += 1
        h = cm.__enter__()
        self.stack.append(cm)
        self.dcnt[h] = 0
        return h

    def _new_csem(self, e):
        self.csem[e] = self._alloc_sem()
        self.ccnt[e] = 0

    def sb(self, name, shape, dt):
        self.uid = getattr(self, "uid", 0) + 1
        name = "sb%d_%s" % (self.uid, name)
        cm = self.nc.sbuf_tensor(name, list(shape), dt)
        h = cm.__enter__()
        self.stack.append(cm)
        return Tile(h, name)

    def ps(self, name, shape, dt=F32):
        self.uid = getattr(self, "uid", 0) + 1
        name = "ps%d_%s" % (self.uid, name)
        nbytes = int(np.prod(shape[1:])) * (4 if dt == F32 else 2)
        assert nbytes == 2048, "PSUM tiles must be exactly one bank"
        cm = self.nc.psum_tensor(name, list(shape), dt)
        h = cm.__enter__()
        self.stack.append(cm)
        t = Tile(h, name)
        t.psum = True
        return t

    def dram(self, name, shape, dt, kind="Internal"):
        t = self.nc.dram_tensor(name, list(shape), dt, kind=kind)
        return Tile(t.ap(), name)

    def _conf(self, ref):
        t = ref.tile
        if ref.key == "*":
            return list(t.reg.values())
        out = []
        if "*" in t.reg:
            out.append(t.reg["*"])
        if ref.key in t.reg:
            out.append(t.reg[ref.key])
        return out

    def emit(self, eng, fn, reads=(), writes=(), dma=False):
        pr = [r for r in reads if getattr(r.tile, "psum", False)]
        if pr:
            reads = [r for r in reads if not getattr(r.tile, "psum", False)]
            writes = list(writes) + pr
        need = {}
        for r in reads:
            for w, _ in self._conf(r):
                _merge(need, w)
        for wr in writes:
            for w, rd in self._conf(wr):
                _merge(need, w)
                _merge(need, rd)
        waits = []
        wd = self.waited[eng]
        own = self.csem[eng]
        for s, v in need.items():
            if wd.get(s, 0) >= v:
                continue
            if (not dma) and s is own and eng not in self.sync_same:
                continue
            wd[s] = v
            waits.append((s, v))
        if dma:
            lst = self.dsems[eng]
            i = self.drr[eng] % len(lst)
            self.drr[eng] += 1
            s = lst[i]
            if self.dcnt[s] + 16 > SEM_LIMIT:
                s = self._alloc_sem()
                lst[i] = s
            self.dcnt[s] += 16
            ev = (s, self.dcnt[s])
            inc = 16
        else:
            if self.ccnt[eng] + 1 > SEM_LIMIT:
                self._new_csem(eng)
            self.ccnt[eng] += 1
            ev = (self.csem[eng], self.ccnt[eng])
            inc = 1
        self.prog[eng].append((waits, fn, ev[0], inc))
        evd = {ev[0]: ev[1]}
        for r in reads:
            reg = r.tile.reg.setdefault(r.key, [{}, {}])
            _merge(reg[1], evd)
        for wr in writes:
            if wr.key == "*":
                wr.tile.reg = {"*": [dict(evd), {}]}
            else:
                wr.tile.reg[wr.key] = [dict(evd), {}]
        return ev

    def wait_all(self, eng, refs):
        need = {}
        for r in refs:
            for w, rd in self._conf(r):
                _merge(need, w)
        waits = [(s, v) for s, v in need.items()]
        self.prog[eng].append((waits, None, None, 0))

    def dma(self, out, in_, eng="sp", **kw):
        return self.emit(eng, lambda e: e.dma_start(out=out.ap, in_=in_.ap, **kw),
                         reads=[in_], writes=[out], dma=True)

    def mm(self, out, lhsT, rhs, start=True, stop=True, **kw):
        return self.emit("pe", lambda e: e.matmul(out.ap, lhsT.ap, rhs.ap, start=start, stop=stop, **kw),
                         reads=[lhsT, rhs], writes=[out])

    def tr(self, out, in_, ident):
        return self.emit("pe", lambda e: e.transpose(out.ap, in_.ap, ident.ap),
                         reads=[in_, ident], writes=[out])

    def act(self, out, in_, func, bias=None, scale=None, accum=None, eng="act", extra_reads=()):
        kw = {}
        reads = [in_] + list(extra_reads)
        writes = [out]
        if bias is not None:
            if isinstance(bias, Ref):
                kw["bias"] = bias.ap
                reads.append(bias)
            else:
                kw["bias"] = bias
        if scale is not None:
            if isinstance(scale, Ref):
                kw["scale"] = scale.ap
                reads.append(scale)
            else:
                kw["scale"] = scale
        if accum is not None:
            kw["accum_out"] = accum.ap
            writes.append(accum)
        return self.emit(eng, lambda e: e.activation(out.ap, in_.ap, func, **kw), reads=reads, writes=writes)

    def tt(self, out, a, b, op, eng="dve"):
        return self.emit(eng, lambda e: e.tensor_tensor(out.ap, a.ap, b.ap, op), reads=[a, b], writes=[out])

    def ts(self, out, a, s1, op0, s2=None, op1=None, accum=None, eng="dve"):
        reads = [a]
        writes = [out]
        v1 = s1
        v2 = s2
        if isinstance(s1, Ref):
            reads.append(s1)
            v1 = s1.ap
        if isinstance(s2, Ref):
            reads.append(s2)
            v2 = s2.ap
        kw = {}
        if op1 is not None:
            kw["op1"] = op1
        if accum is not None:
            kw["accum_out"] = accum.ap
            writes.append(accum)
        return self.emit(eng, lambda e: e.tensor_scalar(out.ap, a.ap, v1, v2, op0, **kw), reads=reads, writes=writes)

    def stt(self, out, a, s, b, op0, op1, eng="dve"):
        reads = [a, b]
        v = s
        if isinstance(s, Ref):
            reads.append(s)
            v = s.ap
        return self.emit(eng, lambda e: e.scalar_tensor_tensor(out.ap, a.ap, v, b.ap, op0, op1), reads=reads, writes=[out])

    def copy(self, out, in_, eng="dve"):
        if eng == "act":
            return self.emit(eng, lambda e: e.copy(out.ap, in_.ap), reads=[in_], writes=[out])
        return self.emit(eng, lambda e: e.tensor_copy(out.ap, in_.ap), reads=[in_], writes=[out])

    def memset(self, out, val, eng="dve"):
        return self.emit(eng, lambda e: e.memset(out.ap, val), reads=[], writes=[out])

    def scan(self, out, d0, d1, init, op0=ALU.mult, op1=ALU.add, eng="dve"):
        reads = [d0, d1]
        v = init
        if isinstance(init, Ref):
            reads.append(init)
            v = init.ap
        return self.emit(eng, lambda e: e.tensor_tensor_scan(out.ap, d0.ap, d1.ap, v, op0, op1), reads=reads, writes=[out])

    def recip(self, out, in_):
        return self.emit("dve", lambda e: e.reciprocal(out.ap, in_.ap), reads=[in_], writes=[out])

    def finish(self):
        nc = self.nc
        prog = self.prog
        with nc.Block() as block:
            def run(e, name):
                for waits, fn, sem, inc in prog[name]:
                    for s, v in waits:
                        e.wait_ge(s, v)
                    if fn is not None:
                        fn(e).then_inc(sem, inc)

            @block.sync
            def _(e):
                run(e, "sp")

            @block.scalar
            def _(e):
                run(e, "act")

            @block.vector
            def _(e):
                run(e, "dve")

            @block.gpsimd
            def _(e):
                run(e, "pool")

            @block.tensor
            def _(e):
                run(e, "pe")
        while self.stack:
            self.stack.pop().__exit__(None, None, None)


def _ref_bc(self, shape):
    return Ref(self.tile, self.key, self.ap.to_broadcast(list(shape)))


def _ref_pb(self, n):
    return Ref(self.tile, self.key, self.ap.partition_broadcast(n))


Ref.bc = _ref_bc
Ref.pb = _ref_pb


class _Phase:
    def __init__(self, K):
        self.K = K

    def __enter__(self):
        self.h = len(self.K.stack)
        return self

    def __exit__(self, *a):
        K = self.K
        K.barrier()
        K.flush()
        while len(K.stack) > self.h:
            K.stack.pop().__exit__(None, None, None)
        return False


def _phase(self):
    return _Phase(self)


def _barrier(self):
    need = {}
    for e in self.ENG:
        if self.ccnt[e] > 0:
            need[self.csem[e]] = self.ccnt[e]
        for s in self.dsems[e]:
            if self.dcnt[s] > 0:
                need[s] = self.dcnt[s]
    for e in self.ENG:
        wd = self.waited[e]
        waits = []
        for s, v in need.items():
            if wd.get(s, 0) >= v:
                continue
            wd[s] = v
            waits.append((s, v))
        if waits:
            self.prog[e].append((waits, None, None, 0))


def _flush(self):
    nc = self.nc
    prog = self.prog
    self.prog = {e: [] for e in self.ENG}
    if not any(prog.values()):
        return
    with nc.Block() as block:
        def run(e, name):
            for waits, fn, sem, inc in prog[name]:
                for s, v in waits:
                    e.wait_ge(s, v)
                if fn is not None:
                    fn(e).then_inc(sem, inc)

        @block.sync
        def _(e):
            run(e, "sp")

        @block.scalar
        def _(e):
            run(e, "act")

        @block.vector
        def _(e):
            run(e, "dve")

        @block.gpsimd
        def _(e):
            run(e, "pool")

        @block.tensor
        def _(e):
            run(e, "pe")


def _finish(self):
    self.barrier()
    self.flush()
    while self.stack:
        self.stack.pop().__exit__(None, None, None)


Kern.phase = _phase
Kern.barrier = _barrier
Kern.flush = _flush
Kern.finish = _finish


import math

NCORES = 8
D = 1024
SEQ = 2048
NSEQ = 4
TOK = NSEQ * SEQ
FF = 2816
NFT = FF // 128
EPS = 1e-6
PI = math.pi
C1 = 6.28125
C2 = 2 * math.pi - 6.28125


def host_consts():
    c = {}
    c["ident"] = np.eye(128, dtype=np.float32)
    invf = np.zeros((128, 1), np.float32)
    sgn = np.zeros((128, 1), np.float32)
    for p in range(128):
        i = p % 64
        if i < 16:
            invf[p, 0] = np.float32(500000.0) ** np.float32(-((i % 8) * 2.0 / 16.0))
            sgn[p, 0] = -1.0 if i < 8 else 1.0
    c["rope_cols"] = np.concatenate([invf, sgn], axis=1)
    x = np.arange(2816)[None, :] - np.arange(128)[:, None] - 384
    m = ((x >= 0) & (x <= 128)).astype(np.float32) + ((x >= 0) & (x % 4 == 0) & (x <= 512)) + ((x >= 0) & (x % 16 == 0) & (x <= 2048))
    c["gmask"] = m.astype(np.float32)
    hm = np.zeros((128, 3), np.float32)
    hm[:64, 0] = 1
    hm[64:, 1] = 1
    hm[:64, 2] = 1
    hm[64:, 2] = -1
    c["hmask"] = hm
    return c


def swap_perm():
    perm = np.arange(512)
    for h in range(8):
        for i in range(16):
            perm[h * 64 + i] = h * 64 + (i + 8 if i < 8 else i - 8)
    return perm


class Ctx:
    pass


def load_bc(K, name, src_row, n, eng="sp"):
    t = K.sb(name, [128, n], F32)
    K.dma(t[:], src_row.pb(128), eng=eng)
    return t


def rms_pre(K, C, xt, nsub, gain, out_bf, tagp):
    for s in range(nsub):
        ss = C.small.k(tagp + "ss%d" % s)[:, C.si:C.si + 1]
        sq = C.scr_j[:, 0:D]
        K.act(sq, xt[:, s, :], AF.Square, accum=ss)
        rs = C.small.k(tagp + "rs%d" % s)[:, C.si + 1:C.si + 2]
        K.act(rs, ss, AF.Sqrt, scale=1.0 / D, bias=C.epsc[:, 0:1])
        K.recip(rs, rs)
        K.stt(out_bf[:, s, :], xt[:, s, :], rs, gain[:], ALU.mult, ALU.mult)
        C.si = (C.si + 2) % 60


def transpose_to(K, C, src_bf, nsub, dstT, col0):
    for kc in range(D // 128):
        pt = C.pst[C.pti % len(C.pst)]
        C.pti += 1
        for s in range(nsub):
            K.tr(pt[:, s * 128:(s + 1) * 128], src_bf[:, s, kc * 128:(kc + 1) * 128], C.identb[:])
        eng = "act" if kc % 2 == 0 else "dve"
        K.copy(dstT.k(kc)[:, kc, col0:col0 + nsub * 128], pt[:, 0:nsub * 128], eng=eng)


def post_norm_add(K, C, ps_pair, gain, xt_sub, tagp, sub=0):
    ssa = C.small.k(tagp + "a")[:, C.si:C.si + 1]
    ssb = C.small.k(tagp + "b")[:, C.si + 1:C.si + 2]
    rs = C.small.k(tagp + "c")[:, C.si + 2:C.si + 3]
    C.si = (C.si + 3) % 60
    C.pn = getattr(C, "pn", 0) + 1
    scr = C.scr_f if C.pn % 2 == 0 else C.scr_g
    K.act(C.scr_j[:, 0:512], ps_pair[0], AF.Square, accum=ssa)
    K.act(C.scr_j[:, 512:1024], ps_pair[1], AF.Square, accum=ssb)
    K.tt(rs, ssa, ssb, ALU.add)
    K.act(rs, rs, AF.Sqrt, scale=1.0 / D, bias=C.epsc[:, 0:1])
    K.recip(rs, rs)
    for h in range(2):
        tmp = scr.k("ab"[h])[:, h * 512:(h + 1) * 512]
        K.stt(tmp, ps_pair[h], rs, gain[:, h * 512:(h + 1) * 512], ALU.mult, ALU.mult)
        xa = Ref(xt_sub.tile, ("xt", sub, h), xt_sub.ap[:, h * 512:(h + 1) * 512])
        K.tt(xa, xa, tmp, ALU.add, eng=("pool" if h == 0 else "dve"))


def rot(C):
    b = C.rotb[C.roti % len(C.rotb)]
    C.roti += 1
    return b


class View:
    def __init__(self, tile, ap):
        self.tile = tile
        self.ap = ap

    def __getitem__(self, idx):
        return Ref(self.tile, "*", self.ap[idx])


def alloc_psum(K, C, tag):
    C.pall = [K.ps("%s%d" % (tag, i), [128, 512], F32) for i in range(8)]
    C.psm = C.pall[0:6]
    C.pst = [View(t, t.h[:].bitcast(BF16)) for t in C.pall[6:8]]


def proj_tokmajor(K, C, lhs_fn, nk, wview, gain, xt, tagp):
    for ps_ in range(2):
        acc = C.pall[ps_ * 4:ps_ * 4 + 4]
        for k in range(nk):
            wb = rot(C)
            K.dma(wb[:], wview[:, k, :], eng="sp")
            for si in range(2):
                sub = ps_ * 2 + si
                for h in range(2):
                    K.mm(acc[si * 2 + h][:], lhs_fn(k, sub), wb[:, h * 512:(h + 1) * 512], start=(k == 0), stop=(k == nk - 1))
        for si in range(2):
            sub = ps_ * 2 + si
            post_norm_add(K, C, (acc[si * 2][:], acc[si * 2 + 1][:]), gain, xt[:, sub, :], tagp, sub)


def ffn_tile(K, C, xt, L, W):
    hn = C.hn
    rms_pre(K, C, xt[:], 4, C.g_ffn_pre[L], hn[:], "f")
    transpose_to(K, C, hn[:], 4, C.hnT, 0)
    NCH = FF // 256
    gv = W["ffn_g%d" % L][:].re("(kc p) n -> p kc n", p=128)
    uv = W["ffn_u%d" % L][:].re("(kc p) n -> p kc n", p=128)
    for ch in range(NCH):
        wg = C.wgb[ch % len(C.wgb)]
        wu = C.wub[ch % len(C.wub)]
        K.dma(wg[:], gv[:, :, ch * 256:(ch + 1) * 256], eng="sp")
        K.dma(wu[:], uv[:, :, ch * 256:(ch + 1) * 256], eng="sp")
        for mi in range(2):
            m = ch * 2 + mi
            pg = C.psm[C.pmi % len(C.psm)]
            pu = C.psm[(C.pmi + 1) % len(C.psm)]
            C.pmi += 2
            for kc in range(8):
                K.mm(pg[:], wg[:, kc, mi * 128:(mi + 1) * 128], C.hnT.k(kc)[:, kc, :], start=(kc == 0), stop=(kc == 7))
            for kc in range(8):
                K.mm(pu[:], wu[:, kc, mi * 128:(mi + 1) * 128], C.hnT.k(kc)[:, kc, :], start=(kc == 0), stop=(kc == 7))
            sg = C.sgb[m % 2]
            K.act(sg[:], pg[:], AF.Silu)
            K.tt(C.actT[:, m, :], sg[:], pu[:], ALU.mult)
    wdv = W["ffn_d%d" % L][:].re("(m p) n -> p m n", p=128)
    proj_tokmajor(K, C, lambda m, sub: C.actT[:, m, sub * 128:(sub + 1) * 128], NFT, wdv, C.g_ffn_post[L], xt, "fp")


def bc_last(ref, n):
    sh = list(ref.ap.shape)
    return Ref(ref.tile, ref.key, ref.ap.unsqueeze(len(sh)).to_broadcast(sh + [n]))


def bc_mid(ref, n):
    sh = list(ref.ap.shape)
    return Ref(ref.tile, ref.key, ref.ap.unsqueeze(1).to_broadcast([sh[0], n] + sh[1:]))


def sincos(K, C, ang, shape, sn, cs, tg):
    if isinstance(tg, str):
        t = K.sb(tg + "_t", shape, F32)
        ti = K.sb(tg + "_ti", shape, I32)
        kf = K.sb(tg + "_kf", shape, F32)
        r = K.sb(tg + "_r", shape, F32)
    else:
        t, ti, kf, r = tg
    for shift, dst in ((0.0, sn), (PI / 2, cs)):
        K.ts(t[:], ang, 1.0 / (2 * PI), ALU.mult, shift / (2 * PI), ALU.add)
        K.copy(ti[:], t[:])
        K.copy(kf[:], ti[:])
        K.stt(r[:], kf[:], -C1, ang, ALU.mult, ALU.add)
        K.stt(r[:], kf[:], -C2, r[:], ALU.mult, ALU.add)
        K.ts(r[:], r[:], shift, ALU.add, PI, ALU.min)
        K.ts(r[:], r[:], -PI, ALU.max)
        K.act(dst, r[:], AF.Sin)


def cast_weight(K, C, dst, src, rows, cols):
    r0 = 0
    while r0 < rows:
        n = min(128, rows - r0)
        b = C.castb[C.casti % len(C.castb)]
        C.casti += 1
        K.dma(b[0:n, 0:cols], src[r0:r0 + n, :], eng="pool")
        K.dma(dst[r0:r0 + n, :], b[0:n, 0:cols], eng="sp")
        r0 += n


class Stop(Exception):
    pass


CUT = [None]


def chk(n):
    if CUT[0] == n:
        raise Stop()


def build(stage="full", dbg=()):
    try:
        return _build(stage, dbg)
    except Stop:
        K = LASTK[0]
        K.finish()
        return K.nc, DBGG[0]


LASTK = [None]
DBGG = [None]


def _build(stage="full", dbg=()):
    nc = bass.Bass("TRN2", target_bir_lowering=False)
    K = Kern(nc)
    LASTK[0] = K
    C = Ctx()
    C.si = 0
    C.pti = 0
    C.pmi = 0
    C.casti = 0
    C.roti = 0
    I = {}

    def inp(name, shape, dt=F32):
        I[name] = K.dram(name, shape, dt, kind="ExternalInput")
        return I[name]

    inp("x", [TOK, D])
    inp("pos", [1, TOK], I32)
    for nm in ("norm_mix_pre", "norm_mix_post", "norm_ffn_pre", "norm_ffn_post"):
        inp(nm, [2, D])
    inp("w_in0", [D, 3072])
    inp("ev_w_out", [D, D])
    inp("s5_a_re", [32, 64]); inp("s5_a_im", [32, 64]); inp("s5_log_dt", [1, 32])
    inp("s5_b_re", [32, 64, 16]); inp("s5_b_im", [32, 64, 16])
    inp("s5_c_re", [32, 16, 64]); inp("s5_c_im", [32, 16, 64]); inp("s5_d", [32, 16])
    inp("s5_w_glu", [512, 512])
    inp("od_w_in", [D, 2048]); inp("od_w_out", [D, D])
    inp("rg_conv_w", [4, D]); inp("rg_conv_b", [1, D])
    inp("rg_w_r", [1024, 256]); inp("rg_b_r", [1, D]); inp("rg_w_i", [1024, 256]); inp("rg_b_i", [1, D]); inp("rg_lam", [1, D])
    for L in range(2):
        inp("ffn_g%d" % L, [D, FF]); inp("ffn_u%d" % L, [D, FF]); inp("ffn_d%d" % L, [FF, D])
    inp("ident", [128, 128]); inp("rope_cols", [128, 2]); inp("gmask", [128, 2816]); inp("hmask", [128, 3])
    out = K.dram("out", [TOK, D], F32, kind="ExternalOutput")
    DBG = {}
    DBGG[0] = DBG

    def dbgout(name, shape, dt=F32):
        DBG[name] = K.dram("dbg_" + name, shape, dt, kind="ExternalOutput")
        return DBG[name]

    W = {}
    for nm, r, c in (("w_in0", D, 3072), ("ev_w_out", D, D), ("s5_w_glu", 512, 512), ("od_w_in", D, 2048), ("od_w_out", D, D),
                     ("rg_w_r", 1024, 256), ("rg_w_i", 1024, 256),
                     ("ffn_g0", D, FF), ("ffn_u0", D, FF), ("ffn_d0", FF, D), ("ffn_g1", D, FF), ("ffn_u1", D, FF), ("ffn_d1", FF, D)):
        if stage != "S5prep":
            W[nm] = K.dram("wb_" + nm, [r, c], BF16)
    x1d = K.dram("x1d", [TOK, D], F32) if stage != "S5prep" else None

    identf = K.sb("identf", [128, 128], F32)
    C.identb = K.sb("identb", [128, 128], BF16)
    C.epsc = K.sb("epsc", [128, 1], F32)
    C.onec = K.sb("onec", [128, 1], F32)
    C.small = K.sb("small", [128, 64], F32)
    onesb = K.sb("onesb", [128, 64], BF16)
    gm = K.sb("gm", [128, 2816], BF16)
    ropec = K.sb("ropec", [128, 2], F32)
    hmask = K.sb("hmask", [128, 3], F32)
    C.dTz = K.dram("dTz", [128, 32, 128], BF16)
    C.dBs = K.dram("dBs", [128, 32, 128], BF16)
    C.dCsRe = K.dram("dCsRe", [128, 16, 128], BF16)
    C.dCsIm = K.dram("dCsIm", [128, 16, 128], BF16)
    C.dMU = K.dram("dMU", [128, 3, 16], F32)
    C.dMUP = K.dram("dMUP", [128, 3, 16, 16], F32)
    K.dma(identf[:], I["ident"][:])
    junk = K.sb("junk", [1, 64], F32)
    junki = K.sb("junki", [1, 4], I32)
    for nm_, t_ in I.items():
        flat = t_[:]
        while len(flat.ap.shape) > 1:
            flat = flat[0]
        if nm_ == "pos":
            K.dma(junki[0:1, 0:1], Ref(flat.tile, "*", flat.ap[0:1].unsqueeze(0)))
        else:
            K.dma(junk[0:1, 0:1], Ref(flat.tile, "*", flat.ap[0:1].unsqueeze(0)))
    if stage != "full":
        K.dma(out[0:1, 0:64], junk[:])
    chk(-3)
    K.copy(C.identb[:], identf[:])
    K.memset(C.epsc[:], EPS)
    K.memset(C.onec[:], 1.0)
    K.memset(onesb[:], 1.0)
    K.dma(ropec[:], I["rope_cols"][:])
    K.dma(hmask[:], I["hmask"][:])
    chk(-2)
    K.dma(gm[:], I["gmask"][:], eng="pool")
    chk(-1)

    with K.phase():
        C.castb = [K.sb("castb%d" % i, [128, 3072], BF16) for i in range(4)]
        names = list(W.keys())
        C.lazy = {}
        if stage == "full":
            names = ["w_in0"]
            l0 = ["ev_w_out", "s5_w_glu", "ffn_g0", "ffn_u0", "ffn_d0"]
            l1 = ["od_w_in", "od_w_out", "rg_w_r", "rg_w_i", "ffn_g1", "ffn_u1", "ffn_d1"]
            for sq_, lst in ((0, l0), (1, l1)):
                jobs = []
                for nm in lst:
                    r, c = W[nm].h.shape
                    for r0 in range(0, r, 128):
                        jobs.append((W[nm], I[nm], r0, min(128, r - r0), c))
                C.lazy[sq_] = jobs
        if stage in ("A", "S5prep"):
            names = ["w_in0"]
        if stage == "L1a":
            names = ["od_w_in", "od_w_out", "rg_w_r", "rg_w_i", "ffn_g1", "ffn_u1", "ffn_d1"]
        if stage == "S5prep":
            names = []
        for nm in names:
            r, c = W[nm].h.shape
            cast_weight(K, C, W[nm], I[nm], r, c)

    chk(0)
    with K.phase():
        ps_a = K.ps("ps_a", [128, 512], F32)
        ps_b = K.ps("ps_b", [128, 512], F32)
        Tz = K.sb("Tz", [128, 32, 128], BF16)
        Bs = K.sb("Bs", [128, 32, 128], BF16)
        CsRe = K.sb("CsRe", [128, 16, 128], BF16)
        CsIm = K.sb("CsIm", [128, 16, 128], BF16)
        MU = K.sb("MU", [128, 3, 16], F32)
        araw = K.sb("araw", [32, 2, 128], F32)
        for j, nm in enumerate(("s5_a_re", "s5_a_im")):
            K.dma(araw[:, j, 0:64], I[nm][:])
            K.dma(araw[:, j, 64:128], I[nm][:])
        are = K.sb("are", [128, 32], F32)
        aim = K.sb("aim", [128, 32], F32)
        K.tr(ps_a[:, 0:32], araw[:, 0, :], identf[0:32, 0:32])
        K.tr(ps_a[:, 32:64], araw[:, 1, :], identf[0:32, 0:32])
        K.copy(are[:], ps_a[:, 0:32])
        K.copy(aim[:], ps_a[:, 32:64])
        chk(1)
        dtb = K.sb("dtb", [128, 32], F32)
        K.dma(dtb[:], I["s5_log_dt"][0:1, :].pb(128))
        K.act(dtb[:], dtb[:], AF.Exp)
        mag = K.sb("mag", [128, 32], F32)
        ang = K.sb("ang", [128, 32], F32)
        K.tt(mag[:], are[:], dtb[:], ALU.mult)
        K.act(mag[:], mag[:], AF.Exp)
        K.tt(ang[:], aim[:], dtb[:], ALU.mult)
        chk(2)
        sn = K.sb("sn", [128, 32], F32)
        cs = K.sb("cs", [128, 32], F32)
        sincos(K, C, ang[:], [128, 32], sn[:], cs[:], "sc1")
        chk(3)
        lr = K.sb("lr", [128, 32], F32)
        li = K.sb("li", [128, 32], F32)
        K.tt(lr[:], mag[:], cs[:], ALU.mult)
        K.tt(li[:], mag[:], sn[:], ALU.mult)
        PA = K.sb("PA", [128, 8, 32], F32)
        PB = K.sb("PB", [128, 8, 32], F32)
        tq = K.sb("tq", [128, 32], F32)
        K.copy(PA[:, 0, :], hmask[:, 0:1].bc([128, 32]))
        K.copy(PB[:, 0, :], hmask[:, 1:2].bc([128, 32]))
        for k in range(1, 8):
            K.tt(tq[:], li[:], PB[:, k - 1, :], ALU.mult)
            K.tt(PA[:, k, :], lr[:], PA[:, k - 1, :], ALU.mult)
            K.tt(PA[:, k, :], PA[:, k, :], tq[:], ALU.add)
            K.tt(tq[:], li[:], PA[:, k - 1, :], ALU.mult)
            K.tt(PB[:, k, :], lr[:], PB[:, k - 1, :], ALU.mult)
            K.tt(PB[:, k, :], PB[:, k, :], tq[:], ALU.subtract)
        lm1 = K.sb("lm1", [128, 32], F32)
        K.ts(lm1[:], lr[:], -1.0, ALU.add)
        den = K.sb("den", [128, 32], F32)
        K.tt(den[:], are[:], are[:], ALU.mult)
        K.tt(tq[:], aim[:], aim[:], ALU.mult)
        K.tt(den[:], den[:], tq[:], ALU.add)
        K.recip(den[:], den[:])
        wr = K.sb("wr", [128, 32], F32)
        wi = K.sb("wi", [128, 32], F32)
        K.tt(wr[:], lm1[:], are[:], ALU.mult)
        K.tt(tq[:], li[:], aim[:], ALU.mult)
        K.tt(wr[:], wr[:], tq[:], ALU.add)
        K.tt(wr[:], wr[:], den[:], ALU.mult)
        K.tt(wi[:], li[:], are[:], ALU.mult)
        K.tt(tq[:], lm1[:], aim[:], ALU.mult)
        K.tt(wi[:], wi[:], tq[:], ALU.subtract)
        K.tt(wi[:], wi[:], den[:], ALU.mult)
        chk(4)
        bre = K.sb("bre", [128, 32, 16], F32)
        bim = K.sb("bim", [128, 32, 16], F32)
        for t_, nm in ((bre, "s5_b_re"), (bim, "s5_b_im")):
            for h in range(2):
                K.dma(t_.k(h)[h * 64:(h + 1) * 64, :, :], I[nm][:].re("g p c -> p g c"), eng=("sp" if h == 0 else "pool"))
        chk(5)
        Bbr = K.sb("Bbr", [128, 32, 16], F32)
        Bbi = K.sb("Bbi", [128, 32, 16], F32)
        t3 = K.sb("t3", [128, 32, 16], F32)
        K.tt(Bbr[:], bre[:], bc_last(wr[:], 16), ALU.mult)
        K.tt(t3[:], bim[:], bc_last(wi[:], 16), ALU.mult)
        K.tt(Bbr[:], Bbr[:], t3[:], ALU.subtract)
        K.tt(Bbi[:], bim[:], bc_last(wr[:], 16), ALU.mult)
        K.tt(t3[:], bre[:], bc_last(wi[:], 16), ALU.mult)
        K.tt(Bbi[:], Bbi[:], t3[:], ALU.add)
        Wpad = K.sb("Wpad", [128, 32, 15, 16], F32)
        K.memset(Wpad[:], 0.0)
        for m in range(8):
            k = 7 - m
            K.tt(Wpad[:, :, m, :], Bbr[:], bc_last(PA[:, k, :], 16), ALU.mult)
            K.tt(t3[:], Bbi[:], bc_last(PB[:, k, :], 16), ALU.mult)
            K.tt(Wpad[:, :, m, :], Wpad[:, :, m, :], t3[:], ALU.add)
        chk(6)
        craw = K.sb("craw", [128, 4, 128], F32)
        for t_ in range(4):
            K.dma(craw.k((t_, 0))[:, t_, 0:64], I["s5_c_re"][t_ * 8:(t_ + 1) * 8].re("g c p -> (g c) p"))
            K.dma(craw.k((t_, 1))[:, t_, 64:128], I["s5_c_im"][t_ * 8:(t_ + 1) * 8].re("g c p -> (g c) p"), eng="pool")
        Vst = K.sb("Vst", [128, 32, 16], F32)
        for t_ in range(4):
            K.tr(ps_a[:, t_ * 128:(t_ + 1) * 128], craw[:, t_, :], identf[:])
        K.ts(Vst[:].re("p g c -> p (g c)"), ps_a[:], hmask[:, 2:3], ALU.mult)
        chk(7)
        dcol = K.sb("dcol", [128, 32], F32)
        for j in range(8):
            K.dma(dcol.k(j)[j * 16:(j + 1) * 16, :], I["s5_d"][:].re("g c -> c g"), allow_slow_non_contiguous=True, eng=("sp" if j % 2 == 0 else "pool"))
        chk(8)
        for g in range(32):
            pz = ps_b if g % 2 == 0 else ps_a
            for i in range(8):
                K.mm(pz[:, i * 16:(i + 1) * 16], Wpad[:, g, 7 - i:15 - i, :].re("p m c -> p (m c)"), Vst[:, g, :])
            K.stt(Tz[:, g, :], identf[:], dcol[:, g:g + 1], pz[:, 0:128], ALU.mult, ALU.add)
            K.tr(pz[:, 128:256], Wpad[:, g, 0:8, :].re("p m c -> p (m c)"), identf[:])
            K.copy(Bs[:, g, :], pz[:, 128:256], eng="act")
        chk(9)
        praw = K.sb("praw", [16, 2, 128], F32)
        K.dma(praw[:, 0, :], I["s5_a_re"][:].re("(pr h) p -> pr (h p)", h=2))
        K.dma(praw[:, 1, :], I["s5_a_im"][:].re("(pr h) p -> pr (h p)", h=2))
        K.tr(ps_a[:, 0:16], praw[:, 0, :], identf[0:16, 0:16])
        K.tr(ps_a[:, 16:32], praw[:, 1, :], identf[0:16, 0:16])
        arp = K.sb("arp", [128, 16], F32)
        aip = K.sb("aip", [128, 16], F32)
        K.copy(arp[:], ps_a[:, 0:16])
        K.copy(aip[:], ps_a[:, 16:32])
        chk(10)
        dtp = K.sb("dtp", [128, 16], F32)
        ldv = I["s5_log_dt"][0:1, :].re("o (pr h) -> o h pr", h=2)
        K.dma(dtp[0:64, :], ldv[:, 0, :].pb(64), allow_slow_non_contiguous=True)
        K.dma(dtp[64:128, :], ldv[:, 1, :].pb(64), allow_slow_non_contiguous=True)
        K.act(dtp[:], dtp[:], AF.Exp)
        chk(11)
        magp = K.sb("magp", [128, 16], F32)
        angp = K.sb("angp", [128, 16], F32)
        K.tt(magp[:], arp[:], dtp[:], ALU.mult)
        K.act(magp[:], magp[:], AF.Exp)
        K.tt(angp[:], aip[:], dtp[:], ALU.mult)
        snp = K.sb("snp", [128, 16], F32)
        csp = K.sb("csp", [128, 16], F32)
        sincos(K, C, angp[:], [128, 16], snp[:], csp[:], "sc2")
        Pr = K.sb("Pr", [128, 9, 16], F32)
        Pi = K.sb("Pi", [128, 9, 16], F32)
        K.tt(Pr[:, 1, :], magp[:], csp[:], ALU.mult)
        K.tt(Pi[:, 1, :], magp[:], snp[:], ALU.mult)
        tp = K.sb("tp", [128, 16], F32)
        for k in range(2, 9):
            K.tt(tp[:], Pi[:, 1, :], Pi[:, k - 1, :], ALU.mult)
            K.tt(Pr[:, k, :], Pr[:, 1, :], Pr[:, k - 1, :], ALU.mult)
            K.tt(Pr[:, k, :], Pr[:, k, :], tp[:], ALU.subtract)
            K.tt(tp[:], Pi[:, 1, :], Pr[:, k - 1, :], ALU.mult)
            K.tt(Pi[:, k, :], Pr[:, 1, :], Pi[:, k - 1, :], ALU.mult)
            K.tt(Pi[:, k, :], Pi[:, k, :], tp[:], ALU.add)
        K.copy(MU[:, 0, :], Pr[:, 8, :])
        K.copy(MU[:, 1, :], Pi[:, 8, :])
        K.ts(MU[:, 2, :], Pi[:, 8, :], -1.0, ALU.mult)
        MUP = K.sb("MUP", [128, 3, 16, 16], F32)
        K.copy(MUP[:, 0, 0, :], Pr[:, 8, :])
        K.copy(MUP[:, 1, 0, :], Pi[:, 8, :])
        for k in range(1, 16):
            K.tt(tp[:], MU[:, 1, :], MUP[:, 1, k - 1, :], ALU.mult)
            K.tt(MUP[:, 0, k, :], MU[:, 0, :], MUP[:, 0, k - 1, :], ALU.mult)
            K.tt(MUP[:, 0, k, :], MUP[:, 0, k, :], tp[:], ALU.subtract)
            K.tt(tp[:], MU[:, 1, :], MUP[:, 0, k - 1, :], ALU.mult)
            K.tt(MUP[:, 1, k, :], MU[:, 0, :], MUP[:, 1, k - 1, :], ALU.mult)
            K.tt(MUP[:, 1, k, :], MUP[:, 1, k, :], tp[:], ALU.add)
        K.ts(MUP[:, 2, :, :], MUP[:, 1, :, :], -1.0, ALU.mult)
        K.dma(C.dMUP[:], MUP[:])
        chk(12)
        cpraw = K.sb("cpraw", [128, 4, 128], F32)
        for ri, nm in enumerate(("s5_c_re", "s5_c_im")):
            for pr in range(16):
                for h in range(2):
                    K.dma(cpraw.k((ri, pr, h))[(pr % 8) * 16:(pr % 8 + 1) * 16, ri * 2 + pr // 8, h * 64:(h + 1) * 64], I[nm][2 * pr + h], eng="sp" if h == 0 else "pool")
        chk(13)
        Cp = K.sb("Cp", [128, 2, 16, 16], F32)
        for q in range(4):
            K.tr(ps_b[:, q * 128:(q + 1) * 128], cpraw[:, q, :], identf[:])
        K.copy(Cp[:].re("p r q c -> p (r q c)"), ps_b[:])
        t4 = K.sb("t4", [128, 16, 16], F32)
        t5 = K.sb("t5", [128, 16, 16], F32)
        for i in range(8):
            pr_b = bc_last(Pr[:, i + 1, :], 16)
            pi_b = bc_last(Pi[:, i + 1, :], 16)
            K.tt(t4[:], Cp[:, 0, :, :], pr_b, ALU.mult)
            K.tt(t5[:], Cp[:, 1, :, :], pi_b, ALU.mult)
            K.tt(CsRe[:].re("p q (i c) -> p q i c", i=8)[:, :, i, :], t4[:], t5[:], ALU.subtract)
            K.tt(t4[:], Cp[:, 0, :, :], pi_b, ALU.mult)
            K.tt(t5[:], Cp[:, 1, :, :], pr_b, ALU.mult)
            K.tt(t4[:], t4[:], t5[:], ALU.add)
            K.ts(CsIm[:].re("p q (i c) -> p q i c", i=8)[:, :, i, :], t4[:], -1.0, ALU.mult)
        chk(14)
        K.dma(C.dTz[:], Tz[:]); K.dma(C.dBs[:], Bs[:]); K.dma(C.dCsRe[:], CsRe[:]); K.dma(C.dCsIm[:], CsIm[:]); K.dma(C.dMU[:], MU[:])
        if "s5prep" in dbg:
            for nm, t_, sh in (("Tz", Tz, [128, 32, 128]), ("Bs", Bs, [128, 32, 128]), ("CsRe", CsRe, [128, 16, 128]), ("CsIm", CsIm, [128, 16, 128])):
                o = dbgout(nm, sh)
                tf = K.sb("dbgf_" + nm, sh, F32)
                K.copy(tf[:], t_[:])
                K.dma(o[:], tf[:])
            o = dbgout("MU", [128, 3, 16])
            K.dma(o[:], MU[:])
    if stage == "S5prep":
        K.finish()
        return nc, DBG
    C.gm = gm; C.ropec = ropec; C.onesb = onesb; C.identf = identf
    build_layers(K, C, I, W, out, x1d, stage, dbg, dbgout)
    K.finish()
    return nc, DBG


def make_in_maps(inputs, ncores=NCORES):
    hc = host_consts()
    perm = swap_perm()
    w_in = np.asarray(inputs["ev_w_in"][0])
    q, k_, v, u = w_in[:, 0:512], w_in[:, 512:1024], w_in[:, 1024:1536], w_in[:, 1536:2048]
    w_in0 = np.ascontiguousarray(np.concatenate([q, q[:, perm], k_, k_[:, perm], v, u], axis=1))
    shared = {
        "norm_mix_pre": inputs["norm_mix_pre"], "norm_mix_post": inputs["norm_mix_post"],
        "norm_ffn_pre": inputs["norm_ffn_pre"], "norm_ffn_post": inputs["norm_ffn_post"],
        "w_in0": w_in0, "ev_w_out": inputs["ev_w_out"][0],
        "s5_a_re": inputs["s5_a_re"][0], "s5_a_im": inputs["s5_a_im"][0], "s5_log_dt": inputs["s5_log_dt"].reshape(1, 32),
        "s5_b_re": inputs["s5_b_re"][0], "s5_b_im": inputs["s5_b_im"][0], "s5_c_re": inputs["s5_c_re"][0], "s5_c_im": inputs["s5_c_im"][0],
        "s5_d": inputs["s5_d"][0], "s5_w_glu": inputs["s5_w_glu"][0],
        "od_w_in": inputs["od_w_in"][0], "od_w_out": inputs["od_w_out"][0],
        "rg_conv_w": inputs["rg_conv_w"][0], "rg_conv_b": inputs["rg_conv_b"].reshape(1, D),
        "rg_w_r": inputs["rg_w_r"][0].reshape(1024, 256), "rg_b_r": inputs["rg_b_r"].reshape(1, D),
        "rg_w_i": inputs["rg_w_i"][0].reshape(1024, 256), "rg_b_i": inputs["rg_b_i"].reshape(1, D), "rg_lam": inputs["rg_lam"].reshape(1, D),
    }
    for L in range(2):
        shared["ffn_g%d" % L] = inputs["ffn_w_gate"][L]
        shared["ffn_u%d" % L] = inputs["ffn_w_up"][L]
        shared["ffn_d%d" % L] = inputs["ffn_w_down"][L]
    shared.update(hc)
    shared = {k: np.ascontiguousarray(np.asarray(v_)) for k, v_ in shared.items()}
    maps = []
    x = np.asarray(inputs["x"])
    pos = np.asarray(inputs["positions"])
    for c in range(ncores):
        m = dict(shared)
        m["x"] = np.ascontiguousarray(x[c * NSEQ:(c + 1) * NSEQ].reshape(TOK, D))
        m["pos"] = np.ascontiguousarray(pos[c * NSEQ:(c + 1) * NSEQ].reshape(1, TOK).astype(np.int32))
        maps.append(m)
    return maps


def build_layers(K, C, I, W, out, x1d, stage, dbg, dbgout):
    nseq = NSEQ
    if stage in ("A", "B", "C", "D0"):
        nseq = 1
    gm = C.gm
    if stage == "L1a":
        C.g_mix_pre = [None, None]; C.g_mix_post = [None, None]; C.g_ffn_pre = [None, None]; C.g_ffn_post = [None, None]
        with K.phase():
            K.dma(x1d[0:1024, :], I["x"][0:1024, :])
        build_layer1(K, C, I, W, out, x1d, stage, dbg, dbgout)
        return
    with K.phase():
        C.g_mix_pre = [None, None]; C.g_mix_post = [None, None]; C.g_ffn_pre = [None, None]; C.g_ffn_post = [None, None]
        C.g_mix_pre[0] = load_bc(K, "g_mp0", I["norm_mix_pre"][0:1, :], D)
        C.g_mix_post[0] = load_bc(K, "g_mo0", I["norm_mix_post"][0:1, :], D)
        C.g_ffn_pre[0] = load_bc(K, "g_fp0", I["norm_ffn_pre"][0:1, :], D)
        C.g_ffn_post[0] = load_bc(K, "g_fo0", I["norm_ffn_post"][0:1, :], D)
        C.scr_j = K.sb("scr_j", [128, D], BF16)
        C.scr_f = K.sb("scr_f", [128, D], F32)
        C.scr_g = K.sb("scr_g", [128, D], F32)
        C.hn = K.sb("hn", [128, 4, D], BF16)
        B1 = K.sb("B1", [128, 4, SEQ], BF16)
        B2 = K.sb("B2", [128, 4, SEQ], BF16)
        B3 = K.sb("B3", [128, 8192], BF16)
        UT = K.sb("UT", [128, 32, 256], BF16)
        QT = B1; attT = B1; KT = B2; ssmT = B2
        V = B3[:].re("p (s f) -> p s f", f=512)
        ygT = B3[:].re("p (k t) -> p k t", k=4)
        for s in range(nseq):
            tok0 = s * SEQ
            with K.phase():
                xt = K.sb("xtA", [128, 4, D], F32)
                xnT = K.sb("xnT", [128, 8, 1024], BF16)
                wch = [K.sb("wch%d" % i, [128, 8, 512], BF16) for i in range(2)]
                posi = K.sb("posi", [128, 512], I32)
                ang = K.sb("angA", [128, 512], F32)
                cosT = K.sb("cosT", [128, 1024], F32)
                sinT = K.sb("sinT", [128, 1024], F32)
                sct = (K.sb("sct", [128, 512], F32), K.sb("scti", [128, 512], I32), K.sb("sckf", [128, 512], F32), K.sb("scr", [128, 512], F32))
                tA = K.sb("tA", [128, 512], F32)
                tB = K.sb("tB", [128, 512], F32)
                UA = K.sb("UA", [128, 32, 8, 16], BF16)
                C.pst = [K.ps("pst%d" % i, [128, 1024], BF16) for i in range(2)]
                psm = [K.ps("psm%d" % i, [128, 512], F32) for i in range(6)]
                pmi = 0
                wv = W["w_in0"][:].re("(kc p) n -> p kc n", p=128)
                for blk in range(2):
                    b0 = tok0 + blk * 1024
                    for half in range(2):
                        K.dma(xt[:], I["x"][b0 + half * 512:b0 + (half + 1) * 512, :].re("(s p) d -> p s d", p=128))
                        rms_pre(K, C, xt[:], 4, C.g_mix_pre[0], C.hn[:], "a")
                        transpose_to(K, C, C.hn[:], 4, xnT, half * 512)
                        K.dma(posi[:], I["pos"][0:1, b0 + half * 512:b0 + (half + 1) * 512].pb(128))
                        K.copy(ang[:], posi[:])
                        K.ts(ang[:], ang[:], C.ropec[:, 0:1], ALU.mult)
                        hsl = slice(half * 512, (half + 1) * 512)
                        sincos(K, C, ang[:], [128, 512], sinT[:, hsl], cosT[:, hsl], sct)
                        K.ts(sinT[:, hsl], sinT[:, hsl], C.ropec[:, 1:2], ALU.mult)
                    for qk in range(2):
                        dst = QT if qk == 0 else KT
                        wq, wsw = wch[0], wch[1]
                        K.dma(wq[:], wv[:, :, (2 * qk) * 512:(2 * qk + 1) * 512])
                        K.dma(wsw[:], wv[:, :, (2 * qk + 1) * 512:(2 * qk + 2) * 512])
                        for t in range(4):
                            for nh in range(2):
                                pq = psm[pmi % 6]; psw = psm[(pmi + 1) % 6]; pmi += 2
                                for kc in range(8):
                                    K.mm(pq[:], wq[:, kc, t * 128:(t + 1) * 128], xnT.k(kc)[:, kc, nh * 512:(nh + 1) * 512], start=(kc == 0), stop=(kc == 7))
                                for kc in range(8):
                                    K.mm(psw[:], wsw[:, kc, t * 128:(t + 1) * 128], xnT.k(kc)[:, kc, nh * 512:(nh + 1) * 512], start=(kc == 0), stop=(kc == 7))
                                K.tt(tA[:], pq[:], cosT[:, nh * 512:(nh + 1) * 512], ALU.mult)
                                K.tt(tB[:], psw[:], sinT[:, nh * 512:(nh + 1) * 512], ALU.mult)
                                c0 = blk * 1024 + nh * 512
                                K.tt(dst[:, t, c0:c0 + 512], tA[:], tB[:], ALU.add, eng="pool")
                    wvv = wch[0]
                    K.dma(wvv[:], wv[:, :, 4 * 512:5 * 512])
                    for sub in range(8):
                        pv = psm[pmi % 6]; pmi += 1
                        for kc in range(8):
                            K.mm(pv[:], xnT.k(kc)[:, kc, sub * 128:(sub + 1) * 128], wvv[:, kc, :], start=(kc == 0), stop=(kc == 7))
                        K.copy(V[:, blk * 8 + sub, :], pv[:], eng="act")
                    wu_ = wch[1]
                    K.dma(wu_[:], wv[:, :, 5 * 512:6 * 512])
                    for j in range(8):
                        pu = psm[pmi % 6]; pmi += 1
                        for kc in range(8):
                            K.mm(pu[:], xnT.k(kc)[:, kc, :].re("p (c j) -> p j c", j=8)[:, j, :], wu_[:, kc, :], start=(kc == 0), stop=(kc == 7))
                        K.copy(UA[:, :, j, :], pu[:].re("p (g c) -> p g c", c=16), eng="act")
                    for g4 in range(8):
                        pt = C.pst[g4 % 2]
                        for gi in range(4):
                            g = g4 * 4 + gi
                            K.tr(pt[:, gi * 128:(gi + 1) * 128], UA[:, g, :, :].re("p j c -> p (j c)"), C.identb[:])
                        K.copy(UT[:, g4 * 4:(g4 + 1) * 4, blk * 128:(blk + 1) * 128], pt[:, 0:512].re("p (g c) -> p g c", g=4), eng=("act" if g4 % 2 == 0 else "dve"))
            if "A" in dbg and s == 0:
                for nm, t_, sh in (("QT", QT[:], [128, 4, SEQ]), ("KT", KT[:], [128, 4, SEQ]), ("V", V, [128, 16, 512]), ("UT", UT[:], [128, 32, 256])):
                    with K.phase():
                        o = dbgout(nm, sh)
                        tf = K.sb("dbgf" + nm, sh, F32)
                        K.copy(tf[:], t_)
                        K.dma(o[:], tf[:])
            if stage == "A":
                raise Stop()
            with K.phase():
                pS = [K.ps("pS%d" % i, [128, 512], F32) for i in range(3)]
                pnd = [[K.ps("pnd%d_%d" % (g_, h_), [128, 512], F32) for h_ in range(2)] for g_ in range(2)]
                pshift = K.ps("pshift", [128, 512], F32)
                Pb = [K.sb("Pb%d" % i, [128, 512], BF16) for i in range(4)]
                Pm = [K.sb("Pm%d" % i, [128, 512], BF16) for i in range(4)]
                rden = K.sb("rden", [128, 512], F32)
                Dsb = K.sb("Dsb", [128, 512], F32)
                Vx = K.sb("Vx", [128, 16, 8, 128], BF16)
                K.memset(Vx[:], 1.0)
                for h_ in range(8):
                    cs_ = slice(0, 64) if h_ % 2 == 0 else slice(64, 128)
                    K.copy(Vx[:, :, h_, cs_], V[:, :, h_ * 64:(h_ + 1) * 64], eng=("act" if h_ % 2 == 0 else "dve"))
                jobs = list(C.lazy.get(s, []))
                njobs = len(jobs)
                if njobs:
                    lzb = [K.sb("lzb%d" % i, [128, 2816], BF16) for i in range(4)]
                lzi = 0

                def emit_job():
                    nonlocal lzi
                    dst_, src_, r0_, n_, c_ = jobs.pop(0)
                    b_ = lzb[lzi % 4]
                    lzi += 1
                    K.dma(b_[0:n_, 0:c_], src_[r0_:r0_ + n_, :], eng="pool")
                    K.dma(dst_.k(r0_)[r0_:r0_ + n_, :], b_[0:n_, 0:c_], eng="sp")
                LA = 2
                items = []
                for t in range(4):
                    for qc in range(4):
                        nkb = 4 * qc + 4
                        for kb in range(nkb):
                            for hh in range(2):
                                items.append((t, qc, kb, hh, nkb))
                n_it = len(items)
                for it_ in range(n_it + LA):
                    if it_ < n_it:
                        t, qc, kb, hh, nkb = items[it_]
                        hs = slice(hh * 64, (hh + 1) * 64)
                        delta = 4 * qc - kb
                        ps_ = pS[it_ % 3]
                        K.mm(ps_[:], KT[hs, t, kb * 128:(kb + 1) * 128], QT.k((t, qc))[hs, t, qc * 512:(qc + 1) * 512])
                        pb_ = Pb[it_ % 4]; pm_ = Pm[it_ % 4]
                        K.act(pb_[:], ps_[:], AF.Exp, scale=0.125)
                        K.tt(pm_[:], pb_[:], gm[:, 128 * (delta + 3):128 * (delta + 3) + 512], ALU.mult, eng=("dve" if (njobs or it_ % 3 != 2) else "pool"))
                        if jobs and it_ % 8 == 4:
                            emit_job()
                    j_ = it_ - LA
                    if j_ >= 0:
                        t, qc, kb, hh, nkb = items[j_]
                        hs = slice(hh * 64, (hh + 1) * 64)
                        grp = t * 4 + qc
                        nd = pnd[grp % 2][hh]
                        pm_ = Pm[j_ % 4]
                        K.mm(nd[:], Vx[:, kb, 2 * t + hh, :], pm_[:], start=(kb == 0), stop=(kb == nkb - 1))
                        if kb == nkb - 1:
                            dsl = slice(64, 128) if hh == 0 else slice(0, 64)
                            K.copy(Dsb.k(hh)[dsl, :], nd[dsl, :], eng="act")
                            K.mm(pshift[hs, :], C.identf[dsl, dsl], Dsb.k(hh)[dsl, :])
                            K.recip(rden.k(hh)[hs, :], pshift[hs, :])
                            K.tt(attT.k((t, qc))[hs, t, qc * 512:(qc + 1) * 512], nd[hs, :], rden.k(hh)[hs, :], ALU.mult)
                while jobs:
                    emit_job()
            if "B" in dbg and s == 0:
                with K.phase():
                    o = dbgout("attT", [128, 4, SEQ]); tf = K.sb("dbgfa", [128, 4, SEQ], F32)
                    K.copy(tf[:], attT[:]); K.dma(o[:], tf[:])
            if stage == "B":
                raise Stop()
            with K.phase():
                EX = K.sb("EX", [128, 2, 16, 257], F32)
                EXb = K.sb("EXb", [128, 2, 16, 257], BF16)
                MU = K.sb("MUc", [128, 3, 16], F32)
                K.dma(MU[:], C.dMU[:])
                wglu = K.sb("wglu", [128, 4, 512], BF16)
                K.dma(wglu[:], W["s5_w_glu"][:].re("(kc p) n -> p kc n", p=128))
                ptr = [K.ps("ptr%d" % i, [128, 1024], BF16) for i in range(2)]
                pe_ = [K.ps("pe%d" % i, [128, 512], F32) for i in range(4)]
                tr1 = K.sb("tr1", [128, 2, 16], F32)
                tr2 = K.sb("tr2", [128, 2, 16], F32)
                with K.phase():
                    Bs = K.sb("BsC", [128, 32, 128], BF16)
                    K.dma(Bs[:], C.dBs[:])
                    K.memset(EX[:, :, :, 0:1], 0.0)
                    for pr in range(16):
                        for ri in range(2):
                            pp = pe_[(pr * 2 + ri) % 4]
                            for h in range(2):
                                g = 2 * pr + h
                                K.mm(pp[h * 64:(h + 1) * 64, 0:256], Bs[:, g, ri * 64:(ri + 1) * 64], UT[:, g, :])
                            K.copy(EX[:, ri, pr, 1:257], pp[:, 0:256], eng=("act" if ri == 0 else "dve"))
                chk(20)
                MUP = K.sb("MUPc", [128, 3, 16, 16], F32)
                K.dma(MUP[:], C.dMUP[:])
                XV = EX[:, :, :, 1:257].re("p r q (b i) -> p r q b i", i=16)
                T1 = K.sb("T1", [128, 2, 16, 16], F32)
                T2 = K.sb("T2", [128, 2, 16, 16], F32)

                def bc4(ref, nb):
                    return Ref(ref.tile, ref.key, ref.ap.unsqueeze(1).unsqueeze(3).to_broadcast([128, 2, 16, nb]))

                def bc3(ref, nb):
                    return Ref(ref.tile, ref.key, ref.ap.unsqueeze(2).to_broadcast([128, 16, nb]))

                def cmul_add(dst, src, k, nb):
                    K.tt(T1[:, :, :, 0:nb], src, bc4(MUP[:, 0, k, :], nb), ALU.mult)
                    K.tt(T2[:, 0, :, 0:nb], src[:, 1], bc3(MUP[:, 2, k, :], nb), ALU.mult)
                    K.tt(T2[:, 1, :, 0:nb], src[:, 0], bc3(MUP[:, 1, k, :], nb), ALU.mult)
                    K.tt(T1[:, :, :, 0:nb], T1[:, :, :, 0:nb], T2[:, :, :, 0:nb], ALU.add)
                    K.tt(dst, dst, T1[:, :, :, 0:nb], ALU.add)

                for i in range(1, 16):
                    cmul_add(XV[:, :, :, :, i], XV[:, :, :, :, i - 1], 0, 16)
                for bb in range(1, 16):
                    cmul_add(XV[:, :, :, bb:bb + 1, 15], XV[:, :, :, bb - 1:bb, 15], 15, 1)
                for i in range(15):
                    cmul_add(XV[:, :, :, 1:16, i], XV[:, :, :, 0:15, 15], i, 15)
                chk(21)
                K.copy(EXb[:], EX[:])
                if "C1" in dbg and s == 0:
                    o = dbgout("EX", [128, 2, 16, 257])
                    K.dma(o[:], EX[:])
                with K.phase():
                    Tz = K.sb("TzC", [128, 32, 128], BF16)
                    CsRe = K.sb("CsReC", [128, 16, 128], BF16)
                    CsIm = K.sb("CsImC", [128, 16, 128], BF16)
                    K.dma(Tz[:], C.dTz[:]); K.dma(CsRe[:], C.dCsRe[:]); K.dma(CsIm[:], C.dCsIm[:])
                    Yg = [K.sb("Yg%d" % i, [128, 128], BF16) for i in range(2)]
                    YGb = K.sb("YGb", [128, 8, 512], BF16)
                    sg_ = [K.sb("sgC%d" % i, [128, 512], BF16) for i in range(2)]
                    if "C1" in dbg and s == 0:
                        oYA = dbgout("YA", [2, 128, 8, 512])
                        YAf = K.sb("YAf", [128, 8, 512], F32)
                    for blk in range(2):
                        cs_ = slice(blk * 128, (blk + 1) * 128)
                        for g4 in range(8):
                            pt = ptr[g4 % 2]
                            for gi in range(4):
                                g = g4 * 4 + gi
                                pr, h = g // 2, g % 2
                                hs = slice(h * 64, (h + 1) * 64)
                                py = pe_[g % 4]
                                K.mm(py[:, 0:128], Tz[:, g, :], UT[:, g, cs_], start=True, stop=False)
                                K.mm(py[:, 0:128], CsRe[hs, pr, :], EXb[hs, 0, pr, blk * 128:blk * 128 + 128], start=False, stop=False)
                                K.mm(py[:, 0:128], CsIm[hs, pr, :], EXb[hs, 1, pr, blk * 128:blk * 128 + 128], start=False, stop=True)
                                yg_ = Yg[g % 2]
                                K.copy(yg_[:], py[:, 0:128], eng="act")
                                K.tr(pt[:, gi * 128:(gi + 1) * 128], yg_[:], C.identb[:])
                            dstv = YGb[:, :, g4 * 64:(g4 + 1) * 64].re("p i (g c) -> p g i c", g=4)
                            srcv = pt[:, 0:512].re("p (g i c) -> p g i c", g=4, i=8)
                            if "C1" in dbg and s == 0:
                                K.copy(YAf[:, :, g4 * 64:(g4 + 1) * 64].re("p i (g c) -> p g i c", g=4), srcv)
                            K.act(dstv, srcv, AF.Gelu_apprx_tanh)
                        if "C1" in dbg and s == 0:
                            K.dma(oYA[blk], YAf[:])
                        for fc in range(4):
                            pt = ptr[fc % 2]
                            for i in range(8):
                                K.tr(pt[:, i * 128:(i + 1) * 128], YGb[:, i, fc * 128:(fc + 1) * 128], C.identb[:])
                            K.copy(ygT[:, fc, blk * 1024:(blk + 1) * 1024].re("p (c i) -> p i c", i=8), pt[:].re("p (i c) -> p i c", i=8),
                                   eng=("act" if fc % 2 == 0 else "dve"))
                    chk(22)
                    for mt in range(4):
                        for nq in range(4):
                            pg = pe_[(mt * 4 + nq) % 4]
                            for kc in range(4):
                                K.mm(pg[:], wglu[:, kc, mt * 128:(mt + 1) * 128], ygT[:, kc, nq * 512:(nq + 1) * 512], start=(kc == 0), stop=(kc == 3))
                            sg = sg_[(mt * 4 + nq) % 2]
                            K.act(sg[:], pg[:], AF.Sigmoid)
                            K.tt(ssmT[:, mt, nq * 512:(nq + 1) * 512], sg[:], ygT[:, mt, nq * 512:(nq + 1) * 512], ALU.mult)
            chk(23)
            if "C" in dbg and s == 0:
                with K.phase():
                    o = dbgout("ssmT", [128, 4, SEQ]); tf = K.sb("dbgfs", [128, 4, SEQ], F32)
                    K.copy(tf[:], ssmT[:]); K.dma(o[:], tf[:])
            if stage == "C":
                raise Stop()
            with K.phase():
                xts = [K.sb("xtD%d" % i, [128, 4, D], F32) for i in range(2)]
                C.hnT = K.sb("hnT", [128, 8, 512], BF16)
                C.actT = K.sb("actT", [128, NFT, 512], BF16)
                C.wgb = [K.sb("wgb%d" % i, [128, 8, 256], BF16) for i in range(3)]
                C.wub = [K.sb("wub%d" % i, [128, 8, 256], BF16) for i in range(3)]
                C.sgb = [K.sb("sgb%d" % i, [128, 512], F32) for i in range(2)]
                C.rotb = [K.sb("rotb%d" % i, [128, D], BF16) for i in range(6)]
                alloc_psum(K, C, "pD")
                wov = W["ev_w_out"][:].re("(kc p) n -> p kc n", p=128)
                K.dma(xts[0][:], I["x"][tok0:tok0 + 512, :].re("(s p) d -> p s d", p=128), eng="pool")
                for tl in range(4):
                    t0 = tok0 + tl * 512
                    xt = xts[tl % 2]
                    if tl + 1 < 4:
                        K.dma(xts[(tl + 1) % 2][:], I["x"][t0 + 512:t0 + 1024, :].re("(s p) d -> p s d", p=128), eng="pool")
                    proj_tokmajor(K, C, lambda kc, sub: (attT if kc < 4 else ssmT)[:, kc % 4, tl * 512 + sub * 128:tl * 512 + (sub + 1) * 128],
                                  8, wov, C.g_mix_post[0], xt, "mp")
                    if "D0" in dbg and s == 0 and tl == 0:
                        o = dbgout("xa0", [128, 4, D])
                        K.dma(o[:], xt[:])
                    ffn_tile(K, C, xt, 0, W)
                    K.dma(x1d[t0:t0 + 512, :].re("(s p) d -> p s d", p=128), xt[:], eng="pool")
                    if "D0" in dbg and s == 0 and tl == 0:
                        o = dbgout("x1", [128, 4, D])
                        K.dma(o[:], xt[:])
                    if stage == "D0":
                        raise Stop()
    build_layer1(K, C, I, W, out, x1d, stage, dbg, dbgout)


def build_layer1(K, C, I, W, out, x1d, stage, dbg, dbgout):
    ntl = 16
    if stage == "L1a":
        ntl = 2
    with K.phase():
        C.g_mix_pre[1] = load_bc(K, "g_mp1", I["norm_mix_pre"][1:2, :], D)
        C.g_mix_post[1] = load_bc(K, "g_mo1", I["norm_mix_post"][1:2, :], D)
        C.g_ffn_pre[1] = load_bc(K, "g_fp1", I["norm_ffn_pre"][1:2, :], D)
        C.g_ffn_post[1] = load_bc(K, "g_fo1", I["norm_ffn_post"][1:2, :], D)
        C.scr_j = K.sb("scr_j1", [128, D], BF16)
        C.scr_f = K.sb("scr_f1", [128, D], F32)
        C.scr_g = K.sb("scr_g1", [128, D], F32)
        C.hn = K.sb("hn1", [128, 4, D], BF16)
        xts = [K.sb("xt1_%d" % i, [128, 4, D], F32) for i in range(2)]
        C.hnT = K.sb("hnT1", [128, 8, 512], BF16)
        C.actT = K.sb("actT1", [128, NFT, 512], BF16)
        C.wgb = [K.sb("wgb1_%d" % i, [128, 8, 256], BF16) for i in range(2)]
        C.wub = [K.sb("wub1_%d" % i, [128, 8, 256], BF16) for i in range(2)]
        C.sgb = [K.sb("sgb1_%d" % i, [128, 512], F32) for i in range(2)]
        C.rotb = [K.sb("rotb1_%d" % i, [128, D], BF16) for i in range(6)]
        alloc_psum(K, C, "pL")
        xbuf = K.sb("xbuf", [128, 8, 515], F32)
        xc = K.sb("xc", [128, 8, 512], F32)
        yT = C.hnT
        xcb = C.actT[:, 0:8, :]
        gz = C.actT[:, 8:16, :]
        GRb = K.sb("GRb", [128, 4, 512], BF16); GIb = K.sb("GIb", [128, 4, 512], BF16)
        A4 = K.sb("A4", [128, 4, 512], F32); M4 = K.sb("M4", [128, 4, 512], F32)
        hbuf = [K.sb("h_%d" % i, [128, 512], F32) for i in range(2)]
        hc = K.sb("hc", [128, 8], F32)
        cw = K.sb("cw", [128, 8, 4], F32)
        cb = K.sb("cb", [128, 8], F32); br = K.sb("br", [128, 8], F32); bi = K.sb("bi", [128, 8], F32)
        cch = K.sb("cch", [128, 8], F32)
        Wr = K.sb("Wr", [128, 8, 256], BF16); Wi = K.sb("Wi", [128, 8, 256], BF16)
        for j in range(4):
            K.dma(cw[:, :, j], I["rg_conv_w"][j:j + 1, :].re("o (ct p) -> p (o ct)", p=128), allow_slow_non_contiguous=True)
        for t_, nm in ((cb, "rg_conv_b"), (br, "rg_b_r"), (bi, "rg_b_i"), (cch, "rg_lam")):
            K.dma(t_[:], I[nm][0:1, :].re("o (ct p) -> p (o ct)", p=128), allow_slow_non_contiguous=True)
        K.act(cch[:], cch[:], AF.Exp, scale=-1.0)
        K.act(cch[:], cch[:], AF.Ln, bias=C.onec[:, 0:1])
        K.ts(cch[:], cch[:], -8.0, ALU.mult)
        cch2 = K.sb("cch2", [128, 8], F32)
        K.ts(cch2[:], cch[:], 2.0, ALU.mult)
        K.dma(Wr[:], W["rg_w_r"][:].re("(q p) n -> p q n", p=128))
        K.dma(Wi[:], W["rg_w_i"][:].re("(q p) n -> p q n", p=128))
        wiv = W["od_w_in"][:].re("(kc p) n -> p kc n", p=128)
        wov = W["od_w_out"][:].re("(kc p) n -> p kc n", p=128)
        K.dma(xts[0][:], x1d[0:512, :].re("(s p) d -> p s d", p=128), eng="pool")
        for tl in range(ntl):
            t0 = tl * 512
            first = (tl % 4 == 0)
            xt = xts[tl % 2]
            if tl + 1 < ntl:
                K.dma(xts[(tl + 1) % 2][:], x1d[t0 + 512:t0 + 1024, :].re("(s p) d -> p s d", p=128), eng="pool")
            rms_pre(K, C, xt[:], 4, C.g_mix_pre[1], C.hn[:], "l")
            transpose_to(K, C, C.hn[:], 4, C.hnT, 0)
            if first:
                K.memset(xbuf[:, :, 0:3], 0.0)
            else:
                K.copy(xbuf[:, :, 0:3], xbuf[:, :, 512:515])
            for ch in range(8):
                wb = C.wgb[ch % 2] if ch % 4 < 2 else C.wub[ch % 2]
                K.dma(wb[:], wiv[:, :, ch * 256:(ch + 1) * 256], eng="sp")
                for mi in range(2):
                    mt = ch * 2 + mi
                    pz = C.psm[C.pmi % 6]; C.pmi += 1
                    for kc in range(8):
                        K.mm(pz[:], wb[:, kc, mi * 128:(mi + 1) * 128], C.hnT.k(kc)[:, kc, :], start=(kc == 0), stop=(kc == 7))
                    if mt < 8:
                        ct = mt
                        K.copy(xbuf.k(ct)[:, ct, 3:515], pz[:], eng="act")
                        K.ts(xc.k(ct)[:, ct, :], xbuf.k(ct)[:, ct, 0:512], cw[:, ct, 0:1], ALU.mult, cb[:, ct:ct + 1], ALU.add)
                        for j in range(1, 4):
                            K.stt(xc.k(ct)[:, ct, :], xbuf.k(ct)[:, ct, j:j + 512], cw[:, ct, j:j + 1], xc.k(ct)[:, ct, :], ALU.mult, ALU.add)
                        K.copy(C.actT.k(("x", ct))[:, ct, :], xc.k(ct)[:, ct, :], eng="pool")
                    else:
                        K.act(C.actT.k(("g", mt - 8))[:, mt, :], pz[:], AF.Gelu_apprx_tanh)
            for half in range(2):
                cts = list(range(half * 4, half * 4 + 4))
                for j, ct in enumerate(cts):
                    hq = (ct // 2) * 2
                    cs_ = slice((ct % 2) * 128, (ct % 2 + 1) * 128)
                    pr_ = C.psm[C.pmi % 6]; pi_ = C.psm[(C.pmi + 1) % 6]; C.pmi += 2
                    for kc in range(2):
                        K.mm(pr_[:], Wr[:, hq + kc, cs_], C.actT.k(("x", hq + kc))[:, hq + kc, :], start=(kc == 0), stop=(kc == 1))
                    for kc in range(2):
                        K.mm(pi_[:], Wi[:, hq + kc, cs_], C.actT.k(("x", hq + kc))[:, hq + kc, :], start=(kc == 0), stop=(kc == 1))
                    K.act(GRb.k(j)[:, j, :], pr_[:], AF.Sigmoid, bias=br[:, ct:ct + 1])
                    K.act(GIb.k(j)[:, j, :], pi_[:], AF.Sigmoid, bias=bi[:, ct:ct + 1])
                    K.tt(xc.k(ct)[:, ct, :], xc.k(ct)[:, ct, :], GIb.k(j)[:, j, :], ALU.mult, eng="pool")
                for j, ct in enumerate(cts):
                    K.act(A4.k(j)[:, j, :], GRb.k(j)[:, j, :], AF.Exp, scale=cch[:, ct:ct + 1])
                    K.act(M4.k(j)[:, j, :], GRb.k(j)[:, j, :], AF.Exp, scale=cch2[:, ct:ct + 1])
                for j, ct in enumerate(cts):
                    K.act(M4.k(j)[:, j, :], M4.k(j)[:, j, :], AF.Sqrt, scale=-1.0, bias=C.onec[:, 0:1])
                for j, ct in enumerate(cts):
                    K.tt(M4.k(j)[:, j, :], M4.k(j)[:, j, :], xc.k(ct)[:, ct, :], ALU.mult)
                    h_ = hbuf[ct % 2]
                    if first:
                        K.scan(h_[:], A4.k(j)[:, j, :], M4.k(j)[:, j, :], 0.0)
                    else:
                        K.scan(h_[:], A4.k(j)[:, j, :], M4.k(j)[:, j, :], hc.k(ct)[:, ct:ct + 1])
                    K.copy(hc.k(ct)[:, ct:ct + 1], h_[:, 511:512], eng="pool")
                    K.tt(yT.k(ct)[:, ct, :], h_[:], C.actT.k(("g", ct))[:, 8 + ct, :], ALU.mult)
                    if "L1" in dbg and tl < 2:
                        if ct == 0 and tl == 0:
                            C.oh = dbgout("h1", [2, 8, 128, 512])
                        K.dma(C.oh[tl, ct], h_[:])
            proj_tokmajor(K, C, lambda kc, sub: yT[:, kc, sub * 128:(sub + 1) * 128], 8, wov, C.g_mix_post[1], xt, "mq")
            if "L1" in dbg and tl < 2:
                if tl == 0:
                    C.oxa = dbgout("xa1", [2, 128, 4, D])
                K.dma(C.oxa[tl], xt[:])
            ffn_tile(K, C, xt, 1, W)
            K.dma(out[t0:t0 + 512, :].re("(s p) d -> p s d", p=128), xt[:], eng="pool")


def kernel(**inputs):
    nc, _ = build("full")
    maps = make_in_maps(inputs)
    res = run_bass_kernel_spmd(nc, maps, core_ids=list(range(NCORES)))
    outs = [np.asarray(r["out"], dtype=np.float32).reshape(NSEQ, SEQ, D) for r in res.results]
    return np.concatenate(outs, axis=0)
```

```python
import numpy as np
import ml_dtypes
import concourse.bass as bass
import concourse.mybir as mybir
from concourse.bass_utils import run_bass_kernel_spmd

F32 = mybir.dt.float32
BF16 = mybir.dt.bfloat16
I32 = mybir.dt.int32
AF = mybir.ActivationFunctionType
ALU = mybir.AluOpType
AX = mybir.AxisListType

SEM_LIMIT = 24000


class Ref:
    __slots__ = ("tile", "key", "ap")

    def __init__(self, tile, key, ap):
        self.tile = tile
        self.key = key
        self.ap = ap

    def __getitem__(self, idx):
        return Ref(self.tile, self.key, self.ap[idx])

    def re(self, pat, **kw):
        return Ref(self.tile, self.key, self.ap.rearrange(pat, **kw))


class _Keyed:
    def __init__(self, tile, key):
        self.tile = tile
        self.key = key

    def __getitem__(self, idx):
        return Ref(self.tile, self.key, self.tile.h[idx])


class Tile:
    def __init__(self, h, name):
        self.h = h
        self.name = name
        self.reg = {}

    def __getitem__(self, idx):
        return Ref(self, "*", self.h[idx])

    def k(self, key):
        return _Keyed(self, key)


def _merge(dst, src):
    for s, v in src.items():
        if dst.get(s, 0) < v:
            dst[s] = v


class Kern:
    ENG = ("pe", "act", "dve", "pool", "sp")

    def __init__(self, nc, sync_same=("act", "dve", "pool")):
        self.nc = nc
        self.stack = []
        self.prog = {e: [] for e in self.ENG}
        self.free_sems = []
        self.nsem = 0
        self.csem = {}
        self.ccnt = {}
        self.waited = {e: {} for e in self.ENG}
        self.dsems = {e: [] for e in self.ENG}
        self.drr = {e: 0 for e in self.ENG}
        self.dcnt = {}
        self.sync_same = set(sync_same)
        self.semname = {}
        for e in self.ENG:
            self._new_csem(e)
        self.ndma_sems = {"sp": 6, "act": 3, "pool": 4, "dve": 0, "pe": 0}
        for e in self.ENG:
            for _ in range(self.ndma_sems[e]):
                self.dsems[e].append(self._alloc_sem())

    def _alloc_sem(self):
        cm = self.nc.semaphore("s%d" % self.nsem)
        self.nsem += 1
        h = cm.__enter__()
        self.stack.append(cm)
        self.dcnt[h] = 0
        return h

    def _new_csem(self, e):
        self.csem[e] = self._alloc_sem()
        self.ccnt[e] = 0

    def sb(self, name, shape, dt):
        self.uid = getattr(self, "uid", 0) + 1
        name = "sb%d_%s" % (self.uid, name)
        cm = self.nc.sbuf_tensor(name, list(shape), dt)
        h = cm.__enter__()
        self.stack.append(cm)
        return Tile(h, name)

    def ps(self, name, shape, dt=F32):
        self.uid = getattr(self, "uid", 0) + 1
        name = "ps%d_%s" % (self.uid, name)
        nbytes = int(np.prod(shape[1:])) * (4 if dt == F32 else 2)
        assert nbytes == 2048, "PSUM tiles must be exactly one bank"
        cm = self.nc.psum_tensor(name, list(shape), dt)
        h = cm.__enter__()
        self.stack.append(cm)
        t = Tile(h, name)
        t.psum = True
        return t

    def dram(self, name, shape, dt, kind="Internal"):
        t = self.nc.dram_tensor(name, list(shape), dt, kind=kind)
        return Tile(t.ap(), name)

    def _conf(self, ref):
        t = ref.tile
        if ref.key == "*":
            return list(t.reg.values())
        out = []
        if "*" in t.reg:
            out.append(t.reg["*"])
        if ref.key in t.reg:
            out.append(t.reg[ref.key])
        return out

    def emit(self, eng, fn, reads=(), writes=(), dma=False):
        pr = [r for r in reads if getattr(r.tile, "psum", False)]
        if pr:
            reads = [r for r in reads if not getattr(r.tile, "psum", False)]
            writes = list(writes) + pr
        need = {}
        for r in reads:
            for w, _ in self._conf(r):
                _merge(need, w)
        for wr in writes:
            for w, rd in self._conf(wr):
                _merge(need, w)
                _merge(need, rd)
        waits = []
        wd = self.waited[eng]
        own = self.csem[eng]
        for s, v in need.items():
            if wd.get(s, 0) >= v:
                continue
            if (not dma) and s is own and eng not in self.sync_same:
                continue
            wd[s] = v
            waits.append((s, v))
        if dma:
            lst = self.dsems[eng]
            i = self.drr[eng] % len(lst)
            self.drr[eng] += 1
            s = lst[i]
            if self.dcnt[s] + 16 > SEM_LIMIT:
                s = self._alloc_sem()
                lst[i] = s
            self.dcnt[s] += 16
            ev = (s, self.dcnt[s])
            inc = 16
        else:
            if self.ccnt[eng] + 1 > SEM_LIMIT:
                self._new_csem(eng)
            self.ccnt[eng] += 1
            ev = (self.csem[eng], self.ccnt[eng])
            inc = 1
        self.prog[eng].append((waits, fn, ev[0], inc))
        evd = {ev[0]: ev[1]}
        for r in reads:
            reg = r.tile.reg.setdefault(r.key, [{}, {}])
            _merge(reg[1], evd)
        for wr in writes:
            if wr.key == "*":
                wr.tile.reg = {"*": [dict(evd), {}]}
            else:
                wr.tile.reg[wr.key] = [dict(evd), {}]
        return ev

    def wait_all(self, eng, refs):
        need = {}
        for r in refs:
            for w, rd in self._conf(r):
                _merge(need, w)
        waits = [(s, v) for s, v in need.items()]
        self.prog[eng].append((waits, None, None, 0))

    def dma(self, out, in_, eng="sp", **kw):
        return self.emit(eng, lambda e: e.dma_start(out=out.ap, in_=in_.ap, **kw),
                         reads=[in_], writes=[out], dma=True)

    def mm(self, out, lhsT, rhs, start=True, stop=True, **kw):
        return self.emit("pe", lambda e: e.matmul(out.ap, lhsT.ap, rhs.ap, start=start, stop=stop, **kw),
                         reads=[lhsT, rhs], writes=[out])

    def tr(self, out, in_, ident):
        return self.emit("pe", lambda e: e.transpose(out.ap, in_.ap, ident.ap),
                         reads=[in_, ident], writes=[out])

    def act(self, out, in_, func, bias=None, scale=None, accum=None, eng="act", extra_reads=()):
        kw = {}
        reads = [in_] + list(extra_reads)
        writes = [out]
        if bias is not None:
            if isinstance(bias, Ref):
                kw["bias"] = bias.ap
                reads.append(bias)
            else:
                kw["bias"] = bias
        if scale is not None:
            if isinstance(scale, Ref):
                kw["scale"] = scale.ap
                reads.append(scale)
            else:
                kw["scale"] = scale
        if accum is not None:
            kw["accum_out"] = accum.ap
            writes.append(accum)
        return self.emit(eng, lambda e: e.activation(out.ap, in_.ap, func, **kw), reads=reads, writes=writes)

    def tt(self, out, a, b, op, eng="dve"):
        return self.emit(eng, lambda e: e.tensor_tensor(out.ap, a.ap, b.ap, op), reads=[a, b], writes=[out])

    def ts(self, out, a, s1, op0, s2=None, op1=None, accum=None, eng="dve"):
        reads = [a]
        writes = [out]
        v1 = s1
        v2 = s2
        if isinstance(s1, Ref):
            reads.append(s1)
            v1 = s1.ap
        if isinstance(s2, Ref):
            reads.append(s2)
            v2 = s2.ap
        kw = {}
        if op1 is not None:
            kw["op1"] = op1
        if accum is not None:
            kw["accum_out"] = accum.ap
            writes.append(accum)
        return self.emit(eng, lambda e: e.tensor_scalar(out.ap, a.ap, v1, v2, op0, **kw), reads=reads, writes=writes)

    def stt(self, out, a, s, b, op0, op1, eng="dve"):
        reads = [a, b]
        v = s
        if isinstance(s, Ref):
            reads.append(s)
            v = s.ap
        return self.emit(eng, lambda e: e.scalar_tensor_tensor(out.ap, a.ap, v, b.ap, op0, op1), reads=reads, writes=[out])

    def copy(self, out, in_, eng="dve"):
        if eng == "act":
            return self.emit(eng, lambda e: e.copy(out.ap, in_.ap), reads=[in_], writes=[out])
        return self.emit(eng, lambda e: e.tensor_copy(out.ap, in_.ap), reads=[in_], writes=[out])

    def memset(self, out, val, eng="dve"):
        return self.emit(eng, lambda e: e.memset(out.ap, val), reads=[], writes=[out])

    def scan(self, out, d0, d1, init, op0=ALU.mult, op1=ALU.add, eng="dve"):
        reads = [d0, d1]
        v = init
        if isinstance(init, Ref):
            reads.append(init)
            v = init.ap
        return self.emit(eng, lambda e: e.tensor_tensor_scan(out.ap, d0.ap, d1.ap, v, op0, op1), reads=reads, writes=[out])

    def recip(self, out, in_):
        return self.emit("dve", lambda e: e.reciprocal(out.ap, in_.ap), reads=[in_], writes=[out])

    def finish(self):
        nc = self.nc
        prog = self.prog
        with nc.Block() as block:
            def run(e, name):
                for waits, fn, sem, inc in prog[name]:
                    for s, v in waits:
                        e.wait_ge(s, v)
                    if fn is not None:
                        fn(e).then_inc(sem, inc)

            @block.sync
            def _(e):
                run(e, "sp")

            @block.scalar
            def _(e):
                run(e, "act")

            @block.vector
            def _(e):
                run(e, "dve")

            @block.gpsimd
            def _(e):
                run(e, "pool")

            @block.tensor
            def _(e):
                run(e, "pe")
        while self.stack:
            self.stack.pop().__exit__(None, None, None)


def _ref_bc(self, shape):
    return Ref(self.tile, self.key, self.ap.to_broadcast(list(shape)))


def _ref_pb(self, n):
    return Ref(self.tile, self.key, self.ap.partition_broadcast(n))


Ref.bc = _ref_bc
Ref.pb = _ref_pb


class _Phase:
    def __init__(self, K):
        self.K = K

    def __enter__(self):
        self.h = len(self.K.stack)
        return self

    def __exit__(self, *a):
        K = self.K
        K.barrier()
        K.flush()
        while len(K.stack) > self.h:
            K.stack.pop().__exit__(None, None, None)
        return False


def _phase(self):
    return _Phase(self)


def _barrier(self):
    need = {}
    for e in self.ENG:
        if self.ccnt[e] > 0:
            need[self.csem[e]] = self.ccnt[e]
        for s in self.dsems[e]:
            if self.dcnt[s] > 0:
                need[s] = self.dcnt[s]
    for e in self.ENG:
        wd = self.waited[e]
        waits = []
        for s, v in need.items():
            if wd.get(s, 0) >= v:
                continue
            wd[s] = v
            waits.append((s, v))
        if waits:
            self.prog[e].append((waits, None, None, 0))


def _flush(self):
    nc = self.nc
    prog = self.prog
    self.prog = {e: [] for e in self.ENG}
    if not any(prog.values()):
        return
    with nc.Block() as block:
        def run(e, name):
            for waits, fn, sem, inc in prog[name]:
                for s, v in waits:
                    e.wait_ge(s, v)
                if fn is not None:
                    fn(e).then_inc(sem, inc)

        @block.sync
        def _(e):
            run(e, "sp")

        @block.scalar
        def _(e):
            run(e, "act")

        @block.vector
        def _(e):
            run(e, "dve")

        @block.gpsimd
        def _(e):
            run(e, "pool")

        @block.tensor
        def _(e):
            run(e, "pe")


def _finish(self):
    self.barrier()
    self.flush()
    while self.stack:
        self.stack.pop().__exit__(None, None, None)


Kern.phase = _phase
Kern.barrier = _barrier
Kern.flush = _flush
Kern.finish = _finish


import math

NCORES = 8
D = 1024
SEQ = 2048
NSEQ = 4
TOK = NSEQ * SEQ
FF = 2816
NFT = FF // 128
EPS = 1e-6
PI = math.pi
C1 = 6.28125
C2 = 2 * math.pi - 6.28125


def host_consts():
    c = {}
    c["ident"] = np.eye(128, dtype=np.float32)
    invf = np.zeros((128, 1), np.float32)
    sgn = np.zeros((128, 1), np.float32)
    for p in range(128):
        i = p % 64
        if i < 16:
            invf[p, 0] = np.float32(500000.0) ** np.float32(-((i % 8) * 2.0 / 16.0))
            sgn[p, 0] = -1.0 if i < 8 else 1.0
    c["rope_cols"] = np.concatenate([invf, sgn], axis=1)
    x = np.arange(2816)[None, :] - np.arange(128)[:, None] - 384
    m = ((x >= 0) & (x <= 128)).astype(np.float32) + ((x >= 0) & (x % 4 == 0) & (x <= 512)) + ((x >= 0) & (x % 16 == 0) & (x <= 2048))
    c["gmask"] = m.astype(np.float32)
    hm = np.zeros((128, 3), np.float32)
    hm[:64, 0] = 1
    hm[64:, 1] = 1
    hm[:64, 2] = 1
    hm[64:, 2] = -1
    c["hmask"] = hm
    return c


def swap_perm():
    perm = np.arange(512)
    for h in range(8):
        for i in range(16):
            perm[h * 64 + i] = h * 64 + (i + 8 if i < 8 else i - 8)
    return perm


class Ctx:
    pass


def load_bc(K, name, src_row, n, eng="sp"):
    t = K.sb(name, [128, n], F32)
    K.dma(t[:], src_row.pb(128), eng=eng)
    return t


def rms_pre(K, C, xt, nsub, gain, out_bf, tagp):
    for s in range(nsub):
        ss = C.small.k(tagp + "ss%d" % s)[:, C.si:C.si + 1]
        sq = C.scr_j[:, 0:D]
        K.act(sq, xt[:, s, :], AF.Square, accum=ss)
        rs = C.small.k(tagp + "rs%d" % s)[:, C.si + 1:C.si + 2]
        K.act(rs, ss, AF.Sqrt, scale=1.0 / D, bias=C.epsc[:, 0:1])
        K.recip(rs, rs)
        K.stt(out_bf[:, s, :], xt[:, s, :], rs, gain[:], ALU.mult, ALU.mult)
        C.si = (C.si + 2) % 60


def transpose_to(K, C, src_bf, nsub, dstT, col0):
    for kc in range(D // 128):
        pt = C.pst[C.pti % len(C.pst)]
        C.pti += 1
        for s in range(nsub):
            K.tr(pt[:, s * 128:(s + 1) * 128], src_bf[:, s, kc * 128:(kc + 1) * 128], C.identb[:])
        eng = "act" if kc % 2 == 0 else "dve"
        K.copy(dstT.k(kc)[:, kc, col0:col0 + nsub * 128], pt[:, 0:nsub * 128], eng=eng)


def post_norm_add(K, C, ps_pair, gain, xt_sub, tagp, sub=0):
    ssa = C.small.k(tagp + "a")[:, C.si:C.si + 1]
    ssb = C.small.k(tagp + "b")[:, C.si + 1:C.si + 2]
    rs = C.small.k(tagp + "c")[:, C.si + 2:C.si + 3]
    C.si = (C.si + 3) % 60
    C.pn = getattr(C, "pn", 0) + 1
    scr = C.scr_f if C.pn % 2 == 0 else C.scr_g
    K.act(C.scr_j[:, 0:512], ps_pair[0], AF.Square, accum=ssa)
    K.act(C.scr_j[:, 512:1024], ps_pair[1], AF.Square, accum=ssb)
    K.tt(rs, ssa, ssb, ALU.add)
    K.act(rs, rs, AF.Sqrt, scale=1.0 / D, bias=C.epsc[:, 0:1])
    K.recip(rs, rs)
    for h in range(2):
        tmp = scr.k("ab"[h])[:, h * 512:(h + 1) * 512]
        K.stt(tmp, ps_pair[h], rs, gain[:, h * 512:(h + 1) * 512], ALU.mult, ALU.mult)
        xa = Ref(xt_sub.tile, ("xt", sub, h), xt_sub.ap[:, h * 512:(h + 1) * 512])
        K.tt(xa, xa, tmp, ALU.add, eng=("pool" if h == 0 else "dve"))


def rot(C):
    b = C.rotb[C.roti % len(C.rotb)]
    C.roti += 1
    return b


class View:
    def __init__(self, tile, ap):
        self.tile = tile
        self.ap = ap

    def __getitem__(self, idx):
        return Ref(self.tile, "*", self.ap[idx])


def alloc_psum(K, C, tag):
    C.pall = [K.ps("%s%d" % (tag, i), [128, 512], F32) for i in range(8)]
    C.psm = C.pall[0:6]
    C.pst = [View(t, t.h[:].bitcast(BF16)) for t in C.pall[6:8]]


def proj_tokmajor(K, C, lhs_fn, nk, wview, gain, xt, tagp):
    for ps_ in range(2):
        acc = C.pall[ps_ * 4:ps_ * 4 + 4]
        for k in range(nk):
            wb = rot(C)
            K.dma(wb[:], wview[:, k, :], eng="sp")
            for si in range(2):
                sub = ps_ * 2 + si
                for h in range(2):
                    K.mm(acc[si * 2 + h][:], lhs_fn(k, sub), wb[:, h * 512:(h + 1) * 512], start=(k == 0), stop=(k == nk - 1))
        for si in range(2):
            sub = ps_ * 2 + si
            post_norm_add(K, C, (acc[si * 2][:], acc[si * 2 + 1][:]), gain, xt[:, sub, :], tagp, sub)


def ffn_tile(K, C, xt, L, W):
    hn = C.hn
    rms_pre(K, C, xt[:], 4, C.g_ffn_pre[L], hn[:], "f")
    transpose_to(K, C, hn[:], 4, C.hnT, 0)
    NCH = FF // 256
    gv = W["ffn_g%d" % L][:].re("(kc p) n -> p kc n", p=128)
    uv = W["ffn_u%d" % L][:].re("(kc p) n -> p kc n", p=128)
    for ch in range(NCH):
        wg = C.wgb[ch % len(C.wgb)]
        wu = C.wub[ch % len(C.wub)]
        K.dma(wg[:], gv[:, :, ch * 256:(ch + 1) * 256], eng="sp")
        K.dma(wu[:], uv[:, :, ch * 256:(ch + 1) * 256], eng="sp")
        for mi in range(2):
            m = ch * 2 + mi
            pg = C.psm[C.pmi % len(C.psm)]
            pu = C.psm[(C.pmi + 1) % len(C.psm)]
            C.pmi += 2
            for kc in range(8):
                K.mm(pg[:], wg[:, kc, mi * 128:(mi + 1) * 128], C.hnT.k(kc)[:, kc, :], start=(kc == 0), stop=(kc == 7))
            for kc in range(8):
                K.mm(pu[:], wu[:, kc, mi * 128:(mi + 1) * 128], C.hnT.k(kc)[:, kc, :], start=(kc == 0), stop=(kc == 7))
            sg = C.sgb[m % 2]
            K.act(sg[:], pg[:], AF.Silu)
            K.tt(C.actT[:, m, :], sg[:], pu[:], ALU.mult)
    wdv = W["ffn_d%d" % L][:].re("(m p) n -> p m n", p=128)
    proj_tokmajor(K, C, lambda m, sub: C.actT[:, m, sub * 128:(sub + 1) * 128], NFT, wdv, C.g_ffn_post[L], xt, "fp")


def bc_last(ref, n):
    sh = list(ref.ap.shape)
    return Ref(ref.tile, ref.key, ref.ap.unsqueeze(len(sh)).to_broadcast(sh + [n]))


def bc_mid(ref, n):
    sh = list(ref.ap.shape)
    return Ref(ref.tile, ref.key, ref.ap.unsqueeze(1).to_broadcast([sh[0], n] + sh[1:]))


def sincos(K, C, ang, shape, sn, cs, tg):
    if isinstance(tg, str):
        t = K.sb(tg + "_t", shape, F32)
        ti = K.sb(tg + "_ti", shape, I32)
        kf = K.sb(tg + "_kf", shape, F32)
        r = K.sb(tg + "_r", shape, F32)
    else:
        t, ti, kf, r = tg
    for shift, dst in ((0.0, sn), (PI / 2, cs)):
        K.ts(t[:], ang, 1.0 / (2 * PI), ALU.mult, shift / (2 * PI), ALU.add)
        K.copy(ti[:], t[:])
        K.copy(kf[:], ti[:])
        K.stt(r[:], kf[:], -C1, ang, ALU.mult, ALU.add)
        K.stt(r[:], kf[:], -C2, r[:], ALU.mult, ALU.add)
        K.ts(r[:], r[:], shift, ALU.add, PI, ALU.min)
        K.ts(r[:], r[:], -PI, ALU.max)
        K.act(dst, r[:], AF.Sin)


def cast_weight(K, C, dst, src, rows, cols):
    r0 = 0
    while r0 < rows:
        n = min(128, rows - r0)
        b = C.castb[C.casti % len(C.castb)]
        C.casti += 1
        K.dma(b[0:n, 0:cols], src[r0:r0 + n, :], eng="pool")
        K.dma(dst[r0:r0 + n, :], b[0:n, 0:cols], eng="sp")
        r0 += n


class Stop(Exception):
    pass


CUT = [None]


def chk(n):
    if CUT[0] == n:
        raise Stop()


def build(stage="full", dbg=()):
    try:
        return _build(stage, dbg)
    except Stop:
        K = LASTK[0]
        K.finish()
        return K.nc, DBGG[0]


LASTK = [None]
DBGG = [None]


def _build(stage="full", dbg=()):
    nc = bass.Bass("TRN2", target_bir_lowering=False)
    K = Kern(nc)
    LASTK[0] = K
    C = Ctx()
    C.si = 0
    C.pti = 0
    C.pmi = 0
    C.casti = 0
    C.roti = 0
    I = {}

    def inp(name, shape, dt=F32):
        I[name] = K.dram(name, shape, dt, kind="ExternalInput")
        return I[name]

    inp("x", [TOK, D])
    inp("pos", [1, TOK], I32)
    for nm in ("norm_mix_pre", "norm_mix_post", "norm_ffn_pre", "norm_ffn_post"):
        inp(nm, [2, D])
    inp("w_in0", [D, 3072])
    inp("ev_w_out", [D, D])
    inp("s5_a_re", [32, 64]); inp("s5_a_im", [32, 64]); inp("s5_log_dt", [1, 32])
    inp("s5_b_re", [32, 64, 16]); inp("s5_b_im", [32, 64, 16])
    inp("s5_c_re", [32, 16, 64]); inp("s5_c_im", [32, 16, 64]); inp("s5_d", [32, 16])
    inp("s5_w_glu", [512, 512])
    inp("od_w_in", [D, 2048]); inp("od_w_out", [D, D])
    inp("rg_conv_w", [4, D]); inp("rg_conv_b", [1, D])
    inp("rg_w_r", [1024, 256]); inp("rg_b_r", [1, D]); inp("rg_w_i", [1024, 256]); inp("rg_b_i", [1, D]); inp("rg_lam", [1, D])
    for L in range(2):
        inp("ffn_g%d" % L, [D, FF]); inp("ffn_u%d" % L, [D, FF]); inp("ffn_d%d" % L, [FF, D])
    inp("ident", [128, 128]); inp("rope_cols", [128, 2]); inp("gmask", [128, 2816]); inp("hmask", [128, 3])
    out = K.dram("out", [TOK, D], F32, kind="ExternalOutput")
    DBG = {}
    DBGG[0] = DBG

    def dbgout(name, shape, dt=F32):
        DBG[name] = K.dram("dbg_" + name, shape, dt, kind="ExternalOutput")
        return DBG[name]

    W = {}
    for nm, r, c in (("w_in0", D, 3072), ("ev_w_out", D, D), ("s5_w_glu", 512, 512), ("od_w_in", D, 2048), ("od_w_out", D, D),
                     ("rg_w_r", 1024, 256), ("rg_w_i", 1024, 256),
                     ("ffn_g0", D, FF), ("ffn_u0", D, FF), ("ffn_d0", FF, D), ("ffn_g1", D, FF), ("ffn_u1", D, FF), ("ffn_d1", FF, D)):
        if stage != "S5prep":
            W[nm] = K.dram("wb_" + nm, [r, c], BF16)
    x1d = K.dram("x1d", [TOK, D], F32) if stage != "S5prep" else None

    identf = K.sb("identf", [128, 128], F32)
    C.identb = K.sb("identb", [128, 128], BF16)
    C.epsc = K.sb("epsc", [128, 1], F32)
    C.onec = K.sb("onec", [128, 1], F32)
    C.small = K.sb("small", [128, 64], F32)
    onesb = K.sb("onesb", [128, 64], BF16)
    gm = K.sb("gm", [128, 2816], BF16)
    ropec = K.sb("ropec", [128, 2], F32)
    hmask = K.sb("hmask", [128, 3], F32)
    C.dTz = K.dram("dTz", [128, 32, 128], BF16)
    C.dBs = K.dram("dBs", [128, 32, 128], BF16)
    C.dCsRe = K.dram("dCsRe", [128, 16, 128], BF16)
    C.dCsIm = K.dram("dCsIm", [128, 16, 128], BF16)
    C.dMU = K.dram("dMU", [128, 3, 16], F32)
    C.dMUP = K.dram("dMUP", [128, 3, 16, 16], F32)
    K.dma(identf[:], I["ident"][:])
    junk = K.sb("junk", [1, 64], F32)
    junki = K.sb("junki", [1, 4], I32)
    for nm_, t_ in I.items():
        flat = t_[:]
        while len(flat.ap.shape) > 1:
            flat = flat[0]
        if nm_ == "pos":
            K.dma(junki[0:1, 0:1], Ref(flat.tile, "*", flat.ap[0:1].unsqueeze(0)))
        else:
            K.dma(junk[0:1, 0:1], Ref(flat.tile, "*", flat.ap[0:1].unsqueeze(0)))
    if stage != "full":
        K.dma(out[0:1, 0:64], junk[:])
    chk(-3)
    K.copy(C.identb[:], identf[:])
    K.memset(C.epsc[:], EPS)
    K.memset(C.onec[:], 1.0)
    K.memset(onesb[:], 1.0)
    K.dma(ropec[:], I["rope_cols"][:])
    K.dma(hmask[:], I["hmask"][:])
    chk(-2)
    K.dma(gm[:], I["gmask"][:], eng="pool")
    chk(-1)

    with K.phase():
        C.castb = [K.sb("castb%d" % i, [128, 3072], BF16) for i in range(4)]
        names = list(W.keys())
        C.lazy = {}
        if stage == "full":
            names = ["w_in0"]
            l0 = ["ev_w_out", "s5_w_glu", "ffn_g0", "ffn_u0", "ffn_d0"]
            l1 = ["od_w_in", "od_w_out", "rg_w_r", "rg_w_i", "ffn_g1", "ffn_u1", "ffn_d1"]
            for sq_, lst in ((0, l0), (1, l1)):
                jobs = []
                for nm in lst:
                    r, c = W[nm].h.shape
                    for r0 in range(0, r, 128):
                        jobs.append((W[nm], I[nm], r0, min(128, r - r0), c))
                C.lazy[sq_] = jobs
        if stage in ("A", "S5prep"):
            names = ["w_in0"]
        if stage == "L1a":
            names = ["od_w_in", "od_w_out", "rg_w_r", "rg_w_i", "ffn_g1", "ffn_u1", "ffn_d1"]
        if stage == "S5prep":
            names = []
        for nm in names:
            r, c = W[nm].h.shape
            cast_weight(K, C, W[nm], I[nm], r, c)

    chk(0)
    with K.phase():
        ps_a = K.ps("ps_a", [128, 512], F32)
        ps_b = K.ps("ps_b", [128, 512], F32)
        Tz = K.sb("Tz", [128, 32, 128], BF16)
        Bs = K.sb("Bs", [128, 32, 128], BF16)
        CsRe = K.sb("CsRe", [128, 16, 128], BF16)
        CsIm = K.sb("CsIm", [128, 16, 128], BF16)
        MU = K.sb("MU", [128, 3, 16], F32)
        araw = K.sb("araw", [32, 2, 128], F32)
        for j, nm in enumerate(("s5_a_re", "s5_a_im")):
            K.dma(araw[:, j, 0:64], I[nm][:])
            K.dma(araw[:, j, 64:128], I[nm][:])
        are = K.sb("are", [128, 32], F32)
        aim = K.sb("aim", [128, 32], F32)
        K.tr(ps_a[:, 0:32], araw[:, 0, :], identf[0:32, 0:32])
        K.tr(ps_a[:, 32:64], araw[:, 1, :], identf[0:32, 0:32])
        K.copy(are[:], ps_a[:, 0:32])
        K.copy(aim[:], ps_a[:, 32:64])
        chk(1)
        dtb = K.sb("dtb", [128, 32], F32)
        K.dma(dtb[:], I["s5_log_dt"][0:1, :].pb(128))
        K.act(dtb[:], dtb[:], AF.Exp)
        mag = K.sb("mag", [128, 32], F32)
        ang = K.sb("ang", [128, 32], F32)
        K.tt(mag[:], are[:], dtb[:], ALU.mult)
        K.act(mag[:], mag[:], AF.Exp)
        K.tt(ang[:], aim[:], dtb[:], ALU.mult)
        chk(2)
        sn = K.sb("sn", [128, 32], F32)
        cs = K.sb("cs", [128, 32], F32)
        sincos(K, C, ang[:], [128, 32], sn[:], cs[:], "sc1")
        chk(3)
        lr = K.sb("lr", [128, 32], F32)
        li = K.sb("li", [128, 32], F32)
        K.tt(lr[:], mag[:], cs[:], ALU.mult)
        K.tt(li[:], mag[:], sn[:], ALU.mult)
        PA = K.sb("PA", [128, 8, 32], F32)
        PB = K.sb("PB", [128, 8, 32], F32)
        tq = K.sb("tq", [128, 32], F32)
        K.copy(PA[:, 0, :], hmask[:, 0:1].bc([128, 32]))
        K.copy(PB[:, 0, :], hmask[:, 1:2].bc([128, 32]))
        for k in range(1, 8):
            K.tt(tq[:], li[:], PB[:, k - 1, :], ALU.mult)
            K.tt(PA[:, k, :], lr[:], PA[:, k - 1, :], ALU.mult)
            K.tt(PA[:, k, :], PA[:, k, :], tq[:], ALU.add)
            K.tt(tq[:], li[:], PA[:, k - 1, :], ALU.mult)
            K.tt(PB[:, k, :], lr[:], PB[:, k - 1, :], ALU.mult)
            K.tt(PB[:, k, :], PB[:, k, :], tq[:], ALU.subtract)
        lm1 = K.sb("lm1", [128, 32], F32)
        K.ts(lm1[:], lr[:], -1.0, ALU.add)
        den = K.sb("den", [128, 32], F32)
        K.tt(den[:], are[:], are[:], ALU.mult)
        K.tt(tq[:], aim[:], aim[:], ALU.mult)
        K.tt(den[:], den[:], tq[:], ALU.add)
        K.recip(den[:], den[:])
        wr = K.sb("wr", [128, 32], F32)
        wi = K.sb("wi", [128, 32], F32)
        K.tt(wr[:], lm1[:], are[:], ALU.mult)
        K.tt(tq[:], li[:], aim[:], ALU.mult)
        K.tt(wr[:], wr[:], tq[:], ALU.add)
        K.tt(wr[:], wr[:], den[:], ALU.mult)
        K.tt(wi[:], li[:], are[:], ALU.mult)
        K.tt(tq[:], lm1[:], aim[:], ALU.mult)
        K.tt(wi[:], wi[:], tq[:], ALU.subtract)
        K.tt(wi[:], wi[:], den[:], ALU.mult)
        chk(4)
        bre = K.sb("bre", [128, 32, 16], F32)
        bim = K.sb("bim", [128, 32, 16], F32)
        for t_, nm in ((bre, "s5_b_re"), (bim, "s5_b_im")):
            for h in range(2):
                K.dma(t_.k(h)[h * 64:(h + 1) * 64, :, :], I[nm][:].re("g p c -> p g c"), eng=("sp" if h == 0 else "pool"))
        chk(5)
        Bbr = K.sb("Bbr", [128, 32, 16], F32)
        Bbi = K.sb("Bbi", [128, 32, 16], F32)
        t3 = K.sb("t3", [128, 32, 16], F32)
        K.tt(Bbr[:], bre[:], bc_last(wr[:], 16), ALU.mult)
        K.tt(t3[:], bim[:], bc_last(wi[:], 16), ALU.mult)
        K.tt(Bbr[:], Bbr[:], t3[:], ALU.subtract)
        K.tt(Bbi[:], bim[:], bc_last(wr[:], 16), ALU.mult)
        K.tt(t3[:], bre[:], bc_last(wi[:], 16), ALU.mult)
        K.tt(Bbi[:], Bbi[:], t3[:], ALU.add)
        Wpad = K.sb("Wpad", [128, 32, 15, 16], F32)
        K.memset(Wpad[:], 0.0)
        for m in range(8):
            k = 7 - m
            K.tt(Wpad[:, :, m, :], Bbr[:], bc_last(PA[:, k, :], 16), ALU.mult)
            K.tt(t3[:], Bbi[:], bc_last(PB[:, k, :], 16), ALU.mult)
            K.tt(Wpad[:, :, m, :], Wpad[:, :, m, :], t3[:], ALU.add)
        chk(6)
        craw = K.sb("craw", [128, 4, 128], F32)
        for t_ in range(4):
            K.dma(craw.k((t_, 0))[:, t_, 0:64], I["s5_c_re"][t_ * 8:(t_ + 1) * 8].re("g c p -> (g c) p"))
            K.dma(craw.k((t_, 1))[:, t_, 64:128], I["s5_c_im"][t_ * 8:(t_ + 1) * 8].re("g c p -> (g c) p"), eng="pool")
        Vst = K.sb("Vst", [128, 32, 16], F32)
        for t_ in range(4):
            K.tr(ps_a[:, t_ * 128:(t_ + 1) * 128], craw[:, t_, :], identf[:])
        K.ts(Vst[:].re("p g c -> p (g c)"), ps_a[:], hmask[:, 2:3], ALU.mult)
        chk(7)
        dcol = K.sb("dcol", [128, 32], F32)
        for j in range(8):
            K.dma(dcol.k(j)[j * 16:(j + 1) * 16, :], I["s5_d"][:].re("g c -> c g"), allow_slow_non_contiguous=True, eng=("sp" if j % 2 == 0 else "pool"))
        chk(8)
        for g in range(32):
            pz = ps_b if g % 2 == 0 else ps_a
            for i in range(8):
                K.mm(pz[:, i * 16:(i + 1) * 16], Wpad[:, g, 7 - i:15 - i, :].re("p m c -> p (m c)"), Vst[:, g, :])
            K.stt(Tz[:, g, :], identf[:], dcol[:, g:g + 1], pz[:, 0:128], ALU.mult, ALU.add)
            K.tr(pz[:, 128:256], Wpad[:, g, 0:8, :].re("p m c -> p (m c)"), identf[:])
            K.copy(Bs[:, g, :], pz[:, 128:256], eng="act")
        chk(9)
        praw = K.sb("praw", [16, 2, 128], F32)
        K.dma(praw[:, 0, :], I["s5_a_re"][:].re("(pr h) p -> pr (h p)", h=2))
        K.dma(praw[:, 1, :], I["s5_a_im"][:].re("(pr h) p -> pr (h p)", h=2))
        K.tr(ps_a[:, 0:16], praw[:, 0, :], identf[0:16, 0:16])
        K.tr(ps_a[:, 16:32], praw[:, 1, :], identf[0:16, 0:16])
        arp = K.sb("arp", [128, 16], F32)
        aip = K.sb("aip", [128, 16], F32)
        K.copy(arp[:], ps_a[:, 0:16])
        K.copy(aip[:], ps_a[:, 16:32])
        chk(10)
        dtp = K.sb("dtp", [128, 16], F32)
        ldv = I["s5_log_dt"][0:1, :].re("o (pr h) -> o h pr", h=2)
        K.dma(dtp[0:64, :], ldv[:, 0, :].pb(64), allow_slow_non_contiguous=True)
        K.dma(dtp[64:128, :], ldv[:, 1, :].pb(64), allow_slow_non_contiguous=True)
        K.act(dtp[:], dtp[:], AF.Exp)
        chk(11)
        magp = K.sb("magp", [128, 16], F32)
        angp = K.sb("angp", [128, 16], F32)
        K.tt(magp[:], arp[:], dtp[:], ALU.mult)
        K.act(magp[:], magp[:], AF.Exp)
        K.tt(angp[:], aip[:], dtp[:], ALU.mult)
        snp = K.sb("snp", [128, 16], F32)
        csp = K.sb("csp", [128, 16], F32)
        sincos(K, C, angp[:], [128, 16], snp[:], csp[:], "sc2")
        Pr = K.sb("Pr", [128, 9, 16], F32)
        Pi = K.sb("Pi", [128, 9, 16], F32)
        K.tt(Pr[:, 1, :], magp[:], csp[:], ALU.mult)
        K.tt(Pi[:, 1, :], magp[:], snp[:], ALU.mult)
        tp = K.sb("tp", [128, 16], F32)
        for k in range(2, 9):
            K.tt(tp[:], Pi[:, 1, :], Pi[:, k - 1, :], ALU.mult)
            K.tt(Pr[:, k, :], Pr[:, 1, :], Pr[:, k - 1, :], ALU.mult)
            K.tt(Pr[:, k, :], Pr[:, k, :], tp[:], ALU.subtract)
            K.tt(tp[:], Pi[:, 1, :], Pr[:, k - 1, :], ALU.mult)
            K.tt(Pi[:, k, :], Pr[:, 1, :], Pi[:, k - 1, :], ALU.mult)
            K.tt(Pi[:, k, :], Pi[:, k, :], tp[:], ALU.add)
        K.copy(MU[:, 0, :], Pr[:, 8, :])
        K.copy(MU[:, 1, :], Pi[:, 8, :])
        K.ts(MU[:, 2, :], Pi[:, 8, :], -1.0, ALU.mult)
        MUP = K.sb("MUP", [128, 3, 16, 16], F32)
        K.copy(MUP[:, 0, 0, :], Pr[:, 8, :])
        K.copy(MUP[:, 1, 0, :], Pi[:, 8, :])
        for k in range(1, 16):
            K.tt(tp[:], MU[:, 1, :], MUP[:, 1, k - 1, :], ALU.mult)
            K.tt(MUP[:, 0, k, :], MU[:, 0, :], MUP[:, 0, k - 1, :], ALU.mult)
            K.tt(MUP[:, 0, k, :], MUP[:, 0, k, :], tp[:], ALU.subtract)
            K.tt(tp[:], MU[:, 1, :], MUP[:, 0, k - 1, :], ALU.mult)
            K.tt(MUP[:, 1, k, :], MU[:, 0, :], MUP[:, 1, k - 1, :], ALU.mult)
            K.tt(MUP[:, 1, k, :], MUP[:, 1, k, :], tp[:], ALU.add)
        K.ts(MUP[:, 2, :, :], MUP[:, 1, :, :], -1.0, ALU.mult)
        K.dma(C.dMUP[:], MUP[:])
        chk(12)
        cpraw = K.sb("cpraw", [128, 4, 128], F32)
        for ri, nm in enumerate(("s5_c_re", "s5_c_im")):
            for pr in range(16):
                for h in range(2):
                    K.dma(cpraw.k((ri, pr, h))[(pr % 8) * 16:(pr % 8 + 1) * 16, ri * 2 + pr // 8, h * 64:(h + 1) * 64], I[nm][2 * pr + h], eng="sp" if h == 0 else "pool")
        chk(13)
        Cp = K.sb("Cp", [128, 2, 16, 16], F32)
        for q in range(4):
            K.tr(ps_b[:, q * 128:(q + 1) * 128], cpraw[:, q, :], identf[:])
        K.copy(Cp[:].re("p r q c -> p (r q c)"), ps_b[:])
        t4 = K.sb("t4", [128, 16, 16], F32)
        t5 = K.sb("t5", [128, 16, 16], F32)
        for i in range(8):
            pr_b = bc_last(Pr[:, i + 1, :], 16)
            pi_b = bc_last(Pi[:, i + 1, :], 16)
            K.tt(t4[:], Cp[:, 0, :, :], pr_b, ALU.mult)
            K.tt(t5[:], Cp[:, 1, :, :], pi_b, ALU.mult)
            K.tt(CsRe[:].re("p q (i c) -> p q i c", i=8)[:, :, i, :], t4[:], t5[:], ALU.subtract)
            K.tt(t4[:], Cp[:, 0, :, :], pi_b, ALU.mult)
            K.tt(t5[:], Cp[:, 1, :, :], pr_b, ALU.mult)
            K.tt(t4[:], t4[:], t5[:], ALU.add)
            K.ts(CsIm[:].re("p q (i c) -> p q i c", i=8)[:, :, i, :], t4[:], -1.0, ALU.mult)
        chk(14)
        K.dma(C.dTz[:], Tz[:]); K.dma(C.dBs[:], Bs[:]); K.dma(C.dCsRe[:], CsRe[:]); K.dma(C.dCsIm[:], CsIm[:]); K.dma(C.dMU[:], MU[:])
        if "s5prep" in dbg:
            for nm, t_, sh in (("Tz", Tz, [128, 32, 128]), ("Bs", Bs, [128, 32, 128]), ("CsRe", CsRe, [128, 16, 128]), ("CsIm", CsIm, [128, 16, 128])):
                o = dbgout(nm, sh)
                tf = K.sb("dbgf_" + nm, sh, F32)
                K.copy(tf[:], t_[:])
                K.dma(o[:], tf[:])
            o = dbgout("MU", [128, 3, 16])
            K.dma(o[:], MU[:])
    if stage == "S5prep":
        K.finish()
        return nc, DBG
    C.gm = gm; C.ropec = ropec; C.onesb = onesb; C.identf = identf
    build_layers(K, C, I, W, out, x1d, stage, dbg, dbgout)
    K.finish()
    return nc, DBG


def make_in_maps(inputs, ncores=NCORES):
    hc = host_consts()
    perm = swap_perm()
    w_in = np.asarray(inputs["ev_w_in"][0])
    q, k_, v, u = w_in[:, 0:512], w_in[:, 512:1024], w_in[:, 1024:1536], w_in[:, 1536:2048]
    w_in0 = np.ascontiguousarray(np.concatenate([q, q[:, perm], k_, k_[:, perm], v, u], axis=1))
    shared = {
        "norm_mix_pre": inputs["norm_mix_pre"], "norm_mix_post": inputs["norm_mix_post"],
        "norm_ffn_pre": inputs["norm_ffn_pre"], "norm_ffn_post": inputs["norm_ffn_post"],
        "w_in0": w_in0, "ev_w_out": inputs["ev_w_out"][0],
        "s5_a_re": inputs["s5_a_re"][0], "s5_a_im": inputs["s5_a_im"][0], "s5_log_dt": inputs["s5_log_dt"].reshape(1, 32),
        "s5_b_re": inputs["s5_b_re"][0], "s5_b_im": inputs["s5_b_im"][0], "s5_c_re": inputs["s5_c_re"][0], "s5_c_im": inputs["s5_c_im"][0],
        "s5_d": inputs["s5_d"][0], "s5_w_glu": inputs["s5_w_glu"][0],
        "od_w_in": inputs["od_w_in"][0], "od_w_out": inputs["od_w_out"][0],
        "rg_conv_w": inputs["rg_conv_w"][0], "rg_conv_b": inputs["rg_conv_b"].reshape(1, D),
        "rg_w_r": inputs["rg_w_r"][0].reshape(1024, 256), "rg_b_r": inputs["rg_b_r"].reshape(1, D),
        "rg_w_i": inputs["rg_w_i"][0].reshape(1024, 256), "rg_b_i": inputs["rg_b_i"].reshape(1, D), "rg_lam": inputs["rg_lam"].reshape(1, D),
    }
    for L in range(2):
        shared["ffn_g%d" % L] = inputs["ffn_w_gate"][L]
        shared["ffn_u%d" % L] = inputs["ffn_w_up"][L]
        shared["ffn_d%d" % L] = inputs["ffn_w_down"][L]
    shared.update(hc)
    shared = {k: np.ascontiguousarray(np.asarray(v_)) for k, v_ in shared.items()}
    maps = []
    x = np.asarray(inputs["x"])
    pos = np.asarray(inputs["positions"])
    for c in range(ncores):
        m = dict(shared)
        m["x"] = np.ascontiguousarray(x[c * NSEQ:(c + 1) * NSEQ].reshape(TOK, D))
        m["pos"] = np.ascontiguousarray(pos[c * NSEQ:(c + 1) * NSEQ].reshape(1, TOK).astype(np.int32))
        maps.append(m)
    return maps


def build_layers(K, C, I, W, out, x1d, stage, dbg, dbgout):
    nseq = NSEQ
    if stage in ("A", "B", "C", "D0"):
        nseq = 1
    gm = C.gm
    if stage == "L1a":
        C.g_mix_pre = [None, None]; C.g_mix_post = [None, None]; C.g_ffn_pre = [None, None]; C.g_ffn_post = [None, None]
        with K.phase():
            K.dma(x1d[0:1024, :], I["x"][0:1024, :])
        build_layer1(K, C, I, W, out, x1d, stage, dbg, dbgout)
        return
    with K.phase():
        C.g_mix_pre = [None, None]; C.g_mix_post = [None, None]; C.g_ffn_pre = [None, None]; C.g_ffn_post = [None, None]
        C.g_mix_pre[0] = load_bc(K, "g_mp0", I["norm_mix_pre"][0:1, :], D)
        C.g_mix_post[0] = load_bc(K, "g_mo0", I["norm_mix_post"][0:1, :], D)
        C.g_ffn_pre[0] = load_bc(K, "g_fp0", I["norm_ffn_pre"][0:1, :], D)
        C.g_ffn_post[0] = load_bc(K, "g_fo0", I["norm_ffn_post"][0:1, :], D)
        C.scr_j = K.sb("scr_j", [128, D], BF16)
        C.scr_f = K.sb("scr_f", [128, D], F32)
        C.scr_g = K.sb("scr_g", [128, D], F32)
        C.hn = K.sb("hn", [128, 4, D], BF16)
        B1 = K.sb("B1", [128, 4, SEQ], BF16)
        B2 = K.sb("B2", [128, 4, SEQ], BF16)
        B3 = K.sb("B3", [128, 8192], BF16)
        UT = K.sb("UT", [128, 32, 256], BF16)
        QT = B1; attT = B1; KT = B2; ssmT = B2
        V = B3[:].re("p (s f) -> p s f", f=512)
        ygT = B3[:].re("p (k t) -> p k t", k=4)
        for s in range(nseq):
            tok0 = s * SEQ
            with K.phase():
                xt = K.sb("xtA", [128, 4, D], F32)
                xnT = K.sb("xnT", [128, 8, 1024], BF16)
                wch = [K.sb("wch%d" % i, [128, 8, 512], BF16) for i in range(2)]
                posi = K.sb("posi", [128, 512], I32)
                ang = K.sb("angA", [128, 512], F32)
                cosT = K.sb("cosT", [128, 1024], F32)
                sinT = K.sb("sinT", [128, 1024], F32)
                sct = (K.sb("sct", [128, 512], F32), K.sb("scti", [128, 512], I32), K.sb("sckf", [128, 512], F32), K.sb("scr", [128, 512], F32))
                tA = K.sb("tA", [128, 512], F32)
                tB = K.sb("tB", [128, 512], F32)
                UA = K.sb("UA", [128, 32, 8, 16], BF16)
                C.pst = [K.ps("pst%d" % i, [128, 1024], BF16) for i in range(2)]
                psm = [K.ps("psm%d" % i, [128, 512], F32) for i in range(6)]
                pmi = 0
                wv = W["w_in0"][:].re("(kc p) n -> p kc n", p=128)
                for blk in range(2):
                    b0 = tok0 + blk * 1024
                    for half in range(2):
                        K.dma(xt[:], I["x"][b0 + half * 512:b0 + (half + 1) * 512, :].re("(s p) d -> p s d", p=128))
                        rms_pre(K, C, xt[:], 4, C.g_mix_pre[0], C.hn[:], "a")
                        transpose_to(K, C, C.hn[:], 4, xnT, half * 512)
                        K.dma(posi[:], I["pos"][0:1, b0 + half * 512:b0 + (half + 1) * 512].pb(128))
                        K.copy(ang[:], posi[:])
                        K.ts(ang[:], ang[:], C.ropec[:, 0:1], ALU.mult)
                        hsl = slice(half * 512, (half + 1) * 512)
                        sincos(K, C, ang[:], [128, 512], sinT[:, hsl], cosT[:, hsl], sct)
                        K.ts(sinT[:, hsl], sinT[:, hsl], C.ropec[:, 1:2], ALU.mult)
                    for qk in range(2):
                        dst = QT if qk == 0 else KT
                        wq, wsw = wch[0], wch[1]
                        K.dma(wq[:], wv[:, :, (2 * qk) * 512:(2 * qk + 1) * 512])
                        K.dma(wsw[:], wv[:, :, (2 * qk + 1) * 512:(2 * qk + 2) * 512])
                        for t in range(4):
                            for nh in range(2):
                                pq = psm[pmi % 6]; psw = psm[(pmi + 1) % 6]; pmi += 2
                                for kc in range(8):
                                    K.mm(pq[:], wq[:, kc, t * 128:(t + 1) * 128], xnT.k(kc)[:, kc, nh * 512:(nh + 1) * 512], start=(kc == 0), stop=(kc == 7))
                                for kc in range(8):
                                    K.mm(psw[:], wsw[:, kc, t * 128:(t + 1) * 128], xnT.k(kc)[:, kc, nh * 512:(nh + 1) * 512], start=(kc == 0), stop=(kc == 7))
                                K.tt(tA[:], pq[:], cosT[:, nh * 512:(nh + 1) * 512], ALU.mult)
                                K.tt(tB[:], psw[:], sinT[:, nh * 512:(nh + 1) * 512], ALU.mult)
                                c0 = blk * 1024 + nh * 512
                                K.tt(dst[:, t, c0:c0 + 512], tA[:], tB[:], ALU.add, eng="pool")
                    wvv = wch[0]
                    K.dma(wvv[:], wv[:, :, 4 * 512:5 * 512])
                    for sub in range(8):
                        pv = psm[pmi % 6]; pmi += 1
                        for kc in range(8):
                            K.mm(pv[:], xnT.k(kc)[:, kc, sub * 128:(sub + 1) * 128], wvv[:, kc, :], start=(kc == 0), stop=(kc == 7))
                        K.copy(V[:, blk * 8 + sub, :], pv[:], eng="act")
                    wu_ = wch[1]
                    K.dma(wu_[:], wv[:, :, 5 * 512:6 * 512])
                    for j in range(8):
                        pu = psm[pmi % 6]; pmi += 1
                        for kc in range(8):
                            K.mm(pu[:], xnT.k(kc)[:, kc, :].re("p (c j) -> p j c", j=8)[:, j, :], wu_[:, kc, :], start=(kc == 0), stop=(kc == 7))
                        K.copy(UA[:, :, j, :], pu[:].re("p (g c) -> p g c", c=16), eng="act")
                    for g4 in range(8):
                        pt = C.pst[g4 % 2]
                        for gi in range(4):
                            g = g4 * 4 + gi
                            K.tr(pt[:, gi * 128:(gi + 1) * 128], UA[:, g, :, :].re("p j c -> p (j c)"), C.identb[:])
                        K.copy(UT[:, g4 * 4:(g4 + 1) * 4, blk * 128:(blk + 1) * 128], pt[:, 0:512].re("p (g c) -> p g c", g=4), eng=("act" if g4 % 2 == 0 else "dve"))
            if "A" in dbg and s == 0:
                for nm, t_, sh in (("QT", QT[:], [128, 4, SEQ]), ("KT", KT[:], [128, 4, SEQ]), ("V", V, [128, 16, 512]), ("UT", UT[:], [128, 32, 256])):
                    with K.phase():
                        o = dbgout(nm, sh)
                        tf = K.sb("dbgf" + nm, sh, F32)
                        K.copy(tf[:], t_)
                        K.dma(o[:], tf[:])
            if stage == "A":
                raise Stop()
            with K.phase():
                pS = [K.ps("pS%d" % i, [128, 512], F32) for i in range(4)]
                pnd = [[K.ps("pnd%d_%d" % (g_, h_), [128, 512], F32) for h_ in range(2)] for g_ in range(2)]
                Dlo = K.sb("Dlo", [128, 512], F32)
                Pb = [K.sb("Pb%d" % i, [128, 512], BF16) for i in range(4)]
                Pm = [K.sb("Pm%d" % i, [128, 512], BF16) for i in range(4)]
                rden = K.sb("rden", [128, 512], F32)
                Dsb = K.sb("Dsb", [128, 512], F32)
                Vx = K.sb("Vx", [128, 16, 8, 128], BF16)
                K.memset(Vx[:], 1.0)
                for h_ in range(8):
                    cs_ = slice(0, 64) if h_ % 2 == 0 else slice(64, 128)
                    K.copy(Vx[:, :, h_, cs_], V[:, :, h_ * 64:(h_ + 1) * 64], eng=("act" if h_ % 2 == 0 else "dve"))
                jobs = list(C.lazy.get(s, []))
                njobs = len(jobs)
                if njobs:
                    lzb = [K.sb("lzb%d" % i, [128, 2816], BF16) for i in range(4)]
                lzi = 0

                def emit_job():
                    nonlocal lzi
                    dst_, src_, r0_, n_, c_ = jobs.pop(0)
                    b_ = lzb[lzi % 4]
                    lzi += 1
                    K.dma(b_[0:n_, 0:c_], src_[r0_:r0_ + n_, :], eng="pool")
                    K.dma(dst_.k(r0_)[r0_:r0_ + n_, :], b_[0:n_, 0:c_], eng="sp")
                LA = 3
                items = []
                for t in range(4):
                    for qc in range(4):
                        nkb = 4 * qc + 4
                        for kb in range(nkb):
                            for hh in range(2):
                                items.append((t, qc, kb, hh, nkb))
                n_it = len(items)
                deferred = []

                def fin(t, qc, hh, nd):
                    hs = slice(hh * 64, (hh + 1) * 64)
                    K.act(Dlo.k(hh)[hs, :], Dlo.k(hh)[hs, :], AF.Ln)
                    K.act(rden.k(hh)[hs, :], Dlo.k(hh)[hs, :], AF.Exp, scale=-1.0)
                    K.tt(attT.k((t, qc))[hs, t, qc * 512:(qc + 1) * 512], nd[hs, :], rden.k(hh)[hs, :], ALU.mult)

                for it_ in range(n_it + LA):
                    while deferred and deferred[0][0] <= it_:
                        fin(*deferred.pop(0)[1])
                    if it_ < n_it:
                        t, qc, kb, hh, nkb = items[it_]
                        hs = slice(hh * 64, (hh + 1) * 64)
                        delta = 4 * qc - kb
                        ps_ = pS[it_ % 4]
                        K.mm(ps_[:], KT[hs, t, kb * 128:(kb + 1) * 128], QT.k((t, qc))[hs, t, qc * 512:(qc + 1) * 512])
                        pb_ = Pb[it_ % 4]; pm_ = Pm[it_ % 4]
                        K.act(pb_[:], ps_[:], AF.Exp, scale=0.125)
                        K.tt(pm_[:], pb_[:], gm[:, 128 * (delta + 3):128 * (delta + 3) + 512], ALU.mult, eng=("dve" if (njobs or it_ % 3 != 2) else "pool"))
                        if jobs and it_ % 8 == 4:
                            emit_job()
                    j_ = it_ - LA
                    if j_ >= 0:
                        t, qc, kb, hh, nkb = items[j_]
                        hs = slice(hh * 64, (hh + 1) * 64)
                        grp = t * 4 + qc
                        nd = pnd[grp % 2][hh]
                        pm_ = Pm[j_ % 4]
                        K.mm(nd[:], Vx[:, kb, 2 * t + hh, :], pm_[:], start=(kb == 0), stop=(kb == nkb - 1))
                        if kb == nkb - 1:
                            dsl = slice(64, 128) if hh == 0 else slice(0, 64)
                            K.copy(Dsb.k(hh)[dsl, :], nd[dsl, :], eng="act")
                            K.dma(Dlo.k(hh)[hs, :], Dsb.k(hh)[dsl, :], eng="sp")
                            deferred.append((it_ + 5, (t, qc, hh, nd)))
                while deferred:
                    fin(*deferred.pop(0)[1])
                while jobs:
                    emit_job()
            if "B" in dbg and s == 0:
                with K.phase():
                    o = dbgout("attT", [128, 4, SEQ]); tf = K.sb("dbgfa", [128, 4, SEQ], F32)
                    K.copy(tf[:], attT[:]); K.dma(o[:], tf[:])
            if stage == "B":
                raise Stop()
            with K.phase():
                EX = K.sb("EX", [128, 2, 16, 257], F32)
                EXb = K.sb("EXb", [128, 2, 16, 257], BF16)
                MU = K.sb("MUc", [128, 3, 16], F32)
                K.dma(MU[:], C.dMU[:])
                wglu = K.sb("wglu", [128, 4, 512], BF16)
                K.dma(wglu[:], W["s5_w_glu"][:].re("(kc p) n -> p kc n", p=128))
                ptr = [K.ps("ptr%d" % i, [128, 1024], BF16) for i in range(2)]
                pe_ = [K.ps("pe%d" % i, [128, 512], F32) for i in range(4)]
                tr1 = K.sb("tr1", [128, 2, 16], F32)
                tr2 = K.sb("tr2", [128, 2, 16], F32)
                with K.phase():
                    Bs = K.sb("BsC", [128, 32, 128], BF16)
                    K.dma(Bs[:], C.dBs[:])
                    K.memset(EX[:, :, :, 0:1], 0.0)
                    for pr in range(16):
                        for ri in range(2):
                            pp = pe_[(pr * 2 + ri) % 4]
                            for h in range(2):
                                g = 2 * pr + h
                                K.mm(pp[h * 64:(h + 1) * 64, 0:256], Bs[:, g, ri * 64:(ri + 1) * 64], UT[:, g, :])
                            K.copy(EX[:, ri, pr, 1:257], pp[:, 0:256], eng=("act" if ri == 0 else "dve"))
                chk(20)
                MUP = K.sb("MUPc", [128, 3, 16, 16], F32)
                K.dma(MUP[:], C.dMUP[:])
                XV = EX[:, :, :, 1:257].re("p r q (b i) -> p r q b i", i=16)
                T1 = K.sb("T1", [128, 2, 16, 16], F32)
                T2 = K.sb("T2", [128, 2, 16, 16], F32)

                def bc4(ref, nb):
                    return Ref(ref.tile, ref.key, ref.ap.unsqueeze(1).unsqueeze(3).to_broadcast([128, 2, 16, nb]))

                def bc3(ref, nb):
                    return Ref(ref.tile, ref.key, ref.ap.unsqueeze(2).to_broadcast([128, 16, nb]))

                def cmul_add(dst, src, k, nb):
                    K.tt(T1[:, :, :, 0:nb], src, bc4(MUP[:, 0, k, :], nb), ALU.mult)
                    K.tt(T2[:, 0, :, 0:nb], src[:, 1], bc3(MUP[:, 2, k, :], nb), ALU.mult)
                    K.tt(T2[:, 1, :, 0:nb], src[:, 0], bc3(MUP[:, 1, k, :], nb), ALU.mult)
                    K.tt(T1[:, :, :, 0:nb], T1[:, :, :, 0:nb], T2[:, :, :, 0:nb], ALU.add)
                    K.tt(dst, dst, T1[:, :, :, 0:nb], ALU.add)

                for i in range(1, 16):
                    cmul_add(XV[:, :, :, :, i], XV[:, :, :, :, i - 1], 0, 16)
                for bb in range(1, 16):
                    cmul_add(XV[:, :, :, bb:bb + 1, 15], XV[:, :, :, bb - 1:bb, 15], 15, 1)
                for i in range(15):
                    cmul_add(XV[:, :, :, 1:16, i], XV[:, :, :, 0:15, 15], i, 15)
                chk(21)
                K.copy(EXb[:], EX[:])
                if "C1" in dbg and s == 0:
                    o = dbgout("EX", [128, 2, 16, 257])
                    K.dma(o[:], EX[:])
                with K.phase():
                    Tz = K.sb("TzC", [128, 32, 128], BF16)
                    CsRe = K.sb("CsReC", [128, 16, 128], BF16)
                    CsIm = K.sb("CsImC", [128, 16, 128], BF16)
                    K.dma(Tz[:], C.dTz[:]); K.dma(CsRe[:], C.dCsRe[:]); K.dma(CsIm[:], C.dCsIm[:])
                    Yg = [K.sb("Yg%d" % i, [128, 128], BF16) for i in range(2)]
                    YGb = K.sb("YGb", [128, 8, 512], BF16)
                    sg_ = [K.sb("sgC%d" % i, [128, 512], BF16) for i in range(2)]
                    if "C1" in dbg and s == 0:
                        oYA = dbgout("YA", [2, 128, 8, 512])
                        YAf = K.sb("YAf", [128, 8, 512], F32)
                    for blk in range(2):
                        cs_ = slice(blk * 128, (blk + 1) * 128)
                        for g4 in range(8):
                            pt = ptr[g4 % 2]
                            for gi in range(4):
                                g = g4 * 4 + gi
                                pr, h = g // 2, g % 2
                                hs = slice(h * 64, (h + 1) * 64)
                                py = pe_[g % 4]
                                K.mm(py[:, 0:128], Tz[:, g, :], UT[:, g, cs_], start=True, stop=False)
                                K.mm(py[:, 0:128], CsRe[hs, pr, :], EXb[hs, 0, pr, blk * 128:blk * 128 + 128], start=False, stop=False)
                                K.mm(py[:, 0:128], CsIm[hs, pr, :], EXb[hs, 1, pr, blk * 128:blk * 128 + 128], start=False, stop=True)
                                yg_ = Yg[g % 2]
                                K.copy(yg_[:], py[:, 0:128], eng="act")
                                K.tr(pt[:, gi * 128:(gi + 1) * 128], yg_[:], C.identb[:])
                            dstv = YGb[:, :, g4 * 64:(g4 + 1) * 64].re("p i (g c) -> p g i c", g=4)
                            srcv = pt[:, 0:512].re("p (g i c) -> p g i c", g=4, i=8)
                            if "C1" in dbg and s == 0:
                                K.copy(YAf[:, :, g4 * 64:(g4 + 1) * 64].re("p i (g c) -> p g i c", g=4), srcv)
                            K.act(dstv, srcv, AF.Gelu_apprx_tanh)
                        if "C1" in dbg and s == 0:
                            K.dma(oYA[blk], YAf[:])
                        for fc in range(4):
                            pt = ptr[fc % 2]
                            for i in range(8):
                                K.tr(pt[:, i * 128:(i + 1) * 128], YGb[:, i, fc * 128:(fc + 1) * 128], C.identb[:])
                            K.copy(ygT[:, fc, blk * 1024:(blk + 1) * 1024].re("p (c i) -> p i c", i=8), pt[:].re("p (i c) -> p i c", i=8),
                                   eng=("act" if fc % 2 == 0 else "dve"))
                    chk(22)
                    for mt in range(4):
                        for nq in range(4):
                            pg = pe_[(mt * 4 + nq) % 4]
                            for kc in range(4):
                                K.mm(pg[:], wglu[:, kc, mt * 128:(mt + 1) * 128], ygT[:, kc, nq * 512:(nq + 1) * 512], start=(kc == 0), stop=(kc == 3))
                            sg = sg_[(mt * 4 + nq) % 2]
                            K.act(sg[:], pg[:], AF.Sigmoid)
                            K.tt(ssmT[:, mt, nq * 512:(nq + 1) * 512], sg[:], ygT[:, mt, nq * 512:(nq + 1) * 512], ALU.mult)
            chk(23)
            if "C" in dbg and s == 0:
                with K.phase():
                    o = dbgout("ssmT", [128, 4, SEQ]); tf = K.sb("dbgfs", [128, 4, SEQ], F32)
                    K.copy(tf[:], ssmT[:]); K.dma(o[:], tf[:])
            if stage == "C":
                raise Stop()
            with K.phase():
                xts = [K.sb("xtD%d" % i, [128, 4, D], F32) for i in range(2)]
                C.hnT = K.sb("hnT", [128, 8, 512], BF16)
                C.actT = K.sb("actT", [128, NFT, 512], BF16)
                C.wgb = [K.sb("wgb%d" % i, [128, 8, 256], BF16) for i in range(3)]
                C.wub = [K.sb("wub%d" % i, [128, 8, 256], BF16) for i in range(3)]
                C.sgb = [K.sb("sgb%d" % i, [128, 512], F32) for i in range(2)]
                C.rotb = [K.sb("rotb%d" % i, [128, D], BF16) for i in range(6)]
                alloc_psum(K, C, "pD")
                wov = W["ev_w_out"][:].re("(kc p) n -> p kc n", p=128)
                K.dma(xts[0][:], I["x"][tok0:tok0 + 512, :].re("(s p) d -> p s d", p=128), eng="pool")
                for tl in range(4):
                    t0 = tok0 + tl * 512
                    xt = xts[tl % 2]
                    if tl + 1 < 4:
                        K.dma(xts[(tl + 1) % 2][:], I["x"][t0 + 512:t0 + 1024, :].re("(s p) d -> p s d", p=128), eng="pool")
                    proj_tokmajor(K, C, lambda kc, sub: (attT if kc < 4 else ssmT)[:, kc % 4, tl * 512 + sub * 128:tl * 512 + (sub + 1) * 128],
                                  8, wov, C.g_mix_post[0], xt, "mp")
                    if "D0" in dbg and s == 0 and tl == 0:
                        o = dbgout("xa0", [128, 4, D])
                        K.dma(o[:], xt[:])
                    ffn_tile(K, C, xt, 0, W)
                    K.dma(x1d[t0:t0 + 512, :].re("(s p) d -> p s d", p=128), xt[:], eng="pool")
                    if "D0" in dbg and s == 0 and tl == 0:
                        o = dbgout("x1", [128, 4, D])
                        K.dma(o[:], xt[:])
                    if stage == "D0":
                        raise Stop()
    build_layer1(K, C, I, W, out, x1d, stage, dbg, dbgout)


def build_layer1(K, C, I, W, out, x1d, stage, dbg, dbgout):
    ntl = 16
    if stage == "L1a":
        ntl = 2
    with K.phase():
        C.g_mix_pre[1] = load_bc(K, "g_mp1", I["norm_mix_pre"][1:2, :], D)
        C.g_mix_post[1] = load_bc(K, "g_mo1", I["norm_mix_post"][1:2, :], D)
        C.g_ffn_pre[1] = load_bc(K, "g_fp1", I["norm_ffn_pre"][1:2, :], D)
        C.g_ffn_post[1] = load_bc(K, "g_fo1", I["norm_ffn_post"][1:2, :], D)
        C.scr_j = K.sb("scr_j1", [128, D], BF16)
        C.scr_f = K.sb("scr_f1", [128, D], F32)
        C.scr_g = K.sb("scr_g1", [128, D], F32)
        C.hn = K.sb("hn1", [128, 4, D], BF16)
        xts = [K.sb("xt1_%d" % i, [128, 4, D], F32) for i in range(2)]
        C.hnT = K.sb("hnT1", [128, 8, 512], BF16)
        C.actT = K.sb("actT1", [128, NFT, 512], BF16)
        C.wgb = [K.sb("wgb1_%d" % i, [128, 8, 256], BF16) for i in range(2)]
        C.wub = [K.sb("wub1_%d" % i, [128, 8, 256], BF16) for i in range(2)]
        C.sgb = [K.sb("sgb1_%d" % i, [128, 512], F32) for i in range(2)]
        C.rotb = [K.sb("rotb1_%d" % i, [128, D], BF16) for i in range(6)]
        alloc_psum(K, C, "pL")
        xbuf = K.sb("xbuf", [128, 8, 515], F32)
        xc = K.sb("xc", [128, 8, 512], F32)
        yT = C.hnT
        xcb = C.actT[:, 0:8, :]
        gz = C.actT[:, 8:16, :]
        GRb = K.sb("GRb", [128, 4, 512], BF16); GIb = K.sb("GIb", [128, 4, 512], BF16)
        A4 = K.sb("A4", [128, 4, 512], F32); M4 = K.sb("M4", [128, 4, 512], F32)
        hbuf = [K.sb("h_%d" % i, [128, 512], F32) for i in range(2)]
        hc = K.sb("hc", [128, 8], F32)
        cw = K.sb("cw", [128, 8, 4], F32)
        cb = K.sb("cb", [128, 8], F32); br = K.sb("br", [128, 8], F32); bi = K.sb("bi", [128, 8], F32)
        cch = K.sb("cch", [128, 8], F32)
        Wr = K.sb("Wr", [128, 8, 256], BF16); Wi = K.sb("Wi", [128, 8, 256], BF16)
        for j in range(4):
            K.dma(cw[:, :, j], I["rg_conv_w"][j:j + 1, :].re("o (ct p) -> p (o ct)", p=128), allow_slow_non_contiguous=True)
        for t_, nm in ((cb, "rg_conv_b"), (br, "rg_b_r"), (bi, "rg_b_i"), (cch, "rg_lam")):
            K.dma(t_[:], I[nm][0:1, :].re("o (ct p) -> p (o ct)", p=128), allow_slow_non_contiguous=True)
        K.act(cch[:], cch[:], AF.Exp, scale=-1.0)
        K.act(cch[:], cch[:], AF.Ln, bias=C.onec[:, 0:1])
        K.ts(cch[:], cch[:], -8.0, ALU.mult)
        cch2 = K.sb("cch2", [128, 8], F32)
        K.ts(cch2[:], cch[:], 2.0, ALU.mult)
        K.dma(Wr[:], W["rg_w_r"][:].re("(q p) n -> p q n", p=128))
        K.dma(Wi[:], W["rg_w_i"][:].re("(q p) n -> p q n", p=128))
        wiv = W["od_w_in"][:].re("(kc p) n -> p kc n", p=128)
        wov = W["od_w_out"][:].re("(kc p) n -> p kc n", p=128)
        K.dma(xts[0][:], x1d[0:512, :].re("(s p) d -> p s d", p=128), eng="pool")
        for tl in range(ntl):
            t0 = tl * 512
            first = (tl % 4 == 0)
            xt = xts[tl % 2]
            if tl + 1 < ntl:
                K.dma(xts[(tl + 1) % 2][:], x1d[t0 + 512:t0 + 1024, :].re("(s p) d -> p s d", p=128), eng="pool")
            rms_pre(K, C, xt[:], 4, C.g_mix_pre[1], C.hn[:], "l")
            transpose_to(K, C, C.hn[:], 4, C.hnT, 0)
            if first:
                K.memset(xbuf[:, :, 0:3], 0.0)
            else:
                K.copy(xbuf[:, :, 0:3], xbuf[:, :, 512:515])
            for ch in range(8):
                wb = C.wgb[ch % 2] if ch % 4 < 2 else C.wub[ch % 2]
                K.dma(wb[:], wiv[:, :, ch * 256:(ch + 1) * 256], eng="sp")
                for mi in range(2):
                    mt = ch * 2 + mi
                    pz = C.psm[C.pmi % 6]; C.pmi += 1
                    for kc in range(8):
                        K.mm(pz[:], wb[:, kc, mi * 128:(mi + 1) * 128], C.hnT.k(kc)[:, kc, :], start=(kc == 0), stop=(kc == 7))
                    if mt < 8:
                        ct = mt
                        K.copy(xbuf.k(ct)[:, ct, 3:515], pz[:], eng="act")
                        K.ts(xc.k(ct)[:, ct, :], xbuf.k(ct)[:, ct, 0:512], cw[:, ct, 0:1], ALU.mult, cb[:, ct:ct + 1], ALU.add)
                        for j in range(1, 4):
                            K.stt(xc.k(ct)[:, ct, :], xbuf.k(ct)[:, ct, j:j + 512], cw[:, ct, j:j + 1], xc.k(ct)[:, ct, :], ALU.mult, ALU.add)
                        K.copy(C.actT.k(("x", ct))[:, ct, :], xc.k(ct)[:, ct, :], eng="pool")
                    else:
                        K.act(C.actT.k(("g", mt - 8))[:, mt, :], pz[:], AF.Gelu_apprx_tanh)
            for half in range(2):
                cts = list(range(half * 4, half * 4 + 4))
                for j, ct in enumerate(cts):
                    hq = (ct // 2) * 2
                    cs_ = slice((ct % 2) * 128, (ct % 2 + 1) * 128)
                    pr_ = C.psm[C.pmi % 6]; pi_ = C.psm[(C.pmi + 1) % 6]; C.pmi += 2
                    for kc in range(2):
                        K.mm(pr_[:], Wr[:, hq + kc, cs_], C.actT.k(("x", hq + kc))[:, hq + kc, :], start=(kc == 0), stop=(kc == 1))
                    for kc in range(2):
                        K.mm(pi_[:], Wi[:, hq + kc, cs_], C.actT.k(("x", hq + kc))[:, hq + kc, :], start=(kc == 0), stop=(kc == 1))
                    K.act(GRb.k(j)[:, j, :], pr_[:], AF.Sigmoid, bias=br[:, ct:ct + 1])
                    K.act(GIb.k(j)[:, j, :], pi_[:], AF.Sigmoid, bias=bi[:, ct:ct + 1])
                    K.tt(xc.k(ct)[:, ct, :], xc.k(ct)[:, ct, :], GIb.k(j)[:, j, :], ALU.mult, eng="pool")
                for j, ct in enumerate(cts):
                    K.act(A4.k(j)[:, j, :], GRb.k(j)[:, j, :], AF.Exp, scale=cch[:, ct:ct + 1])
                    K.act(M4.k(j)[:, j, :], GRb.k(j)[:, j, :], AF.Exp, scale=cch2[:, ct:ct + 1])
                for j, ct in enumerate(cts):
                    K.act(M4.k(j)[:, j, :], M4.k(j)[:, j, :], AF.Sqrt, scale=-1.0, bias=C.onec[:, 0:1])
                for j, ct in enumerate(cts):
                    K.tt(M4.k(j)[:, j, :], M4.k(j)[:, j, :], xc.k(ct)[:, ct, :], ALU.mult)
                    h_ = hbuf[ct % 2]
                    if first:
                        K.scan(h_[:], A4.k(j)[:, j, :], M4.k(j)[:, j, :], 0.0)
                    else:
                        K.scan(h_[:], A4.k(j)[:, j, :], M4.k(j)[:, j, :], hc.k(ct)[:, ct:ct + 1])
                    K.copy(hc.k(ct)[:, ct:ct + 1], h_[:, 511:512], eng="pool")
                    K.tt(yT.k(ct)[:, ct, :], h_[:], C.actT.k(("g", ct))[:, 8 + ct, :], ALU.mult)
                    if "L1" in dbg and tl < 2:
                        if ct == 0 and tl == 0:
                            C.oh = dbgout("h1", [2, 8, 128, 512])
                        K.dma(C.oh[tl, ct], h_[:])
            proj_tokmajor(K, C, lambda kc, sub: yT[:, kc, sub * 128:(sub + 1) * 128], 8, wov, C.g_mix_post[1], xt, "mq")
            if "L1" in dbg and tl < 2:
                if tl == 0:
                    C.oxa = dbgout("xa1", [2, 128, 4, D])
                K.dma(C.oxa[tl], xt[:])
            ffn_tile(K, C, xt, 1, W)
            K.dma(out[t0:t0 + 512, :].re("(s p) d -> p s d", p=128), xt[:], eng="pool")


def kernel(**inputs):
    nc, _ = build("full")
    maps = make_in_maps(inputs)
    res = run_bass_kernel_spmd(nc, maps, core_ids=list(range(NCORES)))
    outs = [np.asarray(r["out"], dtype=np.float32).reshape(NSEQ, SEQ, D) for r in res.results]
    return np.concatenate(outs, axis=0)
```

```python
import numpy as np
import ml_dtypes
import concourse.bass as bass
import concourse.mybir as mybir
from concourse.bass_utils import run_bass_kernel_spmd

F32 = mybir.dt.float32
BF16 = mybir.dt.bfloat16
I32 = mybir.dt.int32
AF = mybir.ActivationFunctionType
ALU = mybir.AluOpType
AX = mybir.AxisListType

SEM_LIMIT = 24000


class Ref:
    __slots__ = ("tile", "key", "ap")

    def __init__(self, tile, key, ap):
        self.tile = tile
        self.key = key
        self.ap = ap

    def __getitem__(self, idx):
        return Ref(self.tile, self.key, self.ap[idx])

    def re(self, pat, **kw):
        return Ref(self.tile, self.key, self.ap.rearrange(pat, **kw))


class _Keyed:
    def __init__(self, tile, key):
        self.tile = tile
        self.key = key

    def __getitem__(self, idx):
        return Ref(self.tile, self.key, self.tile.h[idx])


class Tile:
    def __init__(self, h, name):
        self.h = h
        self.name = name
        self.reg = {}

    def __getitem__(self, idx):
        return Ref(self, "*", self.h[idx])

    def k(self, key):
        return _Keyed(self, key)


def _merge(dst, src):
    for s, v in src.items():
        if dst.get(s, 0) < v:
            dst[s] = v


class Kern:
    ENG = ("pe", "act", "dve", "pool", "sp")

    def __init__(self, nc, sync_same=("act", "dve", "pool")):
        self.nc = nc
        self.stack = []
        self.prog = {e: [] for e in self.ENG}
        self.free_sems = []
        self.nsem = 0
        self.csem = {}
        self.ccnt = {}
        self.waited = {e: {} for e in self.ENG}
        self.dsems = {e: [] for e in self.ENG}
        self.drr = {e: 0 for e in self.ENG}
        self.dcnt = {}
        self.sync_same = set(sync_same)
        self.semname = {}
        for e in self.ENG:
            self._new_csem(e)
        self.ndma_sems = {"sp": 6, "act": 3, "pool": 4, "dve": 0, "pe": 0}
        for e in self.ENG:
            for _ in range(self.ndma_sems[e]):
                self.dsems[e].append(self._alloc_sem())

    def _alloc_sem(self):
        cm = self.nc.semaphore("s%d" % self.nsem)
        self.nsem += 1
        h = cm.__enter__()
        self.stack.append(cm)
        self.dcnt[h] = 0
        return h

    def _new_csem(self, e):
        self.csem[e] = self._alloc_sem()
        self.ccnt[e] = 0

    def sb(self, name, shape, dt):
        self.uid = getattr(self, "uid", 0) + 1
        name = "sb%d_%s" % (self.uid, name)
        cm = self.nc.sbuf_tensor(name, list(shape), dt)
        h = cm.__enter__()
        self.stack.append(cm)
        return Tile(h, name)

    def ps(self, name, shape, dt=F32):
        self.uid = getattr(self, "uid", 0) + 1
        name = "ps%d_%s" % (self.uid, name)
        nbytes = int(np.prod(shape[1:])) * (4 if dt == F32 else 2)
        assert nbytes == 2048, "PSUM tiles must be exactly one bank"
        cm = self.nc.psum_tensor(name, list(shape), dt)
        h = cm.__enter__()
        self.stack.append(cm)
        t = Tile(h, name)
        t.psum = True
        return t

    def dram(self, name, shape, dt, kind="Internal"):
        t = self.nc.dram_tensor(name, list(shape), dt, kind=kind)
        return Tile(t.ap(), name)

    def _conf(self, ref):
        t = ref.tile
        if ref.key == "*":
            return list(t.reg.values())
        out = []
        if "*" in t.reg:
            out.append(t.reg["*"])
        if ref.key in t.reg:
            out.append(t.reg[ref.key])
        return out

    def emit(self, eng, fn, reads=(), writes=(), dma=False):
        pr = [r for r in reads if getattr(r.tile, "psum", False)]
        if pr:
            reads = [r for r in reads if not getattr(r.tile, "psum", False)]
            writes = list(writes) + pr
        need = {}
        for r in reads:
            for w, _ in self._conf(r):
                _merge(need, w)
        for wr in writes:
            for w, rd in self._conf(wr):
                _merge(need, w)
                _merge(need, rd)
        waits = []
        wd = self.waited[eng]
        own = self.csem[eng]
        for s, v in need.items():
            if wd.get(s, 0) >= v:
                continue
            if (not dma) and s is own and eng not in self.sync_same:
                continue
            wd[s] = v
            waits.append((s, v))
        if dma:
            lst = self.dsems[eng]
            i = self.drr[eng] % len(lst)
            self.drr[eng] += 1
            s = lst[i]
            if self.dcnt[s] + 16 > SEM_LIMIT:
                s = self._alloc_sem()
                lst[i] = s
            self.dcnt[s] += 16
            ev = (s, self.dcnt[s])
            inc = 16
        else:
            if self.ccnt[eng] + 1 > SEM_LIMIT:
                self._new_csem(eng)
            self.ccnt[eng] += 1
            ev = (self.csem[eng], self.ccnt[eng])
            inc = 1
        self.prog[eng].append((waits, fn, ev[0], inc))
        evd = {ev[0]: ev[1]}
        for r in reads:
            reg = r.tile.reg.setdefault(r.key, [{}, {}])
            _merge(reg[1], evd)
        for wr in writes:
            if wr.key == "*":
                wr.tile.reg = {"*": [dict(evd), {}]}
            else:
                wr.tile.reg[wr.key] = [dict(evd), {}]
        return ev

    def wait_all(self, eng, refs):
        need = {}
        for r in refs:
            for w, rd in self._conf(r):
                _merge(need, w)
        waits = [(s, v) for s, v in need.items()]
        self.prog[eng].append((waits, None, None, 0))

    def dma(self, out, in_, eng="sp", **kw):
        return self.emit(eng, lambda e: e.dma_start(out=out.ap, in_=in_.ap, **kw),
                         reads=[in_], writes=[out], dma=True)

    def mm(self, out, lhsT, rhs, start=True, stop=True, **kw):
        return self.emit("pe", lambda e: e.matmul(out.ap, lhsT.ap, rhs.ap, start=start, stop=stop, **kw),
                         reads=[lhsT, rhs], writes=[out])

    def tr(self, out, in_, ident):
        return self.emit("pe", lambda e: e.transpose(out.ap, in_.ap, ident.ap),
                         reads=[in_, ident], writes=[out])

    def act(self, out, in_, func, bias=None, scale=None, accum=None, eng="act", extra_reads=()):
        kw = {}
        reads = [in_] + list(extra_reads)
        writes = [out]
        if bias is not None:
            if isinstance(bias, Ref):
                kw["bias"] = bias.ap
                reads.append(bias)
            else:
                kw["bias"] = bias
        if scale is not None:
            if isinstance(scale, Ref):
                kw["scale"] = scale.ap
                reads.append(scale)
            else:
                kw["scale"] = scale
        if accum is not None:
            kw["accum_out"] = accum.ap
            writes.append(accum)
        return self.emit(eng, lambda e: e.activation(out.ap, in_.ap, func, **kw), reads=reads, writes=writes)

    def tt(self, out, a, b, op, eng="dve"):
        return self.emit(eng, lambda e: e.tensor_tensor(out.ap, a.ap, b.ap, op), reads=[a, b], writes=[out])

    def ts(self, out, a, s1, op0, s2=None, op1=None, accum=None, eng="dve"):
        reads = [a]
        writes = [out]
        v1 = s1
        v2 = s2
        if isinstance(s1, Ref):
            reads.append(s1)
            v1 = s1.ap
        if isinstance(s2, Ref):
            reads.append(s2)
            v2 = s2.ap
        kw = {}
        if op1 is not None:
            kw["op1"] = op1
        if accum is not None:
            kw["accum_out"] = accum.ap
            writes.append(accum)
        return self.emit(eng, lambda e: e.tensor_scalar(out.ap, a.ap, v1, v2, op0, **kw), reads=reads, writes=writes)

    def stt(self, out, a, s, b, op0, op1, eng="dve"):
        reads = [a, b]
        v = s
        if isinstance(s, Ref):
            reads.append(s)
            v = s.ap
        return self.emit(eng, lambda e: e.scalar_tensor_tensor(out.ap, a.ap, v, b.ap, op0, op1), reads=reads, writes=[out])

    def copy(self, out, in_, eng="dve"):
        if eng == "act":
            return self.emit(eng, lambda e: e.copy(out.ap, in_.ap), reads=[in_], writes=[out])
        return self.emit(eng, lambda e: e.tensor_copy(out.ap, in_.ap), reads=[in_], writes=[out])

    def memset(self, out, val, eng="dve"):
        return self.emit(eng, lambda e: e.memset(out.ap, val), reads=[], writes=[out])

    def scan(self, out, d0, d1, init, op0=ALU.mult, op1=ALU.add, eng="dve"):
        reads = [d0, d1]
        v = init
        if isinstance(init, Ref):
            reads.append(init)
            v = init.ap
        return self.emit(eng, lambda e: e.tensor_tensor_scan(out.ap, d0.ap, d1.ap, v, op0, op1), reads=reads, writes=[out])

    def recip(self, out, in_):
        return self.emit("dve", lambda e: e.reciprocal(out.ap, in_.ap), reads=[in_], writes=[out])

    def finish(self):
        nc = self.nc
        prog = self.prog
        with nc.Block() as block:
            def run(e, name):
                for waits, fn, sem, inc in prog[name]:
                    for s, v in waits:
                        e.wait_ge(s, v)
                    if fn is not None:
                        fn(e).then_inc(sem, inc)

            @block.sync
            def _(e):
                run(e, "sp")

            @block.scalar
            def _(e):
                run(e, "act")

            @block.vector
            def _(e):
                run(e, "dve")

            @block.gpsimd
            def _(e):
                run(e, "pool")

            @block.tensor
            def _(e):
                run(e, "pe")
        while self.stack:
            self.stack.pop().__exit__(None, None, None)


def _ref_bc(self, shape):
    return Ref(self.tile, self.key, self.ap.to_broadcast(list(shape)))


def _ref_pb(self, n):
    return Ref(self.tile, self.key, self.ap.partition_broadcast(n))


Ref.bc = _ref_bc
Ref.pb = _ref_pb


class _Phase:
    def __init__(self, K):
        self.K = K

    def __enter__(self):
        self.h = len(self.K.stack)
        return self

    def __exit__(self, *a):
        K = self.K
        K.barrier()
        K.flush()
        while len(K.stack) > self.h:
            K.stack.pop().__exit__(None, None, None)
        return False


def _phase(self):
    return _Phase(self)


def _barrier(self):
    need = {}
    for e in self.ENG:
        if self.ccnt[e] > 0:
            need[self.csem[e]] = self.ccnt[e]
        for s in self.dsems[e]:
            if self.dcnt[s] > 0:
                need[s] = self.dcnt[s]
    for e in self.ENG:
        wd = self.waited[e]
        waits = []
        for s, v in need.items():
            if wd.get(s, 0) >= v:
                continue
            wd[s] = v
            waits.append((s, v))
        if waits:
            self.prog[e].append((waits, None, None, 0))


def _flush(self):
    nc = self.nc
    prog = self.prog
    self.prog = {e: [] for e in self.ENG}
    if not any(prog.values()):
        return
    with nc.Block() as block:
        def run(e, name):
            for waits, fn, sem, inc in prog[name]:
                for s, v in waits:
                    e.wait_ge(s, v)
                if fn is not None:
                    fn(e).then_inc(sem, inc)

        @block.sync
        def _(e):
            run(e, "sp")

        @block.scalar
        def _(e):
            run(e, "act")

        @block.vector
        def _(e):
            run(e, "dve")

        @block.gpsimd
        def _(e):
            run(e, "pool")

        @block.tensor
        def _(e):
            run(e, "pe")


def _finish(self):
    self.barrier()
    self.flush()
    while self.stack:
        self.stack.pop().__exit__(None, None, None)


Kern.phase = _phase
Kern.barrier = _barrier
Kern.flush = _flush
Kern.finish = _finish


import math

NCORES = 8
D = 1024
SEQ = 2048
NSEQ = 4
TOK = NSEQ * SEQ
FF = 2816
NFT = FF // 128
EPS = 1e-6
PI = math.pi
C1 = 6.28125
C2 = 2 * math.pi - 6.28125


def host_consts():
    c = {}
    c["ident"] = np.eye(128, dtype=np.float32)
    invf = np.zeros((128, 1), np.float32)
    sgn = np.zeros((128, 1), np.float32)
    for p in range(128):
        i = p % 64
        if i < 16:
            invf[p, 0] = np.float32(500000.0) ** np.float32(-((i % 8) * 2.0 / 16.0))
            sgn[p, 0] = -1.0 if i < 8 else 1.0
    c["rope_cols"] = np.concatenate([invf, sgn], axis=1)
    x = np.arange(2816)[None, :] - np.arange(128)[:, None] - 384
    m = ((x >= 0) & (x <= 128)).astype(np.float32) + ((x >= 0) & (x % 4 == 0) & (x <= 512)) + ((x >= 0) & (x % 16 == 0) & (x <= 2048))
    c["gmask"] = m.astype(np.float32)
    hm = np.zeros((128, 3), np.float32)
    hm[:64, 0] = 1
    hm[64:, 1] = 1
    hm[:64, 2] = 1
    hm[64:, 2] = -1
    c["hmask"] = hm
    return c


def swap_perm():
    perm = np.arange(512)
    for h in range(8):
        for i in range(16):
            perm[h * 64 + i] = h * 64 + (i + 8 if i < 8 else i - 8)
    return perm


class Ctx:
    pass


def load_bc(K, name, src_row, n, eng="sp"):
    t = K.sb(name, [128, n], F32)
    K.dma(t[:], src_row.pb(128), eng=eng)
    return t


def rms_pre(K, C, xt, nsub, gain, out_bf, tagp):
    for s in range(nsub):
        xk = [Ref(xt.tile, ("xt", s, h), xt.ap[:, s, h * 512:(h + 1) * 512]) for h in range(2)]
        row = Ref(xt.tile, ("xt", s, 0), xt.ap[:, s, :])
        ss = C.small.k(tagp + "ss%d" % s)[:, C.si:C.si + 1]
        sq = C.scr_j[:, 0:D]
        K.act(sq, row, AF.Square, accum=ss, extra_reads=[xk[1]])
        rs = C.small.k(tagp + "rs%d" % s)[:, C.si + 1:C.si + 2]
        K.act(rs, ss, AF.Sqrt, scale=1.0 / D, bias=C.epsc[:, 0:1])
        K.recip(rs, rs)
        K.emit("dve", (lambda o=out_bf[:, s, :], r=row, rr=rs, g=gain[:]: (lambda e: e.scalar_tensor_tensor(o.ap, r.ap, rr.ap, g.ap, ALU.mult, ALU.mult)))(),
               reads=[row, xk[1], rs, gain[:]], writes=[out_bf[:, s, :]])
        C.si = (C.si + 2) % 60


def transpose_to(K, C, src_bf, nsub, dstT, col0):
    for kc in range(D // 128):
        pt = C.pst[C.pti % len(C.pst)]
        C.pti += 1
        for s in range(nsub):
            K.tr(pt[:, s * 128:(s + 1) * 128], src_bf[:, s, kc * 128:(kc + 1) * 128], C.identb[:])
        eng = "act" if kc % 2 == 0 else "dve"
        K.copy(dstT.k(kc)[:, kc, col0:col0 + nsub * 128], pt[:, 0:nsub * 128], eng=eng)


def post_norm_add(K, C, ps_pair, gain, xt_sub, tagp, sub=0):
    ssa = C.small.k(tagp + "a")[:, C.si:C.si + 1]
    ssb = C.small.k(tagp + "b")[:, C.si + 1:C.si + 2]
    rs = C.small.k(tagp + "c")[:, C.si + 2:C.si + 3]
    C.si = (C.si + 3) % 60
    C.pn = getattr(C, "pn", 0) + 1
    scr = C.scr_f if C.pn % 2 == 0 else C.scr_g
    K.act(C.scr_j[:, 0:512], ps_pair[0], AF.Square, accum=ssa)
    K.act(C.scr_j[:, 512:1024], ps_pair[1], AF.Square, accum=ssb)
    K.tt(rs, ssa, ssb, ALU.add)
    K.act(rs, rs, AF.Sqrt, scale=1.0 / D, bias=C.epsc[:, 0:1])
    K.recip(rs, rs)
    for h in range(2):
        tmp = scr.k("ab"[h])[:, h * 512:(h + 1) * 512]
        K.stt(tmp, ps_pair[h], rs, gain[:, h * 512:(h + 1) * 512], ALU.mult, ALU.mult)
        xa = Ref(xt_sub.tile, ("xt", sub, h), xt_sub.ap[:, h * 512:(h + 1) * 512])
        K.tt(xa, xa, tmp, ALU.add, eng=("pool" if h == 0 else "dve"))


def rot(C):
    b = C.rotb[C.roti % len(C.rotb)]
    C.roti += 1
    return b


class View:
    def __init__(self, tile, ap):
        self.tile = tile
        self.ap = ap

    def __getitem__(self, idx):
        return Ref(self.tile, "*", self.ap[idx])


def alloc_psum(K, C, tag):
    C.pall = [K.ps("%s%d" % (tag, i), [128, 512], F32) for i in range(8)]
    C.psm = C.pall[0:6]
    C.pst = [View(t, t.h[:].bitcast(BF16)) for t in C.pall[6:8]]


def proj_tokmajor(K, C, lhs_fn, nk, wview, gain, xt, tagp):
    for ps_ in range(2):
        acc = C.pall[ps_ * 4:ps_ * 4 + 4]
        for k in range(nk):
            wb = rot(C)
            K.dma(wb[:], wview[:, k, :], eng="sp")
            for si in range(2):
                sub = ps_ * 2 + si
                for h in range(2):
                    K.mm(acc[si * 2 + h][:], lhs_fn(k, sub), wb[:, h * 512:(h + 1) * 512], start=(k == 0), stop=(k == nk - 1))
        for si in range(2):
            sub = ps_ * 2 + si
            post_norm_add(K, C, (acc[si * 2][:], acc[si * 2 + 1][:]), gain, xt[:, sub, :], tagp, sub)


def ffn_tile(K, C, xt, L, W):
    hn = C.hn
    rms_pre(K, C, xt[:], 4, C.g_ffn_pre[L], hn[:], "f")
    transpose_to(K, C, hn[:], 4, C.hnT, 0)
    NCH = FF // 256
    gv = W["ffn_g%d" % L][:].re("(kc p) n -> p kc n", p=128)
    uv = W["ffn_u%d" % L][:].re("(kc p) n -> p kc n", p=128)
    for ch in range(NCH):
        wg = C.wgb[ch % len(C.wgb)]
        wu = C.wub[ch % len(C.wub)]
        K.dma(wg[:], gv[:, :, ch * 256:(ch + 1) * 256], eng="sp")
        K.dma(wu[:], uv[:, :, ch * 256:(ch + 1) * 256], eng="sp")
        for mi in range(2):
            m = ch * 2 + mi
            pg = C.psm[C.pmi % len(C.psm)]
            pu = C.psm[(C.pmi + 1) % len(C.psm)]
            C.pmi += 2
            for kc in range(8):
                K.mm(pg[:], wg[:, kc, mi * 128:(mi + 1) * 128], C.hnT.k(kc)[:, kc, :], start=(kc == 0), stop=(kc == 7))
            for kc in range(8):
                K.mm(pu[:], wu[:, kc, mi * 128:(mi + 1) * 128], C.hnT.k(kc)[:, kc, :], start=(kc == 0), stop=(kc == 7))
            sg = C.sgb[m % 2]
            K.act(sg[:], pg[:], AF.Silu)
            K.tt(C.actT[:, m, :], sg[:], pu[:], ALU.mult)
    wdv = W["ffn_d%d" % L][:].re("(m p) n -> p m n", p=128)
    proj_tokmajor(K, C, lambda m, sub: C.actT[:, m, sub * 128:(sub + 1) * 128], NFT, wdv, C.g_ffn_post[L], xt, "fp")


def bc_last(ref, n):
    sh = list(ref.ap.shape)
    return Ref(ref.tile, ref.key, ref.ap.unsqueeze(len(sh)).to_broadcast(sh + [n]))


def bc_mid(ref, n):
    sh = list(ref.ap.shape)
    return Ref(ref.tile, ref.key, ref.ap.unsqueeze(1).to_broadcast([sh[0], n] + sh[1:]))


def sincos(K, C, ang, shape, sn, cs, tg):
    if isinstance(tg, str):
        t = K.sb(tg + "_t", shape, F32)
        ti = K.sb(tg + "_ti", shape, I32)
        kf = K.sb(tg + "_kf", shape, F32)
        r = K.sb(tg + "_r", shape, F32)
    else:
        t, ti, kf, r = tg
    for shift, dst in ((0.0, sn), (PI / 2, cs)):
        K.ts(t[:], ang, 1.0 / (2 * PI), ALU.mult, shift / (2 * PI), ALU.add)
        K.copy(ti[:], t[:])
        K.copy(kf[:], ti[:])
        K.stt(r[:], kf[:], -C1, ang, ALU.mult, ALU.add)
        K.stt(r[:], kf[:], -C2, r[:], ALU.mult, ALU.add)
        K.ts(r[:], r[:], shift, ALU.add, PI, ALU.min)
        K.ts(r[:], r[:], -PI, ALU.max)
        K.act(dst, r[:], AF.Sin)


def cast_weight(K, C, dst, src, rows, cols):
    r0 = 0
    while r0 < rows:
        n = min(128, rows - r0)
        b = C.castb[C.casti % len(C.castb)]
        C.casti += 1
        K.dma(b[0:n, 0:cols], src[r0:r0 + n, :], eng="pool")
        K.dma(dst[r0:r0 + n, :], b[0:n, 0:cols], eng="sp")
        r0 += n


class Stop(Exception):
    pass


CUT = [None]


def chk(n):
    if CUT[0] == n:
        raise Stop()


def build(stage="full", dbg=()):
    try:
        return _build(stage, dbg)
    except Stop:
        K = LASTK[0]
        K.finish()
        return K.nc, DBGG[0]


LASTK = [None]
DBGG = [None]


def _build(stage="full", dbg=()):
    nc = bass.Bass("TRN2", target_bir_lowering=False)
    K = Kern(nc)
    LASTK[0] = K
    C = Ctx()
    C.si = 0
    C.pti = 0
    C.pmi = 0
    C.casti = 0
    C.roti = 0
    I = {}

    def inp(name, shape, dt=F32):
        I[name] = K.dram(name, shape, dt, kind="ExternalInput")
        return I[name]

    inp("x", [TOK, D])
    inp("pos", [1, TOK], I32)
    for nm in ("norm_mix_pre", "norm_mix_post", "norm_ffn_pre", "norm_ffn_post"):
        inp(nm, [2, D])
    inp("w_in0", [D, 3072])
    inp("ev_w_out", [D, D])
    inp("s5_a_re", [32, 64]); inp("s5_a_im", [32, 64]); inp("s5_log_dt", [1, 32])
    inp("s5_b_re", [32, 64, 16]); inp("s5_b_im", [32, 64, 16])
    inp("s5_c_re", [32, 16, 64]); inp("s5_c_im", [32, 16, 64]); inp("s5_d", [32, 16])
    inp("s5_w_glu", [512, 512])
    inp("od_w_in", [D, 2048]); inp("od_w_out", [D, D])
    inp("rg_conv_w", [4, D]); inp("rg_conv_b", [1, D])
    inp("rg_w_r", [1024, 256]); inp("rg_b_r", [1, D]); inp("rg_w_i", [1024, 256]); inp("rg_b_i", [1, D]); inp("rg_lam", [1, D])
    for L in range(2):
        inp("ffn_g%d" % L, [D, FF]); inp("ffn_u%d" % L, [D, FF]); inp("ffn_d%d" % L, [FF, D])
    inp("ident", [128, 128]); inp("rope_cols", [128, 2]); inp("gmask", [128, 2816]); inp("hmask", [128, 3])
    out = K.dram("out", [TOK, D], F32, kind="ExternalOutput")
    DBG = {}
    DBGG[0] = DBG

    def dbgout(name, shape, dt=F32):
        DBG[name] = K.dram("dbg_" + name, shape, dt, kind="ExternalOutput")
        return DBG[name]

    W = {}
    for nm, r, c in (("w_in0", D, 3072), ("ev_w_out", D, D), ("s5_w_glu", 512, 512), ("od_w_in", D, 2048), ("od_w_out", D, D),
                     ("rg_w_r", 1024, 256), ("rg_w_i", 1024, 256),
                     ("ffn_g0", D, FF), ("ffn_u0", D, FF), ("ffn_d0", FF, D), ("ffn_g1", D, FF), ("ffn_u1", D, FF), ("ffn_d1", FF, D)):
        if stage != "S5prep":
            W[nm] = K.dram("wb_" + nm, [r, c], BF16)
    x1d = K.dram("x1d", [TOK, D], F32) if stage != "S5prep" else None

    identf = K.sb("identf", [128, 128], F32)
    C.identb = K.sb("identb", [128, 128], BF16)
    C.epsc = K.sb("epsc", [128, 1], F32)
    C.onec = K.sb("onec", [128, 1], F32)
    C.small = K.sb("small", [128, 64], F32)
    onesb = K.sb("onesb", [128, 64], BF16)
    gm = K.sb("gm", [128, 2816], BF16)
    ropec = K.sb("ropec", [128, 2], F32)
    hmask = K.sb("hmask", [128, 3], F32)
    C.dTz = K.dram("dTz", [128, 32, 128], BF16)
    C.dBs = K.dram("dBs", [128, 32, 128], BF16)
    C.dCsRe = K.dram("dCsRe", [128, 16, 128], BF16)
    C.dCsIm = K.dram("dCsIm", [128, 16, 128], BF16)
    C.dMU = K.dram("dMU", [128, 3, 16], F32)
    C.dMUP = K.dram("dMUP", [128, 3, 16, 16], F32)
    K.dma(identf[:], I["ident"][:])
    junk = K.sb("junk", [1, 64], F32)
    junki = K.sb("junki", [1, 4], I32)
    for nm_, t_ in I.items():
        flat = t_[:]
        while len(flat.ap.shape) > 1:
            flat = flat[0]
        if nm_ == "pos":
            K.dma(junki[0:1, 0:1], Ref(flat.tile, "*", flat.ap[0:1].unsqueeze(0)))
        else:
            K.dma(junk[0:1, 0:1], Ref(flat.tile, "*", flat.ap[0:1].unsqueeze(0)))
    if stage != "full":
        K.dma(out[0:1, 0:64], junk[:])
    chk(-3)
    K.copy(C.identb[:], identf[:])
    K.memset(C.epsc[:], EPS)
    K.memset(C.onec[:], 1.0)
    K.memset(onesb[:], 1.0)
    K.dma(ropec[:], I["rope_cols"][:])
    K.dma(hmask[:], I["hmask"][:])
    chk(-2)
    K.dma(gm[:], I["gmask"][:], eng="pool")
    chk(-1)

    with K.phase():
        C.castb = [K.sb("castb%d" % i, [128, 3072], BF16) for i in range(4)]
        names = list(W.keys())
        C.lazy = {}
        if stage == "full":
            names = ["w_in0"]
            l0 = ["ev_w_out", "s5_w_glu", "ffn_g0", "ffn_u0", "ffn_d0"]
            l1 = ["od_w_in", "od_w_out", "rg_w_r", "rg_w_i", "ffn_g1", "ffn_u1", "ffn_d1"]
            for sq_, lst in ((0, l0), (1, l1)):
                jobs = []
                for nm in lst:
                    r, c = W[nm].h.shape
                    for r0 in range(0, r, 128):
                        jobs.append((W[nm], I[nm], r0, min(128, r - r0), c))
                C.lazy[sq_] = jobs
        if stage in ("A", "S5prep"):
            names = ["w_in0"]
        if stage == "L1a":
            names = ["od_w_in", "od_w_out", "rg_w_r", "rg_w_i", "ffn_g1", "ffn_u1", "ffn_d1"]
        if stage == "S5prep":
            names = []
        for nm in names:
            r, c = W[nm].h.shape
            cast_weight(K, C, W[nm], I[nm], r, c)

    chk(0)
    with K.phase():
        ps_a = K.ps("ps_a", [128, 512], F32)
        ps_b = K.ps("ps_b", [128, 512], F32)
        Tz = K.sb("Tz", [128, 32, 128], BF16)
        Bs = K.sb("Bs", [128, 32, 128], BF16)
        CsRe = K.sb("CsRe", [128, 16, 128], BF16)
        CsIm = K.sb("CsIm", [128, 16, 128], BF16)
        MU = K.sb("MU", [128, 3, 16], F32)
        araw = K.sb("araw", [32, 2, 128], F32)
        for j, nm in enumerate(("s5_a_re", "s5_a_im")):
            K.dma(araw[:, j, 0:64], I[nm][:])
            K.dma(araw[:, j, 64:128], I[nm][:])
        are = K.sb("are", [128, 32], F32)
        aim = K.sb("aim", [128, 32], F32)
        K.tr(ps_a[:, 0:32], araw[:, 0, :], identf[0:32, 0:32])
        K.tr(ps_a[:, 32:64], araw[:, 1, :], identf[0:32, 0:32])
        K.copy(are[:], ps_a[:, 0:32])
        K.copy(aim[:], ps_a[:, 32:64])
        chk(1)
        dtb = K.sb("dtb", [128, 32], F32)
        K.dma(dtb[:], I["s5_log_dt"][0:1, :].pb(128))
        K.act(dtb[:], dtb[:], AF.Exp)
        mag = K.sb("mag", [128, 32], F32)
        ang = K.sb("ang", [128, 32], F32)
        K.tt(mag[:], are[:], dtb[:], ALU.mult)
        K.act(mag[:], mag[:], AF.Exp)
        K.tt(ang[:], aim[:], dtb[:], ALU.mult)
        chk(2)
        sn = K.sb("sn", [128, 32], F32)
        cs = K.sb("cs", [128, 32], F32)
        sincos(K, C, ang[:], [128, 32], sn[:], cs[:], "sc1")
        chk(3)
        lr = K.sb("lr", [128, 32], F32)
        li = K.sb("li", [128, 32], F32)
        K.tt(lr[:], mag[:], cs[:], ALU.mult)
        K.tt(li[:], mag[:], sn[:], ALU.mult)
        PA = K.sb("PA", [128, 8, 32], F32)
        PB = K.sb("PB", [128, 8, 32], F32)
        tq = K.sb("tq", [128, 32], F32)
        K.copy(PA[:, 0, :], hmask[:, 0:1].bc([128, 32]))
        K.copy(PB[:, 0, :], hmask[:, 1:2].bc([128, 32]))
        for k in range(1, 8):
            K.tt(tq[:], li[:], PB[:, k - 1, :], ALU.mult)
            K.tt(PA[:, k, :], lr[:], PA[:, k - 1, :], ALU.mult)
            K.tt(PA[:, k, :], PA[:, k, :], tq[:], ALU.add)
            K.tt(tq[:], li[:], PA[:, k - 1, :], ALU.mult)
            K.tt(PB[:, k, :], lr[:], PB[:, k - 1, :], ALU.mult)
            K.tt(PB[:, k, :], PB[:, k, :], tq[:], ALU.subtract)
        lm1 = K.sb("lm1", [128, 32], F32)
        K.ts(lm1[:], lr[:], -1.0, ALU.add)
        den = K.sb("den", [128, 32], F32)
        K.tt(den[:], are[:], are[:], ALU.mult)
        K.tt(tq[:], aim[:], aim[:], ALU.mult)
        K.tt(den[:], den[:], tq[:], ALU.add)
        K.recip(den[:], den[:])
        wr = K.sb("wr", [128, 32], F32)
        wi = K.sb("wi", [128, 32], F32)
        K.tt(wr[:], lm1[:], are[:], ALU.mult)
        K.tt(tq[:], li[:], aim[:], ALU.mult)
        K.tt(wr[:], wr[:], tq[:], ALU.add)
        K.tt(wr[:], wr[:], den[:], ALU.mult)
        K.tt(wi[:], li[:], are[:], ALU.mult)
        K.tt(tq[:], lm1[:], aim[:], ALU.mult)
        K.tt(wi[:], wi[:], tq[:], ALU.subtract)
        K.tt(wi[:], wi[:], den[:], ALU.mult)
        chk(4)
        bre = K.sb("bre", [128, 32, 16], F32)
        bim = K.sb("bim", [128, 32, 16], F32)
        for t_, nm in ((bre, "s5_b_re"), (bim, "s5_b_im")):
            for h in range(2):
                K.dma(t_.k(h)[h * 64:(h + 1) * 64, :, :], I[nm][:].re("g p c -> p g c"), eng=("sp" if h == 0 else "pool"))
        chk(5)
        Bbr = K.sb("Bbr", [128, 32, 16], F32)
        Bbi = K.sb("Bbi", [128, 32, 16], F32)
        t3 = K.sb("t3", [128, 32, 16], F32)
        K.tt(Bbr[:], bre[:], bc_last(wr[:], 16), ALU.mult)
        K.tt(t3[:], bim[:], bc_last(wi[:], 16), ALU.mult)
        K.tt(Bbr[:], Bbr[:], t3[:], ALU.subtract)
        K.tt(Bbi[:], bim[:], bc_last(wr[:], 16), ALU.mult)
        K.tt(t3[:], bre[:], bc_last(wi[:], 16), ALU.mult)
        K.tt(Bbi[:], Bbi[:], t3[:], ALU.add)
        Wpad = K.sb("Wpad", [128, 32, 15, 16], F32)
        K.memset(Wpad[:], 0.0)
        for m in range(8):
            k = 7 - m
            K.tt(Wpad[:, :, m, :], Bbr[:], bc_last(PA[:, k, :], 16), ALU.mult)
            K.tt(t3[:], Bbi[:], bc_last(PB[:, k, :], 16), ALU.mult)
            K.tt(Wpad[:, :, m, :], Wpad[:, :, m, :], t3[:], ALU.add)
        chk(6)
        craw = K.sb("craw", [128, 4, 128], F32)
        for t_ in range(4):
            K.dma(craw.k((t_, 0))[:, t_, 0:64], I["s5_c_re"][t_ * 8:(t_ + 1) * 8].re("g c p -> (g c) p"))
            K.dma(craw.k((t_, 1))[:, t_, 64:128], I["s5_c_im"][t_ * 8:(t_ + 1) * 8].re("g c p -> (g c) p"), eng="pool")
        Vst = K.sb("Vst", [128, 32, 16], F32)
        for t_ in range(4):
            K.tr(ps_a[:, t_ * 128:(t_ + 1) * 128], craw[:, t_, :], identf[:])
        K.ts(Vst[:].re("p g c -> p (g c)"), ps_a[:], hmask[:, 2:3], ALU.mult)
        chk(7)
        dcol = K.sb("dcol", [128, 32], F32)
        for j in range(8):
            K.dma(dcol.k(j)[j * 16:(j + 1) * 16, :], I["s5_d"][:].re("g c -> c g"), allow_slow_non_contiguous=True, eng=("sp" if j % 2 == 0 else "pool"))
        chk(8)
        for g in range(32):
            pz = ps_b if g % 2 == 0 else ps_a
            for i in range(8):
                K.mm(pz[:, i * 16:(i + 1) * 16], Wpad[:, g, 7 - i:15 - i, :].re("p m c -> p (m c)"), Vst[:, g, :])
            K.stt(Tz[:, g, :], identf[:], dcol[:, g:g + 1], pz[:, 0:128], ALU.mult, ALU.add)
            K.tr(pz[:, 128:256], Wpad[:, g, 0:8, :].re("p m c -> p (m c)"), identf[:])
            K.copy(Bs[:, g, :], pz[:, 128:256], eng="act")
        chk(9)
        praw = K.sb("praw", [16, 2, 128], F32)
        K.dma(praw[:, 0, :], I["s5_a_re"][:].re("(pr h) p -> pr (h p)", h=2))
        K.dma(praw[:, 1, :], I["s5_a_im"][:].re("(pr h) p -> pr (h p)", h=2))
        K.tr(ps_a[:, 0:16], praw[:, 0, :], identf[0:16, 0:16])
        K.tr(ps_a[:, 16:32], praw[:, 1, :], identf[0:16, 0:16])
        arp = K.sb("arp", [128, 16], F32)
        aip = K.sb("aip", [128, 16], F32)
        K.copy(arp[:], ps_a[:, 0:16])
        K.copy(aip[:], ps_a[:, 16:32])
        chk(10)
        dtp = K.sb("dtp", [128, 16], F32)
        ldv = I["s5_log_dt"][0:1, :].re("o (pr h) -> o h pr", h=2)
        K.dma(dtp[0:64, :], ldv[:, 0, :].pb(64), allow_slow_non_contiguous=True)
        K.dma(dtp[64:128, :], ldv[:, 1, :].pb(64), allow_slow_non_contiguous=True)
        K.act(dtp[:], dtp[:], AF.Exp)
        chk(11)
        magp = K.sb("magp", [128, 16], F32)
        angp = K.sb("angp", [128, 16], F32)
        K.tt(magp[:], arp[:], dtp[:], ALU.mult)
        K.act(magp[:], magp[:], AF.Exp)
        K.tt(angp[:], aip[:], dtp[:], ALU.mult)
        snp = K.sb("snp", [128, 16], F32)
        csp = K.sb("csp", [128, 16], F32)
        sincos(K, C, angp[:], [128, 16], snp[:], csp[:], "sc2")
        Pr = K.sb("Pr", [128, 9, 16], F32)
        Pi = K.sb("Pi", [128, 9, 16], F32)
        K.tt(Pr[:, 1, :], magp[:], csp[:], ALU.mult)
        K.tt(Pi[:, 1, :], magp[:], snp[:], ALU.mult)
        tp = K.sb("tp", [128, 16], F32)
        for k in range(2, 9):
            K.tt(tp[:], Pi[:, 1, :], Pi[:, k - 1, :], ALU.mult)
            K.tt(Pr[:, k, :], Pr[:, 1, :], Pr[:, k - 1, :], ALU.mult)
            K.tt(Pr[:, k, :], Pr[:, k, :], tp[:], ALU.subtract)
            K.tt(tp[:], Pi[:, 1, :], Pr[:, k - 1, :], ALU.mult)
            K.tt(Pi[:, k, :], Pr[:, 1, :], Pi[:, k - 1, :], ALU.mult)
            K.tt(Pi[:, k, :], Pi[:, k, :], tp[:], ALU.add)
        K.copy(MU[:, 0, :], Pr[:, 8, :])
        K.copy(MU[:, 1, :], Pi[:, 8, :])
        K.ts(MU[:, 2, :], Pi[:, 8, :], -1.0, ALU.mult)
        MUP = K.sb("MUP", [128, 3, 16, 16], F32)
        K.copy(MUP[:, 0, 0, :], Pr[:, 8, :])
        K.copy(MUP[:, 1, 0, :], Pi[:, 8, :])
        for k in range(1, 16):
            K.tt(tp[:], MU[:, 1, :], MUP[:, 1, k - 1, :], ALU.mult)
            K.tt(MUP[:, 0, k, :], MU[:, 0, :], MUP[:, 0, k - 1, :], ALU.mult)
            K.tt(MUP[:, 0, k, :], MUP[:, 0, k, :], tp[:], ALU.subtract)
            K.tt(tp[:], MU[:, 1, :], MUP[:, 0, k - 1, :], ALU.mult)
            K.tt(MUP[:, 1, k, :], MU[:, 0, :], MUP[:, 1, k - 1, :], ALU.mult)
            K.tt(MUP[:, 1, k, :], MUP[:, 1, k, :], tp[:], ALU.add)
        K.ts(MUP[:, 2, :, :], MUP[:, 1, :, :], -1.0, ALU.mult)
        K.dma(C.dMUP[:], MUP[:])
        chk(12)
        cpraw = K.sb("cpraw", [128, 4, 128], F32)
        for ri, nm in enumerate(("s5_c_re", "s5_c_im")):
            for pr in range(16):
                for h in range(2):
                    K.dma(cpraw.k((ri, pr, h))[(pr % 8) * 16:(pr % 8 + 1) * 16, ri * 2 + pr // 8, h * 64:(h + 1) * 64], I[nm][2 * pr + h], eng="sp" if h == 0 else "pool")
        chk(13)
        Cp = K.sb("Cp", [128, 2, 16, 16], F32)
        for q in range(4):
            K.tr(ps_b[:, q * 128:(q + 1) * 128], cpraw[:, q, :], identf[:])
        K.copy(Cp[:].re("p r q c -> p (r q c)"), ps_b[:])
        t4 = K.sb("t4", [128, 16, 16], F32)
        t5 = K.sb("t5", [128, 16, 16], F32)
        for i in range(8):
            pr_b = bc_last(Pr[:, i + 1, :], 16)
            pi_b = bc_last(Pi[:, i + 1, :], 16)
            K.tt(t4[:], Cp[:, 0, :, :], pr_b, ALU.mult)
            K.tt(t5[:], Cp[:, 1, :, :], pi_b, ALU.mult)
            K.tt(CsRe[:].re("p q (i c) -> p q i c", i=8)[:, :, i, :], t4[:], t5[:], ALU.subtract)
            K.tt(t4[:], Cp[:, 0, :, :], pi_b, ALU.mult)
            K.tt(t5[:], Cp[:, 1, :, :], pr_b, ALU.mult)
            K.tt(t4[:], t4[:], t5[:], ALU.add)
            K.ts(CsIm[:].re("p q (i c) -> p q i c", i=8)[:, :, i, :], t4[:], -1.0, ALU.mult)
        chk(14)
        K.dma(C.dTz[:], Tz[:]); K.dma(C.dBs[:], Bs[:]); K.dma(C.dCsRe[:], CsRe[:]); K.dma(C.dCsIm[:], CsIm[:]); K.dma(C.dMU[:], MU[:])
        if "s5prep" in dbg:
            for nm, t_, sh in (("Tz", Tz, [128, 32, 128]), ("Bs", Bs, [128, 32, 128]), ("CsRe", CsRe, [128, 16, 128]), ("CsIm", CsIm, [128, 16, 128])):
                o = dbgout(nm, sh)
                tf = K.sb("dbgf_" + nm, sh, F32)
                K.copy(tf[:], t_[:])
                K.dma(o[:], tf[:])
            o = dbgout("MU", [128, 3, 16])
            K.dma(o[:], MU[:])
    if stage == "S5prep":
        K.finish()
        return nc, DBG
    C.gm = gm; C.ropec = ropec; C.onesb = onesb; C.identf = identf
    build_layers(K, C, I, W, out, x1d, stage, dbg, dbgout)
    K.finish()
    return nc, DBG


def make_in_maps(inputs, ncores=NCORES):
    hc = host_consts()
    perm = swap_perm()
    w_in = np.asarray(inputs["ev_w_in"][0])
    q, k_, v, u = w_in[:, 0:512], w_in[:, 512:1024], w_in[:, 1024:1536], w_in[:, 1536:2048]
    w_in0 = np.ascontiguousarray(np.concatenate([q, q[:, perm], k_, k_[:, perm], v, u], axis=1))
    shared = {
        "norm_mix_pre": inputs["norm_mix_pre"], "norm_mix_post": inputs["norm_mix_post"],
        "norm_ffn_pre": inputs["norm_ffn_pre"], "norm_ffn_post": inputs["norm_ffn_post"],
        "w_in0": w_in0, "ev_w_out": inputs["ev_w_out"][0],
        "s5_a_re": inputs["s5_a_re"][0], "s5_a_im": inputs["s5_a_im"][0], "s5_log_dt": inputs["s5_log_dt"].reshape(1, 32),
        "s5_b_re": inputs["s5_b_re"][0], "s5_b_im": inputs["s5_b_im"][0], "s5_c_re": inputs["s5_c_re"][0], "s5_c_im": inputs["s5_c_im"][0],
        "s5_d": inputs["s5_d"][0], "s5_w_glu": inputs["s5_w_glu"][0],
        "od_w_in": inputs["od_w_in"][0], "od_w_out": inputs["od_w_out"][0],
        "rg_conv_w": inputs["rg_conv_w"][0], "rg_conv_b": inputs["rg_conv_b"].reshape(1, D),
        "rg_w_r": inputs["rg_w_r"][0].reshape(1024, 256), "rg_b_r": inputs["rg_b_r"].reshape(1, D),
        "rg_w_i": inputs["rg_w_i"][0].reshape(1024, 256), "rg_b_i": inputs["rg_b_i"].reshape(1, D), "rg_lam": inputs["rg_lam"].reshape(1, D),
    }
    for L in range(2):
        shared["ffn_g%d" % L] = inputs["ffn_w_gate"][L]
        shared["ffn_u%d" % L] = inputs["ffn_w_up"][L]
        shared["ffn_d%d" % L] = inputs["ffn_w_down"][L]
    shared.update(hc)
    shared = {k: np.ascontiguousarray(np.asarray(v_)) for k, v_ in shared.items()}
    maps = []
    x = np.asarray(inputs["x"])
    pos = np.asarray(inputs["positions"])
    for c in range(ncores):
        m = dict(shared)
        m["x"] = np.ascontiguousarray(x[c * NSEQ:(c + 1) * NSEQ].reshape(TOK, D))
        m["pos"] = np.ascontiguousarray(pos[c * NSEQ:(c + 1) * NSEQ].reshape(1, TOK).astype(np.int32))
        maps.append(m)
    return maps


def build_layers(K, C, I, W, out, x1d, stage, dbg, dbgout):
    nseq = NSEQ
    if stage in ("A", "B", "C", "D0"):
        nseq = 1
    gm = C.gm
    if stage == "L1a":
        C.g_mix_pre = [None, None]; C.g_mix_post = [None, None]; C.g_ffn_pre = [None, None]; C.g_ffn_post = [None, None]
        with K.phase():
            K.dma(x1d[0:1024, :], I["x"][0:1024, :])
        build_layer1(K, C, I, W, out, x1d, stage, dbg, dbgout)
        return
    with K.phase():
        C.g_mix_pre = [None, None]; C.g_mix_post = [None, None]; C.g_ffn_pre = [None, None]; C.g_ffn_post = [None, None]
        C.g_mix_pre[0] = load_bc(K, "g_mp0", I["norm_mix_pre"][0:1, :], D)
        C.g_mix_post[0] = load_bc(K, "g_mo0", I["norm_mix_post"][0:1, :], D)
        C.g_ffn_pre[0] = load_bc(K, "g_fp0", I["norm_ffn_pre"][0:1, :], D)
        C.g_ffn_post[0] = load_bc(K, "g_fo0", I["norm_ffn_post"][0:1, :], D)
        C.scr_j = K.sb("scr_j", [128, D], BF16)
        C.scr_f = K.sb("scr_f", [128, D], F32)
        C.scr_g = K.sb("scr_g", [128, D], F32)
        C.hn = K.sb("hn", [128, 4, D], BF16)
        B1 = K.sb("B1", [128, 4, SEQ], BF16)
        B2 = K.sb("B2", [128, 4, SEQ], BF16)
        B3 = K.sb("B3", [128, 8192], BF16)
        UT = K.sb("UT", [128, 32, 256], BF16)
        QT = B1; attT = B1; KT = B2; ssmT = B2
        V = B3[:].re("p (s f) -> p s f", f=512)
        ygT = B3[:].re("p (k t) -> p k t", k=4)
        for s in range(nseq):
            tok0 = s * SEQ
            with K.phase():
                xt = K.sb("xtA", [128, 4, D], F32)
                xnT = K.sb("xnT", [128, 8, 1024], BF16)
                wch = [K.sb("wch%d" % i, [128, 8, 512], BF16) for i in range(2)]
                posi = K.sb("posi", [128, 512], I32)
                ang = K.sb("angA", [128, 512], F32)
                cosT = K.sb("cosT", [128, 1024], F32)
                sinT = K.sb("sinT", [128, 1024], F32)
                sct = (K.sb("sct", [128, 512], F32), K.sb("scti", [128, 512], I32), K.sb("sckf", [128, 512], F32), K.sb("scr", [128, 512], F32))
                tA = K.sb("tA", [128, 512], F32)
                tB = K.sb("tB", [128, 512], F32)
                UA = K.sb("UA", [128, 32, 8, 16], BF16)
                C.pst = [K.ps("pst%d" % i, [128, 1024], BF16) for i in range(2)]
                psm = [K.ps("psm%d" % i, [128, 512], F32) for i in range(6)]
                pmi = 0
                wv = W["w_in0"][:].re("(kc p) n -> p kc n", p=128)
                for blk in range(2):
                    b0 = tok0 + blk * 1024
                    for half in range(2):
                        K.dma(xt[:], I["x"][b0 + half * 512:b0 + (half + 1) * 512, :].re("(s p) d -> p s d", p=128))
                        rms_pre(K, C, xt[:], 4, C.g_mix_pre[0], C.hn[:], "a")
                        transpose_to(K, C, C.hn[:], 4, xnT, half * 512)
                        K.dma(posi[:], I["pos"][0:1, b0 + half * 512:b0 + (half + 1) * 512].pb(128))
                        K.copy(ang[:], posi[:])
                        K.ts(ang[:], ang[:], C.ropec[:, 0:1], ALU.mult)
                        hsl = slice(half * 512, (half + 1) * 512)
                        sincos(K, C, ang[:], [128, 512], sinT[:, hsl], cosT[:, hsl], sct)
                        K.ts(sinT[:, hsl], sinT[:, hsl], C.ropec[:, 1:2], ALU.mult)
                    for qk in range(2):
                        dst = QT if qk == 0 else KT
                        wq, wsw = wch[0], wch[1]
                        K.dma(wq[:], wv[:, :, (2 * qk) * 512:(2 * qk + 1) * 512])
                        K.dma(wsw[:], wv[:, :, (2 * qk + 1) * 512:(2 * qk + 2) * 512])
                        for t in range(4):
                            for nh in range(2):
                                pq = psm[pmi % 6]; psw = psm[(pmi + 1) % 6]; pmi += 2
                                for kc in range(8):
                                    K.mm(pq[:], wq[:, kc, t * 128:(t + 1) * 128], xnT.k(kc)[:, kc, nh * 512:(nh + 1) * 512], start=(kc == 0), stop=(kc == 7))
                                for kc in range(8):
                                    K.mm(psw[:], wsw[:, kc, t * 128:(t + 1) * 128], xnT.k(kc)[:, kc, nh * 512:(nh + 1) * 512], start=(kc == 0), stop=(kc == 7))
                                K.tt(tA[:], pq[:], cosT[:, nh * 512:(nh + 1) * 512], ALU.mult)
                                K.tt(tB[:], psw[:], sinT[:, nh * 512:(nh + 1) * 512], ALU.mult)
                                c0 = blk * 1024 + nh * 512
                                K.tt(dst[:, t, c0:c0 + 512], tA[:], tB[:], ALU.add, eng="pool")
                    wvv = wch[0]
                    K.dma(wvv[:], wv[:, :, 4 * 512:5 * 512])
                    for sub in range(8):
                        pv = psm[pmi % 6]; pmi += 1
                        for kc in range(8):
                            K.mm(pv[:], xnT.k(kc)[:, kc, sub * 128:(sub + 1) * 128], wvv[:, kc, :], start=(kc == 0), stop=(kc == 7))
                        K.copy(V[:, blk * 8 + sub, :], pv[:], eng="act")
                    wu_ = wch[1]
                    K.dma(wu_[:], wv[:, :, 5 * 512:6 * 512])
                    for j in range(8):
                        pu = psm[pmi % 6]; pmi += 1
                        for kc in range(8):
                            K.mm(pu[:], xnT.k(kc)[:, kc, :].re("p (c j) -> p j c", j=8)[:, j, :], wu_[:, kc, :], start=(kc == 0), stop=(kc == 7))
                        K.copy(UA[:, :, j, :], pu[:].re("p (g c) -> p g c", c=16), eng="act")
                    for g4 in range(8):
                        pt = C.pst[g4 % 2]
                        for gi in range(4):
                            g = g4 * 4 + gi
                            K.tr(pt[:, gi * 128:(gi + 1) * 128], UA[:, g, :, :].re("p j c -> p (j c)"), C.identb[:])
                        K.copy(UT[:, g4 * 4:(g4 + 1) * 4, blk * 128:(blk + 1) * 128], pt[:, 0:512].re("p (g c) -> p g c", g=4), eng=("act" if g4 % 2 == 0 else "dve"))
            if "A" in dbg and s == 0:
                for nm, t_, sh in (("QT", QT[:], [128, 4, SEQ]), ("KT", KT[:], [128, 4, SEQ]), ("V", V, [128, 16, 512]), ("UT", UT[:], [128, 32, 256])):
                    with K.phase():
                        o = dbgout(nm, sh)
                        tf = K.sb("dbgf" + nm, sh, F32)
                        K.copy(tf[:], t_)
                        K.dma(o[:], tf[:])
            if stage == "A":
                raise Stop()
            with K.phase():
                pS = [K.ps("pS%d" % i, [128, 512], F32) for i in range(4)]
                pnd = [[K.ps("pnd%d_%d" % (g_, h_), [128, 512], F32) for h_ in range(2)] for g_ in range(2)]
                Dlo = K.sb("Dlo", [128, 512], F32)
                Pb = [K.sb("Pb%d" % i, [128, 512], BF16) for i in range(4)]
                Pm = [K.sb("Pm%d" % i, [128, 512], BF16) for i in range(4)]
                rden = K.sb("rden", [128, 512], F32)
                Dsb = K.sb("Dsb", [128, 512], F32)
                Vx = K.sb("Vx", [128, 16, 8, 128], BF16)
                K.memset(Vx[:], 1.0)
                for h_ in range(8):
                    cs_ = slice(0, 64) if h_ % 2 == 0 else slice(64, 128)
                    K.copy(Vx[:, :, h_, cs_], V[:, :, h_ * 64:(h_ + 1) * 64], eng=("act" if h_ % 2 == 0 else "dve"))
                jobs = list(C.lazy.get(s, []))
                njobs = len(jobs)
                if njobs:
                    lzb = [K.sb("lzb%d" % i, [128, 2816], BF16) for i in range(4)]
                lzi = 0

                def emit_job():
                    nonlocal lzi
                    dst_, src_, r0_, n_, c_ = jobs.pop(0)
                    b_ = lzb[lzi % 4]
                    lzi += 1
                    K.dma(b_[0:n_, 0:c_], src_[r0_:r0_ + n_, :], eng="pool")
                    K.dma(dst_.k(r0_)[r0_:r0_ + n_, :], b_[0:n_, 0:c_], eng="sp")
                LA = 3
                items = []
                for t in range(4):
                    for qc in range(4):
                        nkb = 4 * qc + 4
                        for kb in range(nkb):
                            for hh in range(2):
                                items.append((t, qc, kb, hh, nkb))
                n_it = len(items)
                deferred = []

                def fin(t, qc, hh, nd):
                    hs = slice(hh * 64, (hh + 1) * 64)
                    K.act(Dlo.k(hh)[hs, :], Dlo.k(hh)[hs, :], AF.Ln)
                    K.act(rden.k(hh)[hs, :], Dlo.k(hh)[hs, :], AF.Exp, scale=-1.0)
                    K.tt(attT.k((t, qc))[hs, t, qc * 512:(qc + 1) * 512], nd[hs, :], rden.k(hh)[hs, :], ALU.mult)

                for it_ in range(n_it + LA):
                    while deferred and deferred[0][0] <= it_:
                        fin(*deferred.pop(0)[1])
                    if it_ < n_it:
                        t, qc, kb, hh, nkb = items[it_]
                        hs = slice(hh * 64, (hh + 1) * 64)
                        delta = 4 * qc - kb
                        ps_ = pS[it_ % 4]
                        K.mm(ps_[:], KT[hs, t, kb * 128:(kb + 1) * 128], QT.k((t, qc))[hs, t, qc * 512:(qc + 1) * 512])
                        pb_ = Pb[it_ % 4]; pm_ = Pm[it_ % 4]
                        K.act(pb_[:], ps_[:], AF.Exp, scale=0.125)
                        K.tt(pm_[:], pb_[:], gm[:, 128 * (delta + 3):128 * (delta + 3) + 512], ALU.mult, eng=("dve" if (njobs or it_ % 3 != 2) else "pool"))
                        if jobs and it_ % 8 == 4:
                            emit_job()
                    j_ = it_ - LA
                    if j_ >= 0:
                        t, qc, kb, hh, nkb = items[j_]
                        hs = slice(hh * 64, (hh + 1) * 64)
                        grp = t * 4 + qc
                        nd = pnd[grp % 2][hh]
                        pm_ = Pm[j_ % 4]
                        K.mm(nd[:], Vx[:, kb, 2 * t + hh, :], pm_[:], start=(kb == 0), stop=(kb == nkb - 1))
                        if kb == nkb - 1:
                            dsl = slice(64, 128) if hh == 0 else slice(0, 64)
                            K.copy(Dsb.k(hh)[dsl, :], nd[dsl, :], eng="act")
                            K.dma(Dlo.k(hh)[hs, :], Dsb.k(hh)[dsl, :], eng="sp")
                            deferred.append((it_ + 5, (t, qc, hh, nd)))
                while deferred:
                    fin(*deferred.pop(0)[1])
                while jobs:
                    emit_job()
            if "B" in dbg and s == 0:
                with K.phase():
                    o = dbgout("attT", [128, 4, SEQ]); tf = K.sb("dbgfa", [128, 4, SEQ], F32)
                    K.copy(tf[:], attT[:]); K.dma(o[:], tf[:])
            if stage == "B":
                raise Stop()
            with K.phase():
                EX = K.sb("EX", [128, 2, 16, 257], F32)
                EXb = K.sb("EXb", [128, 2, 16, 257], BF16)
                MU = K.sb("MUc", [128, 3, 16], F32)
                K.dma(MU[:], C.dMU[:])
                wglu = K.sb("wglu", [128, 4, 512], BF16)
                K.dma(wglu[:], W["s5_w_glu"][:].re("(kc p) n -> p kc n", p=128))
                ptr = [K.ps("ptr%d" % i, [128, 1024], BF16) for i in range(2)]
                pe_ = [K.ps("pe%d" % i, [128, 512], F32) for i in range(4)]
                tr1 = K.sb("tr1", [128, 2, 16], F32)
                tr2 = K.sb("tr2", [128, 2, 16], F32)
                with K.phase():
                    Bs = K.sb("BsC", [128, 32, 128], BF16)
                    K.dma(Bs[:], C.dBs[:])
                    K.memset(EX[:, :, :, 0:1], 0.0)
                    for pr in range(16):
                        for ri in range(2):
                            pp = pe_[(pr * 2 + ri) % 4]
                            for h in range(2):
                                g = 2 * pr + h
                                K.mm(pp[h * 64:(h + 1) * 64, 0:256], Bs[:, g, ri * 64:(ri + 1) * 64], UT[:, g, :])
                            K.copy(EX[:, ri, pr, 1:257], pp[:, 0:256], eng=("act" if ri == 0 else "dve"))
                chk(20)
                MUP = K.sb("MUPc", [128, 3, 16, 16], F32)
                K.dma(MUP[:], C.dMUP[:])
                XV = EX[:, :, :, 1:257].re("p r q (b i) -> p r q b i", i=16)
                T1 = K.sb("T1", [128, 2, 16, 16], F32)
                T2 = K.sb("T2", [128, 2, 16, 16], F32)

                def bc4(ref, nb):
                    return Ref(ref.tile, ref.key, ref.ap.unsqueeze(1).unsqueeze(3).to_broadcast([128, 2, 16, nb]))

                def bc3(ref, nb):
                    return Ref(ref.tile, ref.key, ref.ap.unsqueeze(2).to_broadcast([128, 16, nb]))

                def cmul_add(dst, src, k, nb):
                    K.tt(T1[:, :, :, 0:nb], src, bc4(MUP[:, 0, k, :], nb), ALU.mult)
                    K.tt(T2[:, 0, :, 0:nb], src[:, 1], bc3(MUP[:, 2, k, :], nb), ALU.mult)
                    K.tt(T2[:, 1, :, 0:nb], src[:, 0], bc3(MUP[:, 1, k, :], nb), ALU.mult)
                    K.tt(T1[:, :, :, 0:nb], T1[:, :, :, 0:nb], T2[:, :, :, 0:nb], ALU.add)
                    K.tt(dst, dst, T1[:, :, :, 0:nb], ALU.add)

                for i in range(1, 16):
                    cmul_add(XV[:, :, :, :, i], XV[:, :, :, :, i - 1], 0, 16)
                for bb in range(1, 16):
                    cmul_add(XV[:, :, :, bb:bb + 1, 15], XV[:, :, :, bb - 1:bb, 15], 15, 1)
                for i in range(15):
                    cmul_add(XV[:, :, :, 1:16, i], XV[:, :, :, 0:15, 15], i, 15)
                chk(21)
                K.copy(EXb[:], EX[:])
                if "C1" in dbg and s == 0:
                    o = dbgout("EX", [128, 2, 16, 257])
                    K.dma(o[:], EX[:])
                with K.phase():
                    Tz = K.sb("TzC", [128, 32, 128], BF16)
                    CsRe = K.sb("CsReC", [128, 16, 128], BF16)
                    CsIm = K.sb("CsImC", [128, 16, 128], BF16)
                    K.dma(Tz[:], C.dTz[:]); K.dma(CsRe[:], C.dCsRe[:]); K.dma(CsIm[:], C.dCsIm[:])
                    Yg = [K.sb("Yg%d" % i, [128, 128], BF16) for i in range(2)]
                    YGb = K.sb("YGb", [128, 8, 512], BF16)
                    sg_ = [K.sb("sgC%d" % i, [128, 512], BF16) for i in range(2)]
                    if "C1" in dbg and s == 0:
                        oYA = dbgout("YA", [2, 128, 8, 512])
                        YAf = K.sb("YAf", [128, 8, 512], F32)
                    for blk in range(2):
                        cs_ = slice(blk * 128, (blk + 1) * 128)
                        for g4 in range(8):
                            pt = ptr[g4 % 2]
                            for gi in range(4):
                                g = g4 * 4 + gi
                                pr, h = g // 2, g % 2
                                hs = slice(h * 64, (h + 1) * 64)
                                py = pe_[g % 4]
                                K.mm(py[:, 0:128], Tz[:, g, :], UT[:, g, cs_], start=True, stop=False)
                                K.mm(py[:, 0:128], CsRe[hs, pr, :], EXb[hs, 0, pr, blk * 128:blk * 128 + 128], start=False, stop=False)
                                K.mm(py[:, 0:128], CsIm[hs, pr, :], EXb[hs, 1, pr, blk * 128:blk * 128 + 128], start=False, stop=True)
                                yg_ = Yg[g % 2]
                                K.copy(yg_[:], py[:, 0:128], eng="act")
                                K.tr(pt[:, gi * 128:(gi + 1) * 128], yg_[:], C.identb[:])
                            dstv = YGb[:, :, g4 * 64:(g4 + 1) * 64].re("p i (g c) -> p g i c", g=4)
                            srcv = pt[:, 0:512].re("p (g i c) -> p g i c", g=4, i=8)
                            if "C1" in dbg and s == 0:
                                K.copy(YAf[:, :, g4 * 64:(g4 + 1) * 64].re("p i (g c) -> p g i c", g=4), srcv)
                            K.act(dstv, srcv, AF.Gelu_apprx_tanh)
                        if "C1" in dbg and s == 0:
                            K.dma(oYA[blk], YAf[:])
                        for fc in range(4):
                            pt = ptr[fc % 2]
                            for i in range(8):
                                K.tr(pt[:, i * 128:(i + 1) * 128], YGb[:, i, fc * 128:(fc + 1) * 128], C.identb[:])
                            K.copy(ygT[:, fc, blk * 1024:(blk + 1) * 1024].re("p (c i) -> p i c", i=8), pt[:].re("p (i c) -> p i c", i=8),
                                   eng=("act" if fc % 2 == 0 else "dve"))
                    chk(22)
                    for mt in range(4):
                        for nq in range(4):
                            pg = pe_[(mt * 4 + nq) % 4]
                            for kc in range(4):
                                K.mm(pg[:], wglu[:, kc, mt * 128:(mt + 1) * 128], ygT[:, kc, nq * 512:(nq + 1) * 512], start=(kc == 0), stop=(kc == 3))
                            sg = sg_[(mt * 4 + nq) % 2]
                            K.act(sg[:], pg[:], AF.Sigmoid)
                            K.tt(ssmT[:, mt, nq * 512:(nq + 1) * 512], sg[:], ygT[:, mt, nq * 512:(nq + 1) * 512], ALU.mult)
            chk(23)
            if "C" in dbg and s == 0:
                with K.phase():
                    o = dbgout("ssmT", [128, 4, SEQ]); tf = K.sb("dbgfs", [128, 4, SEQ], F32)
                    K.copy(tf[:], ssmT[:]); K.dma(o[:], tf[:])
            if stage == "C":
                raise Stop()
            with K.phase():
                xts = [K.sb("xtD%d" % i, [128, 4, D], F32) for i in range(2)]
                C.hnT = K.sb("hnT", [128, 8, 512], BF16)
                C.actT = K.sb("actT", [128, NFT, 512], BF16)
                C.wgb = [K.sb("wgb%d" % i, [128, 8, 256], BF16) for i in range(3)]
                C.wub = [K.sb("wub%d" % i, [128, 8, 256], BF16) for i in range(3)]
                C.sgb = [K.sb("sgb%d" % i, [128, 512], F32) for i in range(2)]
                C.rotb = [K.sb("rotb%d" % i, [128, D], BF16) for i in range(6)]
                alloc_psum(K, C, "pD")
                wov = W["ev_w_out"][:].re("(kc p) n -> p kc n", p=128)
                K.dma(xts[0][:], I["x"][tok0:tok0 + 512, :].re("(s p) d -> p s d", p=128), eng="pool")
                for tl in range(4):
                    t0 = tok0 + tl * 512
                    xt = xts[tl % 2]
                    if tl + 1 < 4:
                        K.dma(xts[(tl + 1) % 2][:], I["x"][t0 + 512:t0 + 1024, :].re("(s p) d -> p s d", p=128), eng="pool")
                    proj_tokmajor(K, C, lambda kc, sub: (attT if kc < 4 else ssmT)[:, kc % 4, tl * 512 + sub * 128:tl * 512 + (sub + 1) * 128],
                                  8, wov, C.g_mix_post[0], xt, "mp")
                    if "D0" in dbg and s == 0 and tl == 0:
                        o = dbgout("xa0", [128, 4, D])
                        K.dma(o[:], xt[:])
                    ffn_tile(K, C, xt, 0, W)
                    K.dma(x1d[t0:t0 + 512, :].re("(s p) d -> p s d", p=128), xt[:], eng="pool")
                    if "D0" in dbg and s == 0 and tl == 0:
                        o = dbgout("x1", [128, 4, D])
                        K.dma(o[:], xt[:])
                    if stage == "D0":
                        raise Stop()
    build_layer1(K, C, I, W, out, x1d, stage, dbg, dbgout)


def build_layer1(K, C, I, W, out, x1d, stage, dbg, dbgout):
    ntl = 16
    if stage == "L1a":
        ntl = 2
    with K.phase():
        C.g_mix_pre[1] = load_bc(K, "g_mp1", I["norm_mix_pre"][1:2, :], D)
        C.g_mix_post[1] = load_bc(K, "g_mo1", I["norm_mix_post"][1:2, :], D)
        C.g_ffn_pre[1] = load_bc(K, "g_fp1", I["norm_ffn_pre"][1:2, :], D)
        C.g_ffn_post[1] = load_bc(K, "g_fo1", I["norm_ffn_post"][1:2, :], D)
        C.scr_j = K.sb("scr_j1", [128, D], BF16)
        C.scr_f = K.sb("scr_f1", [128, D], F32)
        C.scr_g = K.sb("scr_g1", [128, D], F32)
        C.hn = K.sb("hn1", [128, 4, D], BF16)
        xts = [K.sb("xt1_%d" % i, [128, 4, D], F32) for i in range(2)]
        C.hnT = K.sb("hnT1", [128, 8, 512], BF16)
        C.actT = K.sb("actT1", [128, NFT, 512], BF16)
        C.wgb = [K.sb("wgb1_%d" % i, [128, 8, 256], BF16) for i in range(2)]
        C.wub = [K.sb("wub1_%d" % i, [128, 8, 256], BF16) for i in range(2)]
        C.sgb = [K.sb("sgb1_%d" % i, [128, 512], F32) for i in range(2)]
        C.rotb = [K.sb("rotb1_%d" % i, [128, D], BF16) for i in range(6)]
        alloc_psum(K, C, "pL")
        xbuf = K.sb("xbuf", [128, 8, 515], F32)
        xc = K.sb("xc", [128, 8, 512], F32)
        yT = C.hnT
        xcb = C.actT[:, 0:8, :]
        gz = C.actT[:, 8:16, :]
        GRb = K.sb("GRb", [128, 4, 512], BF16); GIb = K.sb("GIb", [128, 4, 512], BF16)
        A4 = K.sb("A4", [128, 4, 512], F32); M4 = K.sb("M4", [128, 4, 512], F32)
        hbuf = [K.sb("h_%d" % i, [128, 512], F32) for i in range(2)]
        hc = K.sb("hc", [128, 8], F32)
        cw = K.sb("cw", [128, 8, 4], F32)
        cb = K.sb("cb", [128, 8], F32); br = K.sb("br", [128, 8], F32); bi = K.sb("bi", [128, 8], F32)
        cch = K.sb("cch", [128, 8], F32)
        Wr = K.sb("Wr", [128, 8, 256], BF16); Wi = K.sb("Wi", [128, 8, 256], BF16)
        for j in range(4):
            K.dma(cw[:, :, j], I["rg_conv_w"][j:j + 1, :].re("o (ct p) -> p (o ct)", p=128), allow_slow_non_contiguous=True)
        for t_, nm in ((cb, "rg_conv_b"), (br, "rg_b_r"), (bi, "rg_b_i"), (cch, "rg_lam")):
            K.dma(t_[:], I[nm][0:1, :].re("o (ct p) -> p (o ct)", p=128), allow_slow_non_contiguous=True)
        K.act(cch[:], cch[:], AF.Exp, scale=-1.0)
        K.act(cch[:], cch[:], AF.Ln, bias=C.onec[:, 0:1])
        K.ts(cch[:], cch[:], -8.0, ALU.mult)
        cch2 = K.sb("cch2", [128, 8], F32)
        K.ts(cch2[:], cch[:], 2.0, ALU.mult)
        K.dma(Wr[:], W["rg_w_r"][:].re("(q p) n -> p q n", p=128))
        K.dma(Wi[:], W["rg_w_i"][:].re("(q p) n -> p q n", p=128))
        wiv = W["od_w_in"][:].re("(kc p) n -> p kc n", p=128)
        wov = W["od_w_out"][:].re("(kc p) n -> p kc n", p=128)
        K.dma(xts[0][:], x1d[0:512, :].re("(s p) d -> p s d", p=128), eng="pool")
        for tl in range(ntl):
            t0 = tl * 512
            first = (tl % 4 == 0)
            xt = xts[tl % 2]
            if tl + 1 < ntl:
                K.dma(xts[(tl + 1) % 2][:], x1d[t0 + 512:t0 + 1024, :].re("(s p) d -> p s d", p=128), eng="pool")
            rms_pre(K, C, xt[:], 4, C.g_mix_pre[1], C.hn[:], "l")
            transpose_to(K, C, C.hn[:], 4, C.hnT, 0)
            if first:
                K.memset(xbuf[:, :, 0:3], 0.0)
            else:
                K.copy(xbuf[:, :, 0:3], xbuf[:, :, 512:515])
            for ch in range(8):
                wb = C.wgb[ch % 2] if ch % 4 < 2 else C.wub[ch % 2]
                K.dma(wb[:], wiv[:, :, ch * 256:(ch + 1) * 256], eng="sp")
                for mi in range(2):
                    mt = ch * 2 + mi
                    pz = C.psm[C.pmi % 6]; C.pmi += 1
                    for kc in range(8):
                        K.mm(pz[:], wb[:, kc, mi * 128:(mi + 1) * 128], C.hnT.k(kc)[:, kc, :], start=(kc == 0), stop=(kc == 7))
                    if mt < 8:
                        ct = mt
                        K.copy(xbuf.k(ct)[:, ct, 3:515], pz[:], eng="act")
                        K.ts(xc.k(ct)[:, ct, :], xbuf.k(ct)[:, ct, 0:512], cw[:, ct, 0:1], ALU.mult, cb[:, ct:ct + 1], ALU.add)
                        for j in range(1, 4):
                            K.stt(xc.k(ct)[:, ct, :], xbuf.k(ct)[:, ct, j:j + 512], cw[:, ct, j:j + 1], xc.k(ct)[:, ct, :], ALU.mult, ALU.add)
                        K.copy(C.actT.k(("x", ct))[:, ct, :], xc.k(ct)[:, ct, :], eng="pool")
                    else:
                        K.act(C.actT.k(("g", mt - 8))[:, mt, :], pz[:], AF.Gelu_apprx_tanh)
            for half in range(2):
                cts = list(range(half * 4, half * 4 + 4))
                for j, ct in enumerate(cts):
                    hq = (ct // 2) * 2
                    cs_ = slice((ct % 2) * 128, (ct % 2 + 1) * 128)
                    pr_ = C.psm[C.pmi % 6]; pi_ = C.psm[(C.pmi + 1) % 6]; C.pmi += 2
                    for kc in range(2):
                        K.mm(pr_[:], Wr[:, hq + kc, cs_], C.actT.k(("x", hq + kc))[:, hq + kc, :], start=(kc == 0), stop=(kc == 1))
                    for kc in range(2):
                        K.mm(pi_[:], Wi[:, hq + kc, cs_], C.actT.k(("x", hq + kc))[:, hq + kc, :], start=(kc == 0), stop=(kc == 1))
                    K.act(GRb.k(j)[:, j, :], pr_[:], AF.Sigmoid, bias=br[:, ct:ct + 1])
                    K.act(GIb.k(j)[:, j, :], pi_[:], AF.Sigmoid, bias=bi[:, ct:ct + 1])
                    K.tt(xc.k(ct)[:, ct, :], xc.k(ct)[:, ct, :], GIb.k(j)[:, j, :], ALU.mult, eng="pool")
                for j, ct in enumerate(cts):
                    K.act(A4.k(j)[:, j, :], GRb.k(j)[:, j, :], AF.Exp, scale=cch[:, ct:ct + 1])
                    K.act(M4.k(j)[:, j, :], GRb.k(j)[:, j, :], AF.Exp, scale=cch2[:, ct:ct + 1])
                for j, ct in enumerate(cts):
                    K.act(M4.k(j)[:, j, :], M4.k(j)[:, j, :], AF.Sqrt, scale=-1.0, bias=C.onec[:, 0:1])
                for j, ct in enumerate(cts):
                    K.tt(M4.k(j)[:, j, :], M4.k(j)[:, j, :], xc.k(ct)[:, ct, :], ALU.mult)
                    h_ = hbuf[ct % 2]
                    if first:
                        K.scan(h_[:], A4.k(j)[:, j, :], M4.k(j)[:, j, :], 0.0)
                    else:
                        K.scan(h_[:], A4.k(j)[:, j, :], M4.k(j)[:, j, :], hc.k(ct)[:, ct:ct + 1])
                    K.copy(hc.k(ct)[:, ct:ct + 1], h_[:, 511:512], eng="pool")
                    K.tt(yT.k(ct)[:, ct, :], h_[:], C.actT.k(("g", ct))[:, 8 + ct, :], ALU.mult)
                    if "L1" in dbg and tl < 2:
                        if ct == 0 and tl == 0:
                            C.oh = dbgout("h1", [2, 8, 128, 512])
                        K.dma(C.oh[tl, ct], h_[:])
            proj_tokmajor(K, C, lambda kc, sub: yT[:, kc, sub * 128:(sub + 1) * 128], 8, wov, C.g_mix_post[1], xt, "mq")
            if "L1" in dbg and tl < 2:
                if tl == 0:
                    C.oxa = dbgout("xa1", [2, 128, 4, D])
                K.dma(C.oxa[tl], xt[:])
            ffn_tile(K, C, xt, 1, W)
            K.dma(out[t0:t0 + 512, :].re("(s p) d -> p s d", p=128), xt[:], eng="pool")


def kernel(**inputs):
    nc, _ = build("full")
    maps = make_in_maps(inputs)
    res = run_bass_kernel_spmd(nc, maps, core_ids=list(range(NCORES)))
    outs = [np.asarray(r["out"], dtype=np.float32).reshape(NSEQ, SEQ, D) for r in res.results]
    return np.concatenate(outs, axis=0)
```
